# Optimizing a Trainium2 kernel written in Bass

```python
import jax, jax.numpy as jnp
from jax import lax
import numpy as np

D_MODEL = 2048
BATCH = 2
SEQ = 16384
DEPTH = 1

RET_HEADS = 8
RET_QK_DIM = 128
RET_V_DIM = 128
RET_QK_WIDTH = RET_HEADS * RET_QK_DIM
RET_WIDTH = RET_HEADS * RET_V_DIM
CONV_WIDTH = D_MODEL // 2
CONV_GROUPS = 8
CONV_K = 3
CHUNK = 128
N_BRANCH = 2
ROPE_BASE = 10000.0
EPS = 1e-6
COL_SIZES = (RET_QK_WIDTH, RET_QK_WIDTH, RET_WIDTH, RET_WIDTH,
             CONV_WIDTH, CONV_WIDTH, CONV_WIDTH, CONV_WIDTH,
             N_BRANCH * D_MODEL)
COL_SPLITS = tuple(int(c) for c in np.cumsum(COL_SIZES)[:-1])
IN_COLS = int(sum(COL_SIZES))
BRANCH_WIDTH = RET_WIDTH

kernel_name = "hybrid_retention_shortconv_gated_block"


def rmsnorm(x, gain):
    xf = x.astype(jnp.float32)
    y = xf * lax.rsqrt(jnp.mean(xf * xf, axis=-1, keepdims=True) + EPS)
    return (y * gain.astype(jnp.float32)).astype(x.dtype)


def rotary(x):
    s, dh = x.shape[1], x.shape[-1]
    pos = jnp.arange(s, dtype=jnp.float32)
    inv_freq = ROPE_BASE ** (-jnp.arange(0, dh, 2, dtype=jnp.float32) / dh)
    ang = pos[:, None] * inv_freq[None, :]
    cos = jnp.cos(ang)[None, :, None, :]
    sin = jnp.sin(ang)[None, :, None, :]
    xf = x.astype(jnp.float32)
    x1, x2 = jnp.split(xf, 2, axis=-1)
    return jnp.concatenate([x1 * cos - x2 * sin, x2 * cos + x1 * sin], axis=-1)


def retention_dir(q, k, v, log_gamma, strict):
    b, h, s, dk = q.shape
    dv = v.shape[-1]
    n = s // CHUNK
    qc = q.reshape(b, h, n, CHUNK, dk)
    kc = k.reshape(b, h, n, CHUNK, dk)
    vc = v.reshape(b, h, n, CHUNK, dv)
    pos = jnp.arange(CHUNK, dtype=jnp.float32)
    diff = pos[:, None] - pos[None, :]
    mask = (diff > 0) if strict else (diff >= 0)
    lg = log_gamma[:, None, None]
    decay = jnp.where(mask[None], jnp.exp(jnp.where(mask, diff, 0.0)[None] * lg), 0.0)
    scores = jnp.einsum('bhncd,bhnld->bhncl', qc, kc) * decay[None, :, None]
    intra = jnp.einsum('bhncl,bhnle->bhnce', scores, vc)
    k_w = jnp.exp((CHUNK - pos)[None, :] * log_gamma[:, None])
    kv = jnp.einsum('bhncd,hc,bhnce->bhnde', kc, k_w, vc)
    chunk_decay = jnp.exp(CHUNK * log_gamma)[None, :, None, None]

    def step(state, kv_n):
        return chunk_decay * state + kv_n, state

    _, states = lax.scan(step, jnp.zeros((b, h, dk, dv), jnp.float32), jnp.moveaxis(kv, 2, 0))
    states = jnp.moveaxis(states, 0, 2)
    q_w = jnp.exp(pos[None, :] * log_gamma[:, None])
    cross = jnp.einsum('bhncd,hc,bhnde->bhnce', qc, q_w, states)
    return (intra + cross).reshape(b, h, s, dv)


def bidirectional_retention(q, k, v, logit_fwd, logit_bwd):
    lg_f = jax.nn.log_sigmoid(logit_fwd.astype(jnp.float32))
    lg_b = jax.nn.log_sigmoid(logit_bwd.astype(jnp.float32))
    fwd = retention_dir(q, k, v, lg_f, strict=False)
    flip = lambda t: jnp.flip(t, axis=2)
    bwd = flip(retention_dir(flip(q), flip(k), flip(v), lg_b, strict=True))
    return fwd + bwd


def short_conv_centred(u, w):
    rhs = w[:, None, :].astype(u.dtype)
    return lax.conv_general_dilated(
        u, rhs, window_strides=(1,), padding=[(CONV_K // 2, CONV_K // 2)],
        dimension_numbers=('NWC', 'WIO', 'NWC'), feature_group_count=u.shape[-1])


def setup_inputs(seed: int = 0) -> dict:
    key = jax.random.key(seed)
    ks = jax.random.split(key, 12)
    f32 = jnp.float32
    x = jax.random.normal(ks[0], (BATCH, SEQ, D_MODEL), f32)
    norm_gain = 1.0 + 0.02 * jax.random.normal(ks[1], (DEPTH, D_MODEL), f32)
    w_in = jax.random.normal(ks[2], (DEPTH, D_MODEL, IN_COLS), f32) * D_MODEL ** -0.5
    base_logit = jnp.log(2.0 ** (5.0 + jnp.arange(RET_HEADS, dtype=f32)) - 1.0)
    decay_logit_fwd = base_logit[None] + 0.1 * jax.random.normal(ks[3], (DEPTH, RET_HEADS), f32)
    decay_logit_bwd = base_logit[None] + 0.1 * jax.random.normal(ks[4], (DEPTH, RET_HEADS), f32)
    ret_gn_gain = 1.0 + 0.02 * jax.random.normal(ks[5], (DEPTH, RET_WIDTH), f32)
    conv_w = jax.random.normal(ks[6], (DEPTH, CONV_K, CONV_WIDTH), f32) * CONV_K ** -0.5
    w_branch = jax.random.normal(ks[7], (DEPTH, N_BRANCH, BRANCH_WIDTH, D_MODEL), f32) * BRANCH_WIDTH ** -0.5
    w_out = jax.random.normal(ks[8], (DEPTH, D_MODEL, D_MODEL), f32) * D_MODEL ** -0.5
    final_gain = 1.0 + 0.02 * jax.random.normal(ks[9], (D_MODEL,), f32)
    return {"x": x, "norm_gain": norm_gain, "w_in": w_in,
            "decay_logit_fwd": decay_logit_fwd, "decay_logit_bwd": decay_logit_bwd,
            "ret_gn_gain": ret_gn_gain, "conv_w": conv_w, "w_branch": w_branch,
            "w_out": w_out, "final_gain": final_gain}


def reference(x, norm_gain, w_in, decay_logit_fwd, decay_logit_bwd, ret_gn_gain,
              conv_w, w_branch, w_out, final_gain):
    b, s, _ = x.shape
    for layer in range(DEPTH):
        h = rmsnorm(x, norm_gain[layer])
        proj = jnp.einsum('bsd,df->bsf', h, w_in[layer])
        q, k, v, g_ret, c_b, c_c, c_x, g_conv, merge_logits = jnp.split(proj, COL_SPLITS, axis=-1)

        q = rotary(q.reshape(b, s, RET_HEADS, RET_QK_DIM))
        k = rotary(k.reshape(b, s, RET_HEADS, RET_QK_DIM)) * (RET_QK_DIM ** -0.5)
        v = v.reshape(b, s, RET_HEADS, RET_V_DIM).astype(jnp.float32)
        to_bhsd = lambda t: jnp.transpose(t, (0, 2, 1, 3))
        o = bidirectional_retention(to_bhsd(q), to_bhsd(k), to_bhsd(v),
                                    decay_logit_fwd[layer], decay_logit_bwd[layer])
        o = jnp.transpose(o, (0, 2, 1, 3))
        o = o * lax.rsqrt(jnp.mean(o * o, axis=-1, keepdims=True) + EPS)
        o = o.reshape(b, s, RET_WIDTH) * ret_gn_gain[layer].astype(jnp.float32)
        branch_ret = (o * jax.nn.silu(g_ret.astype(jnp.float32))).astype(x.dtype)

        conv = short_conv_centred(c_c * c_x, conv_w[layer])
        branch_conv = c_b * conv * jax.nn.silu(g_conv)

        branches = jnp.stack([branch_ret, branch_conv], axis=2)
        up = jnp.einsum('bsnf,nfd->bsnd', branches, w_branch[layer])
        gates = jax.nn.sigmoid(merge_logits.reshape(b, s, N_BRANCH, D_MODEL))
        merged = jnp.sum(gates * up, axis=2)
        out = jnp.einsum('bsd,de->bse', merged, w_out[layer])
        x = x + out.astype(x.dtype)
    return rmsnorm(x, final_gain)
```

```python
import numpy as np
import concourse.bass as bass
import concourse.mybir as mybir
from concourse.bass_utils import run_bass_kernel_spmd

F32 = mybir.dt.float32
BF16 = mybir.dt.bfloat16
U8 = mybir.dt.uint8
AF = mybir.ActivationFunctionType
ALU = mybir.AluOpType

D = 2048
NH = 8
SEGT = 4096
NCH = 32
T = 256
NT = 16
EPS = 1e-6
NCORE = 8
NPIECE = 56
DEBUG = False
STAGE = 2
CONV_DVE = True
REORDER = True


class Prog:
    def __init__(self):
        self.ops = []
        self.lw = {}
        self.rd = {}
        self.chan_last = {}
        self.last_eng = {}
        self.fence = None
        self.fenced = set()

    def add(self, eng, fn, reads=(), writes=(), chan=None, inc=16):
        i = len(self.ops)
        deps = set()
        for k in reads:
            if k in self.lw:
                deps.add(self.lw[k])
            if isinstance(k, tuple) and k[0] == 'ps':
                me = ('c', chan) if chan is not None else ('e', eng)
                for wh, r in self.rd.get(k, {}).items():
                    if wh != me:
                        deps.add(r)
        for k in writes:
            if k in self.lw:
                deps.add(self.lw[k])
            for r in self.rd.get(k, {}).values():
                deps.add(r)
        if chan is not None and chan in self.chan_last:
            deps.add(self.chan_last[chan])
        if self.fence is not None and eng not in self.fenced:
            deps |= self.fence
            self.fenced.add(eng)
        deps.discard(i)
        who = ('c', chan) if chan is not None else ('e', eng)
        for k in reads:
            self.rd.setdefault(k, {})[who] = i
        for k in writes:
            self.lw[k] = i
            self.rd[k] = {}
        if chan is not None:
            self.chan_last[chan] = i
        else:
            self.last_eng[eng] = i
        self.ops.append((eng, fn, deps, chan, inc))
        return i

    def set_fence(self, skip_chans=()):
        self.fence = set(self.last_eng.values()) | {v for c, v in self.chan_last.items() if c not in skip_chans}
        self.fenced = set()

    def emit(self, nc, block):
        ops = self.ops
        n = len(ops)
        need = [False] * n
        for (eng, fn, deps, chan, inc) in ops:
            for d in deps:
                de = ops[d]
                if de[3] is None and de[0] == 'pe' and eng == 'pe' and chan is None:
                    continue
                need[d] = True
        engs = ['pe', 'act', 'dve', 'pool', 'sp']
        sem_e = {e: nc.semaphore('s_' + e).__enter__() for e in engs}
        chans = sorted({o[3] for o in ops if o[3] is not None}, key=str)
        sem_c = {c: nc.semaphore('c%d' % k).__enter__() for k, c in enumerate(chans)}
        cnt = {e: 0 for e in engs}
        ccnt = {c: 0 for c in chans}
        val = [None] * n
        for i, (eng, fn, deps, chan, inc) in enumerate(ops):
            if chan is not None:
                ccnt[chan] += inc
                val[i] = (('c', chan), ccnt[chan])
            elif need[i]:
                cnt[eng] += 1
                val[i] = (('e', eng), cnt[eng])

        def semof(key):
            return sem_c[key[1]] if key[0] == 'c' else sem_e[key[1]]

        per = {e: [] for e in engs}
        for i, o in enumerate(ops):
            per[o[0]].append(i)
        self.nwaits = 0

        def body(e, eo):
            waited = {}
            for i in per[e]:
                eng, fn, deps, chan, inc = ops[i]
                w = {}
                for d in deps:
                    de = ops[d]
                    if de[3] is None and de[0] == 'pe' and e == 'pe' and chan is None:
                        continue
                    k, v = val[d]
                    if w.get(k, 0) < v:
                        w[k] = v
                for k, v in w.items():
                    if waited.get(k, 0) < v:
                        eo.wait_ge(semof(k), v)
                        waited[k] = v
                        self.nwaits += 1
                ins = fn(eo)
                if val[i] is not None and ins is not None:
                    k, v = val[i]
                    ins.then_inc(semof(k), inc if chan is not None else 1)

        @block.tensor
        def _(eo):
            body('pe', eo)

        @block.scalar
        def _(eo):
            body('act', eo)

        @block.vector
        def _(eo):
            body('dve', eo)

        @block.gpsimd
        def _(eo):
            body('pool', eo)

        @block.sync
        def _(eo):
            body('sp', eo)


class Arena:
    def __init__(self, base_ap, size):
        self.base = base_ap
        self.size = size
        self.off = 0

    def mark(self):
        return self.off

    def reset(self, m):
        self.off = m

    def alloc(self, shape, dt):
        esz = 4 if dt == F32 else 2
        free = int(np.prod(shape[1:]))
        nb = free * esz
        nb_al = (nb + 63) // 64 * 64
        assert self.off + nb_al <= self.size, ("SBUF arena overflow", self.off, nb_al, self.size)
        v = self.base[:, self.off:self.off + nb].bitcast(dt)
        self.off += nb_al
        if len(shape) == 3:
            v = v.rearrange("p (a b) -> p a b", b=shape[2])
        elif len(shape) == 4:
            v = v.rearrange("p (a b c) -> p a b c", b=shape[2], c=shape[3])
        return v


def piece_srcs(idx):
    if idx < 4:
        return [('w_in', idx * 256, 256)]
    if idx < 8:
        return [('w_in', 3072 + (idx - 4) * 256, 256)]
    if idx < 24:
        ctp, t = divmod(idx - 8, 4)
        base = (5120, 6144, 4096, 7168)[t]
        return [('w_in', base + ctp * 256, 256)]
    if idx < 48:
        j, t = divmod(idx - 24, 3)
        if t == 0:
            return [('w_in', 8192 + j * 256, 256)]
        if t == 1:
            return [('w_in', 10240 + j * 256, 256)]
        return [('w_br', j * 256, 256)]
    return [('w_out', (idx - 48) * 256, 256)]


def build():
    nc = bass.Bass("TRN2", target_bir_lowering=False)
    P = Prog()
    dbg_outs = {}

    def dram_in(name, shape, dt=F32):
        return nc.dram_tensor(name, list(shape), dt, kind="ExternalInput").ap()

    x = dram_in("x", [SEGT, D])
    xh = dram_in("xh", [64, D])
    w_in = dram_in("w_in", [D, 12288])
    w_br = dram_in("w_br", [D, D])
    w_out = dram_in("w_out", [D, D])
    ng = dram_in("ng", [1, D])
    fg = dram_in("fg", [1, D])
    gng = dram_in("gng", [1, 1024])
    cw_d = dram_in("cw", [128, 24])
    dl = dram_in("dl", [1, 16])
    ropeq_d = dram_in("ropeq", [128, NCH * 128])
    ropek_d = dram_in("ropek", [128, NCH * 128])
    consts_d = dram_in("consts", [128, 1184])
    seg_d = dram_in("seg", [1, 16])
    y = nc.dram_tensor("y", [SEGT, D], F32, kind="ExternalOutput").ap()
    wsrc = {'w_in': w_in, 'w_br': w_br, 'w_out': w_out}

    skind = "ExternalOutput" if DEBUG else "Internal"
    kvs = nc.dram_tensor("kvs", [NCH * 128, 2048], BF16, kind=skind).ap()
    bst = nc.dram_tensor("bst", [NCH * 128, 1024], F32, kind=skind).ap()
    wsc = nc.dram_tensor("wsc", [NPIECE * 128, 16 * 256], BF16).ap()
    bounce = nc.dram_tensor("bounce", [128, 2048], F32).ap()
    gath = nc.dram_tensor("gath", [4 * 128, 2048], F32).ap()

    ARENA = 207 * 1024
    arena_t = nc.sbuf_tensor("arena", [128, ARENA], U8).__enter__()
    psum_t = nc.psum_tensor("psum", [128, 4096], F32).__enter__()
    A = Arena(arena_t, ARENA)
    ps_all = psum_t

    psptr = {'a': 0, 'b': 0}

    def psalloc(nb, pool):
        lo, n = (0, 4) if pool == 'a' else (4, 4)
        if pool == 'all':
            lo, n = 0, 8
        p = psptr.setdefault(pool, 0)
        if p % nb:
            p += nb - p % nb
        if p + nb > n:
            p = 0
        psptr[pool] = (p + nb) % n
        b0 = lo + p
        return ps_all[:, b0 * 512:(b0 + nb) * 512], [('ps', b0 + t) for t in range(nb)]

    def dma(q, out, in_, reads, writes, chan):
        P.add(q, lambda e: e.dma_start(out=out, in_=in_), reads, writes, chan=chan)

    def mm(out, lhsT, rhs, start, stop, reads, writes):
        P.add('pe', lambda e: e.matmul(out, lhsT, rhs, start=start, stop=stop), reads, writes)

    def tp(out, in_, ident, reads, writes):
        P.add('pe', lambda e: e.transpose(out, in_, ident), reads + ['ident'], writes)

    def act(out, in_, func, reads, writes, scale=None, accum=None):
        kw = {}
        if scale is not None:
            kw['scale'] = scale
        if accum is not None:
            kw['accum_out'] = accum
        P.add('act', lambda e: e.activation(out=out, in_=in_, func=func, **kw), reads, writes)

    def tt(eng, out, in0, in1, op, reads, writes):
        P.add(eng, lambda e: e.tensor_tensor(out=out, in0=in0, in1=in1, op=op), reads, writes)

    def ts(eng, out, in0, s1, s2, op0, op1, reads, writes):
        if s2 is None:
            P.add(eng, lambda e: e.tensor_scalar(out=out, in0=in0, scalar1=s1, scalar2=None, op0=op0), reads, writes)
        else:
            P.add(eng, lambda e: e.tensor_scalar(out=out, in0=in0, scalar1=s1, scalar2=s2, op0=op0, op1=op1), reads, writes)

    def stt(eng, out, in0, scalar, in1, op0, op1, reads, writes):
        P.add(eng, lambda e: e.scalar_tensor_tensor(out=out, in0=in0, scalar=scalar, in1=in1, op0=op0, op1=op1), reads, writes)

    def cp(eng, out, in_, reads, writes):
        if eng == 'act':
            act(out, in_, AF.Copy, reads, writes)
        else:
            P.add(eng, lambda e: e.tensor_copy(out=out, in_=in_), reads, writes)

    def xload(row0, xslot):
        dma('sp', xin[:, xslot, :], x[row0:row0 + 128, :], [], [('xin', xslot)], ('xin', xslot))

    def rsq(ap, keys):
        act(ap, ap, AF.Sqrt, keys, keys)
        P.add('dve', lambda e: e.reciprocal(out=ap, in_=ap), keys, keys)

    def dbg(name, ap, key, shape):
        if not DEBUG:
            return
        t = nc.dram_tensor("dbg_" + name, list(shape), F32, kind="ExternalOutput").ap()
        dbg_outs[name] = t
        dma('sp', t, ap, [key], [('dbg', name)], ('dbg', name))

    identb = A.alloc([128, 128], BF16)
    lg = A.alloc([128, 16], F32)
    DT = A.alloc([128, 1024], BF16)
    QWF = A.alloc([128, 1024], BF16)
    QWB = A.alloc([128, 1024], BF16)
    TF = A.alloc([128, 1024], BF16)
    g128 = A.alloc([128, 16], F32)
    coefB = A.alloc([128, NCH, 8], F32)
    ng_bc = A.alloc([128, D], F32)
    cwt = A.alloc([128, 24], F32)
    Fst = A.alloc([128, 2, 1024], F32)
    B_in = A.alloc([128, 1024], F32)
    ssq = A.alloc([128, 4], F32)
    rstd = A.alloc([128, 4], F32)
    xin = A.alloc([128, 2, D], F32)
    hbuf = A.alloc([128, 2, D], BF16)
    segc = A.alloc([128, 2, 4, 8], F32)
    pers_mark = A.mark()
    cst = A.alloc([128, 1184], F32)
    dlb = A.alloc([128, 16], F32)
    sm = A.alloc([128, 6, 16], F32)
    segb = A.alloc([128, 16], F32)
    TB = A.alloc([128, 1024], BF16)
    coefF = A.alloc([128, NCH, 8], F32)
    tmpA = A.alloc([128, 128], F32)
    tmpB = A.alloc([128, 128], F32)
    rt_box = [A.alloc([128, 4, 512], F32)]

    C_ID, C_EF, C_EB, C_MF, C_MB, C_EL1, C_E128L, C_EC, C_E127C, C_ENB = [k * 128 for k in range(10)]

    dma('sp', cst, consts_d, [], ['cst'], 'cst')
    dma('sp', dlb, dl.partition_broadcast(128), [], ['dlb'], 'dlb')
    dma('sp', segb, seg_d.partition_broadcast(128), [], ['segb'], 'segb')
    dma('sp', ng_bc, ng.partition_broadcast(128), [], ['ng_bc'], 'ng_bc')
    dma('sp', cwt, cw_d, [], ['cwt'], 'cwt')
    cp('dve', identb, cst[:, C_ID:C_ID + 128], ['cst'], ['ident'])
    s_e, s_w, s_l, s_d, s_r = [sm[:, k, :] for k in range(5)]
    act(s_e, dlb, AF.Exp, ['dlb'], ['s_e'], scale=-1.0)
    ts('dve', s_w, s_e, 1.0, None, ALU.add, None, ['s_e'], ['s_w'])
    act(s_l, s_w, AF.Ln, ['s_w'], ['s_l'])
    ts('dve', s_d, s_w, -1.0, 1e-30, ALU.add, ALU.max, ['s_w'], ['s_d'])
    P.add('dve', lambda e: e.reciprocal(out=s_d, in_=s_d), ['s_d'], ['s_d'])
    tt('dve', s_r, s_e, s_d, ALU.mult, ['s_e', 's_d'], ['s_r'])
    stt('dve', lg, s_l, -1.0, s_r, ALU.mult, ALU.mult, ['s_l', 's_r'], ['lg'])
    for h in range(NH):
        lf = lg[:, h:h + 1]
        lb = lg[:, 8 + h:9 + h]
        hs = slice(h * 128, (h + 1) * 128)
        act(tmpA, cst[:, C_EF:C_EF + 128], AF.Exp, ['cst', 'lg'], ['tmpA'], scale=lf)
        act(tmpB, cst[:, C_EB:C_EB + 128], AF.Exp, ['cst', 'lg'], ['tmpB'], scale=lb)
        tt('dve', tmpA, tmpA, cst[:, C_MF:C_MF + 128], ALU.mult, ['tmpA', 'cst'], ['tmpA'])
        tt('dve', tmpB, tmpB, cst[:, C_MB:C_MB + 128], ALU.mult, ['tmpB', 'cst'], ['tmpB'])
        tt('dve', DT[:, hs], tmpA, tmpB, ALU.add, ['tmpA', 'tmpB'], ['DT'])
        act(QWF[:, hs], cst[:, C_EC:C_EC + 128], AF.Exp, ['cst', 'lg'], ['QWF'], scale=lf)
        act(QWB[:, hs], cst[:, C_E127C:C_E127C + 128], AF.Exp, ['cst', 'lg'], ['QWB'], scale=lb)
        act(TB[:, hs], cst[:, C_EL1:C_EL1 + 128], AF.Exp, ['cst', 'lg'], ['TB'], scale=lb)
        act(TF[:, hs], cst[:, C_E128L:C_E128L + 128], AF.Exp, ['cst', 'lg'], ['TF'], scale=lf)
        act(coefF[:, :, h], cst[:, C_ENB:C_ENB + NCH], AF.Exp, ['cst', 'lg'], ['coefF'], scale=lf)
        act(coefB[:, :, h], cst[:, C_ENB:C_ENB + NCH], AF.Exp, ['cst', 'lg'], ['coefB'], scale=lb)
    act(g128, lg, AF.Exp, ['lg'], ['g128'], scale=128.0)
    for r in range(4):
        act(segc[:, 0, r, :], lg[:, 0:8], AF.Exp, ['lg', 'segb'], ['segc'], scale=segb[:, r:r + 1])
        act(segc[:, 1, r, :], lg[:, 8:16], AF.Exp, ['lg', 'segb'], ['segc'], scale=segb[:, 4 + r:5 + r])
        ts('dve', segc[:, 0, r, :], segc[:, 0, r, :], segb[:, 8 + r:9 + r], None, ALU.mult, None, ['segc', 'segb'], ['segc'])
        ts('dve', segc[:, 1, r, :], segc[:, 1, r, :], segb[:, 12 + r:13 + r], None, ALU.mult, None, ['segc', 'segb'], ['segc'])
    dbg('lg', lg, 'lg', [128, 16])
    dbg('segc', segc.rearrange("p a b c -> p (a b c)"), 'segc', [128, 64])

    def prep_elem(row0, nrows, xslot, src=None, hslot=0, noload=False):
        src = x if src is None else src
        sq, rs = ssq[:nrows, hslot:hslot + 1], rstd[:nrows, hslot:hslot + 1]
        ksq, krs = ('ssq', hslot), ('rstd', hslot)
        if not noload:
            dma('sp', xin[:nrows, xslot, :], src[row0:row0 + nrows, :], [], [('xin', xslot)], ('xin', xslot))
        act(hbuf[:nrows, hslot, :], xin[:nrows, xslot, :], AF.Square, [('xin', xslot)], [('hbuf', hslot), ksq], accum=sq)
        ts('dve', rs, sq, 1.0 / D, EPS, ALU.mult, ALU.add, [ksq], [krs])
        rsq(rs, [krs])
        stt('dve', hbuf[:nrows, hslot, :], xin[:nrows, xslot, :], rs, ng_bc[:nrows], ALU.mult, ALU.mult,
            [('xin', xslot), krs, 'ng_bc'], [('hbuf', hslot)])

    def prep_pe(nrows, dst, dkeys, hslot=0, bank=None):
        if bank is None:
            pst, pk = psalloc(2, 'b')
        else:
            pst, pk = ps_all[:, bank * 512:(bank + 2) * 512], [('ps', bank), ('ps', bank + 1)]
        pstb = pst.bitcast(BF16)
        for k in range(16):
            tp(pstb[:, k * nrows:(k + 1) * nrows], hbuf[:nrows, hslot, k * 128:(k + 1) * 128], identb[:nrows, :nrows],
               [('hbuf', hslot)], pk)
        v = pstb[:, 0:16 * nrows].rearrange("p (k t) -> p k t", t=nrows)
        cp('act', dst[:, 0:8, :], v[:, 0:8, :], pk, dkeys)
        cp('dve', dst[:, 8:16, :], v[:, 8:16, :], pk, dkeys)

    def rotary(ps3, nh, cos, sin, dst3, rkeys, wkeys, comb='pool'):
        cb = cos.unsqueeze(1).broadcast_to([128, nh, 64])
        sb = sin.unsqueeze(1).broadcast_to([128, nh, 64])
        x1 = ps3[:, :, 0:64]
        x2 = ps3[:, :, 64:128]
        t = [rt_box[0][:, k, 0:nh * 64].rearrange("p (h e) -> p h e", e=64) for k in range(4)]
        tt('dve', t[0], x1, cb, ALU.mult, rkeys, [('rt', 0)])
        tt('dve', t[1], x2, sb, ALU.mult, rkeys, [('rt', 1)])
        tt('dve', t[2], x2, cb, ALU.mult, rkeys, [('rt', 2)])
        tt('dve', t[3], x1, sb, ALU.mult, rkeys, [('rt', 3)])
        tt(comb, dst3[:, :, 0:64], t[0], t[1], ALU.subtract, [('rt', 0), ('rt', 1)], wkeys)
        tt(comb, dst3[:, :, 64:128], t[2], t[3], ALU.add, [('rt', 2), ('rt', 3)], wkeys)

    wkv = A.alloc([128, 16, 2048], BF16)
    ropek = A.alloc([128, NCH, 128], F32)
    hT1 = A.alloc([128, 2, 16, 128], BF16)
    kvo = A.alloc([128, 2, 2048], BF16)
    Bst = A.alloc([128, 2, 1024], F32)
    Fac = A.alloc([128, 2, 1024], F32)

    w_in_v = w_in.rearrange("(k p) f -> p k f", p=128)
    for q4 in range(4):
        P.add('pool', (lambda q4: lambda e: e.dma_start(out=wkv[:, :, q4 * 512:(q4 + 1) * 512],
                                                        in_=w_in_v[:, :, 1024 + q4 * 512:1024 + (q4 + 1) * 512]))(q4),
              [], [('wkv', q4)], chan=('wkvld', q4))
    dma('sp', ropek, ropek_d.rearrange("p (n c) -> p n c", c=128), [], ['ropek'], 'ropek')
    P.add('dve', lambda e: e.memset(Bst[:, 0, :], 0.0), [], [('Bst', 0)])
    P.add('dve', lambda e: e.memset(Fac[:, 0, :], 0.0), [], [('Fac', 0)])

    halo_list = [8 + 4 * ctp + t for ctp in range(4) for t in range(2)]
    conv_list = halo_list + [i for i in range(NPIECE) if i not in halo_list]
    conv_pos = [0]
    NCV = 32

    def issue_conv(nmax):
        for _ in range(nmax):
            if conv_pos[0] >= len(conv_list):
                return
            idx = conv_list[conv_pos[0]]
            cch = conv_pos[0] % NCV
            conv_pos[0] += 1
            dstp = wsc[idx * 128:(idx + 1) * 128, :].rearrange("p (k f) -> p k f", f=256)
            col = 0
            for (sn, c0, wd) in piece_srcs(idx):
                srcv = wsrc[sn].rearrange("(k p) f -> p k f", p=128)[:, :, c0:c0 + wd]
                P.add('pool', (lambda o, s: lambda e: e.dma_start(out=o, in_=s))(dstp[:, :, col:col + wd], srcv),
                      [], [('wsc', idx)], chan=('cv', cch))
                col += wd

    k32 = A.alloc([128, 1024], F32)
    vBF = A.alloc([128, 2, 2, 1024], BF16)

    def PSB(b, nb):
        return ps_all[:, b * 512:(b + nb) * 512], [('ps', b + t) for t in range(nb)]

    def p1_prep_e(n):
        s_ = n % 2
        prep_elem(n * 128, 128, s_, hslot=s_, noload=True)
        if n - 2 >= 0:
            xload((n - 2) * 128, s_)

    def p1_prep_p(n):
        s_ = n % 2
        prep_pe(128, hT1[:, s_], [('hT1', s_)], hslot=s_, bank=4)

    def p1_proj(n):
        hs_ = n % 2
        o = n % 2
        psk, kk = PSB(0, 2)
        psv, kvk = PSB(2, 2)
        for (psx, kx, cbase, q0) in ((psk, kk, 0, 0), (psv, kvk, 1024, 2)):
            for ft in range(2):
                for k in range(16):
                    mm(psx[:, ft * 512:(ft + 1) * 512], hT1[:, hs_, k, :], wkv[:, k, cbase + ft * 512:cbase + (ft + 1) * 512],
                       k == 0, k == 15, [('hT1', hs_), ('wkv', q0 + ft)], [kx[ft]])
            if cbase == 0:
                cp('act', k32, psk, kk, ['k32'])
            else:
                cp('act', kvo[:, o, 1024:2048], psv, kvk, [('kvo', o, 'v')])

    def p1_post_elem(n):
        o = n % 2
        rotary(k32.rearrange("p (h e) -> p h e", e=128), 8, ropek[:, n, 0:64], ropek[:, n, 64:128],
               kvo[:, o, 0:1024].rearrange("p (h e) -> p h e", e=128), ['k32', 'ropek'], [('kvo', o, 'k')], comb='dve')
        dma('sp', kvs[n * 128:(n + 1) * 128, :], kvo[:, o, :], [('kvo', o, 'k'), ('kvo', o, 'v')], [('kvs', n)], ('kvst', o))
        tt('dve', vBF[:, o, 0, :], kvo[:, o, 1024:2048], TB, ALU.mult, [('kvo', o, 'v'), 'TB'], [('vB', o)])
        tt('dve', vBF[:, o, 1, :], kvo[:, o, 1024:2048], TF, ALU.mult, [('kvo', o, 'v'), 'TF'], [('vF', o)])

    def p1_post_pe(n):
        o = n % 2
        cur = (NCH - 1 - n) % 2
        new = 1 - cur
        dma('sp', bst[n * 128:(n + 1) * 128, :], Bst[:, cur, :], [('Bst', cur)], [('bst', n)], ('bstst', cur))
        pkb, kb = PSB(6, 2)
        pkf, kf = PSB(4, 2)
        for h in range(NH):
            hs = slice(h * 128, (h + 1) * 128)
            mm(pkb[:, hs], kvo[:, o, hs], vBF[:, o, 0, hs], True, True, [('kvo', o, 'k'), ('vB', o)], [kb[h // 4]])
        for h in range(NH):
            hs = slice(h * 128, (h + 1) * 128)
            mm(pkf[:, hs], kvo[:, o, hs], vBF[:, o, 1, hs], True, True, [('kvo', o, 'k'), ('vF', o)], [kf[h // 4]])
        for h in range(NH):
            hs = slice(h * 128, (h + 1) * 128)
            stt('dve', Bst[:, new, hs], Bst[:, cur, hs], g128[:, 8 + h:9 + h], pkb[:, hs], ALU.mult, ALU.add,
                [('Bst', cur), 'g128', kb[h // 4]], [('Bst', new)])
        for h in range(NH):
            hs = slice(h * 128, (h + 1) * 128)
            stt('dve', Fac[:, new, hs], pkf[:, hs], coefF[:, n, h:h + 1], Fac[:, cur, hs], ALU.mult, ALU.add,
                [('Fac', cur), 'coefF', kf[h // 4]], [('Fac', new)])

    xload((NCH - 1) * 128, (NCH - 1) % 2)
    xload((NCH - 2) * 128, (NCH - 2) % 2)
    p1_prep_e(NCH - 1)
    p1_prep_p(NCH - 1)
    issue_conv(100)
    for n in range(NCH - 1, -1, -1):
        if n > 0:
            p1_prep_e(n - 1)
        p1_proj(n)
        if n > 0:
            p1_prep_p(n - 1)
        if n < NCH - 1:
            p1_post_pe(n + 1)
        p1_post_elem(n)
    p1_post_pe(0)
    dma('sp', xin[:64, 1, :], xh[0:64, :], [], [('xin', 1)], ('xin', 1))
    xload(0, 0)
    issue_conv(100)
    fin = NCH % 2
    dma('sp', bounce[:, 0:1024], Fac[:, fin, :], [('Fac', fin)], ['bounce'], 'bnc0')
    dma('sp', bounce[:, 1024:2048], Bst[:, fin, :], [('Bst', fin)], ['bounce2'], 'bnc1')
    P.add('pool', lambda e: e.collective_compute("AllGather", ALU.bypass, replica_groups=[[0, 1, 2, 3], [4, 5, 6, 7]],
                                                 ins=[bounce], outs=[gath]),
          ['bounce', 'bounce2'], ['gath'], chan='cc', inc=1)
    def finish_exchange(Gbuf, gkeys):
        gv = gath.rearrange("(r p) f -> p r f", p=128)
        dma('sp', Gbuf, gv[:, :, 0:1024], ['gath'], gkeys, 'gld')
        for h in range(NH):
            hs = slice(h * 128, (h + 1) * 128)
            ts('dve', Fst[:, 0, hs], Gbuf[:, 0, hs], segc[:, 0, 0, h:h + 1], None, ALU.mult, None, gkeys + ['segc'], [('Fst', 0)])
            for r in range(1, 4):
                stt('dve', Fst[:, 0, hs], Gbuf[:, r, hs], segc[:, 0, r, h:h + 1], Fst[:, 0, hs], ALU.mult, ALU.add,
                    gkeys + ['segc', ('Fst', 0)], [('Fst', 0)])
        dma('sp', Gbuf, gv[:, :, 1024:2048], ['gath'], gkeys, 'gld')
        for h in range(NH):
            hs = slice(h * 128, (h + 1) * 128)
            ts('dve', B_in[:, hs], Gbuf[:, 0, hs], segc[:, 1, 0, h:h + 1], None, ALU.mult, None, gkeys + ['segc'], ['B_in'])
            for r in range(1, 4):
                stt('dve', B_in[:, hs], Gbuf[:, r, hs], segc[:, 1, r, h:h + 1], B_in[:, hs], ALU.mult, ALU.add,
                    gkeys + ['segc', 'B_in'], ['B_in'])
        dbg('F_in', Fst[:, 0, :], ('Fst', 0), [128, 1024])
        dbg('B_in', B_in, 'B_in', [128, 1024])

    if STAGE < 2:
        Gb = wkv.rearrange("p k f -> p (k f)")[:, 0:8192].bitcast(F32).rearrange("p (r f) -> p r f", f=1024)
        finish_exchange(Gb, [('wkv', q) for q in range(4)])
    dbg('Fsum', Fac[:, fin, :], ('Fac', fin), [128, 1024])

    out_keys = []
    if STAGE >= 2:
        P.set_fence(skip_chans=('cc',) + tuple(('cv', k) for k in range(32)))
        A.reset(pers_mark)
        out_keys = phase2(nc, P, A, locals())

    P.add('sp', lambda e: None, reads=out_keys + [('dbg', k) for k in dbg_outs] + ['B_in', ('Fst', 0)] +
          ([('kvs', n) for n in range(NCH)] + [('bst', n) for n in range(NCH)] if STAGE < 2 else []), writes=[])
    if STAGE < 2:
        dma('sp', y[0:128, :], x[0:128, :], [], [('y', 0)], 'ycopy')
        P.add('sp', lambda e: None, reads=[('y', 0)], writes=[])

    with nc.allow_low_precision("bf16 matmul operands, fp32 accumulation"):
        with nc.Block() as block:
            P.emit(nc, block)
    return nc, dbg_outs


def phase2(nc, P, A, env):
    g = env
    x, xh, y, kvs, bst, wsc = g['x'], g['xh'], g['y'], g['kvs'], g['bst'], g['wsc']
    dma, mm, tp, act, tt, ts, stt, cp = g['dma'], g['mm'], g['tp'], g['act'], g['tt'], g['ts'], g['stt'], g['cp']
    psalloc, prep_elem, prep_pe, rotary, rsq = g['psalloc'], g['prep_elem'], g['prep_pe'], g['rotary'], g['rsq']
    identb, xin, hbuf, ssq, rstd = g['identb'], g['xin'], g['hbuf'], g['ssq'], g['rstd']
    DT, QWF, QWB, TF, g128, coefB, Fst, B_in, cwt = (g['DT'], g['QWF'], g['QWB'], g['TF'], g['g128'], g['coefB'],
                                                     g['Fst'], g['B_in'], g['cwt'])
    fg, gng, ropeq_d, dbg = g['fg'], g['gng'], g['ropeq_d'], g['dbg']

    g['rt_box'][0] = A.alloc([128, 4, 128], F32)
    fg_bc = A.alloc([128, D], F32)
    gn_bc = A.alloc([128, 1024], F32)
    NW = 4
    Wr = A.alloc([128, NW, 16, 256], BF16)
    ropeq = A.alloc([128, 2, 2, 128], F32)
    hT = A.alloc([128, 2, 16, T], BF16)
    uh = A.alloc([128, 8, 64], F32)
    q_rot = A.alloc([128, 2, 1024], BF16)
    sgn = A.alloc([128, 2, 1024], BF16)
    ccs = A.alloc([128, 2, T], F32)
    ubuf = A.alloc([128, 2, T + 2], F32)
    tacc = A.alloc([128, 2, T], F32)
    sgc = A.alloc([128, 2, T], F32)
    bconvT = A.alloc([128, 8, T], BF16)
    kvb = A.alloc([128, 2, 2048], BF16)
    Bl = A.alloc([128, 1024], F32)
    qT = A.alloc([128, 1024], BF16)
    kT = A.alloc([128, 1024], BF16)
    qTf = A.alloc([128, 1024], BF16)
    qTb = A.alloc([128, 1024], BF16)
    Bfull = A.alloc([128, 1024], BF16)
    Fbf = A.alloc([128, 1024], BF16)
    vF = A.alloc([128, 1024], BF16)
    STm = A.alloc([128, 1024], BF16)
    ssqo = A.alloc([128, 8], F32)
    rso = A.alloc([128, 8], F32)
    bret = A.alloc([128, 1024], BF16)
    hTh = bret.rearrange("p (k t) -> p k t", t=64)
    bretT = A.alloc([128, 8, T], BF16)
    gates = A.alloc([128, 2, 2, 256], BF16)
    m12 = A.alloc([128, 2, 2, 256], F32)
    merged = A.alloc([128, 2, 2, 256], BF16)
    mT = A.alloc([128, 16, T], BF16)
    xo = A.alloc([128, 2, D], F32)

    dma('sp', fg_bc, fg.partition_broadcast(128), [], ['fg_bc'], 'fg_bc')
    dma('sp', gn_bc, gng.partition_broadcast(128), [], ['gn_bc'], 'gn_bc')

    if REORDER:
        wseq = [8 + 4 * ctp + t for ctp in range(4) for t in range(2)] + list(range(8))
        for i in range(NT):
            wseq += list(range(8, 48)) + (list(range(8)) if i + 1 < NT else []) + list(range(48, 56))
    else:
        wseq = [8 + 4 * ctp + t for ctp in range(4) for t in range(2)]
        for i in range(NT):
            wseq += list(range(NPIECE))
    wstate = {'issued': 0, 'next': 0}

    def wget(expect_idx):
        gq = wstate['next']
        assert wseq[gq] == expect_idx, (gq, wseq[gq], expect_idx)
        while wstate['issued'] < min(gq + NW, len(wseq)):
            q = wstate['issued']
            s = q % NW
            idx = wseq[q]
            dma('sp', Wr[:, s], wsc[idx * 128:(idx + 1) * 128, :].rearrange("p (k f) -> p k f", f=256),
                [('wsc', idx)], [('W', s)], ('W', s))
            wstate['issued'] += 1
        wstate['next'] += 1
        s = gq % NW
        return Wr[:, s], ('W', s)

    prep_elem(0, 64, 1, src=xh, noload=True)
    prep_pe(64, hTh, ['bret'])
    g['xload'](128, 1)
    ccs_f = ccs.rearrange("p g t -> p (g t)")
    for ctp in range(4):
        pss = []
        for t in range(2):
            W, wk = wget(8 + 4 * ctp + t)
            psh, hk = psalloc(1, 'a')
            for gq in range(2):
                for k in range(16):
                    mm(psh[:, gq * 64:(gq + 1) * 64], W[:, k, gq * 128:(gq + 1) * 128], hTh[:, k, :], k == 0, k == 15,
                       [wk, 'bret'], hk)
            pss.append((psh, hk))
            if t == 0:
                cp('act', ccs_f[:, 0:128], psh[:, 0:128], hk, ['ccs'])
        psh, hk = pss[1]
        tt('dve', uh[:, 2 * ctp:2 * ctp + 2, :], psh[:, 0:128].rearrange("p (g t) -> p g t", t=64),
           ccs_f[:, 0:128].rearrange("p (g t) -> p g t", t=64), ALU.mult, hk + ['ccs'], ['uh'])
    dbg('uh', uh.rearrange("p a b -> p (a b)"), 'uh', [128, 512])

    xs = {'n': 0}

    def p2_prep_elem(i):
        slots = []
        for c in range(2):
            s = xs['n'] % 2
            xs['n'] += 1
            slots.append(s)
        return slots

    def p2_prep_e(i):
        hs_ = i % 2
        dma('sp', ropeq[:, hs_], ropeq_d.rearrange("p (n c) -> p n c", c=128)[:, 2 * i:2 * i + 2, :], [],
            [('ropeq', hs_)], ('ropeq', hs_))
        for c in range(2):
            prep_elem((2 * i + c) * 128, 128, c, hslot=c, noload=True)
            if i + 1 < NT:
                g['xload']((2 * (i + 1) + c) * 128, c)

    def p2_prep_p(i):
        hs_ = i % 2
        for c in range(2):
            pst, pk = psalloc(2, 'b')
            pstb = pst.bitcast(BF16)
            for k in range(16):
                tp(pstb[:, k * 128:(k + 1) * 128], hbuf[:, c, k * 128:(k + 1) * 128], identb, [('hbuf', c)], pk)
            v = pstb.rearrange("p (k t) -> p k t", t=128)
            cp('act', hT[:, hs_, 0:8, c * 128:(c + 1) * 128], v[:, 0:8, :], pk, [('hT', hs_, c)])
            cp('dve', hT[:, hs_, 8:16, c * 128:(c + 1) * 128], v[:, 8:16, :], pk, [('hT', hs_, c)])

    def tok_piece(i, idx):
        hs_ = i % 2
        W, wk = wget(idx)
        ps, pk = psalloc(1, 'a')
        for c in range(2):
            for k in range(16):
                mm(ps[:, c * 256:(c + 1) * 256], hT[:, hs_, k, c * 128:(c + 1) * 128], W[:, k, :], k == 0, k == 15,
                   [('hT', hs_, c), wk], pk)
        return ps.rearrange("p (c f) -> p c f", f=256), pk

    def feat_piece(i, idx):
        hs_ = i % 2
        W, wk = wget(idx)
        ps, pk = psalloc(1, 'a')
        for gq in range(2):
            for k in range(16):
                mm(ps[:, gq * 256:(gq + 1) * 256], W[:, k, gq * 128:(gq + 1) * 128], hT[:, hs_, k, :], k == 0, k == 15,
                   [('hT', hs_, 0), ('hT', hs_, 1), wk], pk)
        return ps.rearrange("p (g t) -> p g t", t=256), pk

    def conv_A(i, ctp):
        ps, pk = feat_piece(i, 8 + 4 * ctp)
        cp('act', ccs, ps, pk, ['ccs'])
        ps, pk = feat_piece(i, 9 + 4 * ctp)
        tt('dve', ubuf[:, :, 1:T + 1], ps, ccs, ALU.mult, pk + ['ccs'], ['ubuf'])
        for gq in range(2):
            ct = 2 * ctp + gq
            ta = tacc[:, gq, :]
            cp('pool', ubuf[:, gq, 0:1], uh[:, ct, 2 * i:2 * i + 1], ['uh'], ['ubuf'])
            cp('pool', ubuf[:, gq, T + 1:T + 2], uh[:, ct, 2 * i + 3:2 * i + 4], ['uh'], ['ubuf'])
            ts('dve', ta, ubuf[:, gq, 0:T], cwt[:, ct * 3:ct * 3 + 1], None, ALU.mult, None, ['ubuf', 'cwt'], ['tacc'])
            stt('dve', ta, ubuf[:, gq, 1:T + 1], cwt[:, ct * 3 + 1:ct * 3 + 2], ta, ALU.mult, ALU.add, ['ubuf', 'cwt', 'tacc'], ['tacc'])
            stt('dve', ta, ubuf[:, gq, 2:T + 2], cwt[:, ct * 3 + 2:ct * 3 + 3], ta, ALU.mult, ALU.add, ['ubuf', 'cwt', 'tacc'], ['tacc'])

    def conv_B(i, ctp):
        ps, pk = feat_piece(i, 10 + 4 * ctp)
        tt('dve', tacc, ps, tacc, ALU.mult, pk + ['tacc'], ['tacc'])
        ps, pk = feat_piece(i, 11 + 4 * ctp)
        act(sgc, ps, AF.Silu, pk, ['sgc'])
        tt('pool', bconvT[:, 2 * ctp:2 * ctp + 2, :], tacc, sgc, ALU.mult, ['tacc', 'sgc'],
           [('bconvT', 2 * ctp), ('bconvT', 2 * ctp + 1)])

    def conv_pair(i, s8):
        if s8 % 2 == 0:
            conv_A(i, s8 // 2)
        else:
            conv_B(i, s8 // 2)

    def ret_stage(i, c, st, S):
        n = 2 * i + c
        cur = n % 2
        new = 1 - cur
        H3 = lambda ap: ap.rearrange("p (h e) -> p h e", e=128)
        if st == 0:
            psq, qk = psalloc(1, 'b')
            psk, kk = psalloc(1, 'b')
            psqb, pskb = psq.bitcast(BF16), psk.bitcast(BF16)
            for h in range(NH):
                hs = slice(h * 128, (h + 1) * 128)
                tp(psqb[:, hs], q_rot[:, c, hs], identb, [('q_rot', c)], qk)
            for h in range(NH):
                hs = slice(h * 128, (h + 1) * 128)
                tp(pskb[:, hs], kvb[:, c, hs], identb, [('kvb', c)], kk)
            cp('act', qT, psqb, qk, ['qT'])
            cp('act', kT, pskb, kk, ['kT'])
            tt('dve', qTf, psqb, QWF, ALU.mult, qk + ['QWF'], ['qTf'])
            tt('dve', qTb, psqb, QWB, ALU.mult, qk + ['QWB'], ['qTb'])
            if n == 0:
                dma('sp', Bl, bst[0:128, :], [('bst', 0)], ['Bl'], 'Bl')
            for h in range(NH):
                hs = slice(h * 128, (h + 1) * 128)
                stt('dve', Bfull[:, hs], B_in[:, hs], coefB[:, n, h:h + 1], Bl[:, hs], ALU.mult, ALU.add,
                    ['B_in', 'coefB', 'Bl'], ['Bfull'])
            if n + 1 < NCH:
                dma('sp', Bl, bst[(n + 1) * 128:(n + 2) * 128, :], [('bst', n + 1)], ['Bl'], 'Bl')
            cp('act', Fbf, Fst[:, cur, :], [('Fst', cur)], ['Fbf'])
            tt('pool', vF, kvb[:, c, 1024:2048], TF, ALU.mult, [('kvb', c), 'TF'], ['vF'])
        elif st == 1:
            psS, sk = psalloc(2, 'b')
            for h in range(NH):
                hs = slice(h * 128, (h + 1) * 128)
                mm(psS[:, hs], kT[:, hs], qT[:, hs], True, True, ['kT', 'qT'], [sk[h // 4]])
            tt('dve', STm, psS, DT, ALU.mult, sk + ['DT'], ['STm'])
        elif st == 2:
            psO, ok = psalloc(2, 'b')
            S['psO'], S['ok'] = psO, ok
            for h in range(NH):
                hs = slice(h * 128, (h + 1) * 128)
                vs = slice(1024 + h * 128, 1024 + (h + 1) * 128)
                mm(psO[:, hs], STm[:, hs], kvb[:, c, vs], True, False, ['STm', ('kvb', c)], [ok[h // 4]])
                mm(psO[:, hs], qTf[:, hs], Fbf[:, hs], False, False, ['qTf', 'Fbf'], [ok[h // 4]])
                mm(psO[:, hs], qTb[:, hs], Bfull[:, hs], False, True, ['qTb', 'Bfull'], [ok[h // 4]])
            psK, kk = psalloc(2, 'b')
            for h in range(NH):
                hs = slice(h * 128, (h + 1) * 128)
                mm(psK[:, hs], kvb[:, c, hs], vF[:, hs], True, True, [('kvb', c), 'vF'], [kk[h // 4]])
            for h in range(NH):
                hs = slice(h * 128, (h + 1) * 128)
                act(bret[:, hs], psO[:, hs], AF.Square, [ok[h // 4]], ['bret', 'ssqo'], accum=ssqo[:, h:h + 1])
            ts('dve', rso, ssqo, 1.0 / 128, EPS, ALU.mult, ALU.add, ['ssqo'], ['rso'])
            rsq(rso, ['rso'])
            for h in range(NH):
                hs = slice(h * 128, (h + 1) * 128)
                stt('dve', bret[:, hs], psO[:, hs], rso[:, h:h + 1], sgn[:, c, hs], ALU.mult, ALU.mult,
                    [ok[h // 4], 'rso', ('sgn', c)], ['bret'])
            for h in range(NH):
                hs = slice(h * 128, (h + 1) * 128)
                stt('dve', Fst[:, new, hs], Fst[:, cur, hs], g128[:, h:h + 1], psK[:, hs], ALU.mult, ALU.add,
                    [('Fst', cur), 'g128', kk[h // 4]], [('Fst', new)])
        else:
            psb, bk = psalloc(1, 'b')
            psbb = psb.bitcast(BF16)
            for h in range(NH):
                hs = slice(h * 128, (h + 1) * 128)
                tp(psbb[:, hs], bret[:, hs], identb, ['bret'], bk)
            cp('act', bretT[:, :, c * 128:(c + 1) * 128], H3(psbb), bk, [('bretT', c)])

    def merged_T(i, j):
        sl = j % 2
        pst, pk = psalloc(1, 'b')
        pstb = pst.bitcast(BF16)
        for c in range(2):
            for kk in range(2):
                tp(pstb[:, kk * 256 + c * 128: kk * 256 + (c + 1) * 128], merged[:, sl, c, kk * 128:(kk + 1) * 128], identb,
                   [('merged', sl)], pk)
        eng = 'act' if j % 2 == 0 else 'dve'
        cp(eng, mT[:, 2 * j:2 * j + 2, :], pstb[:, 0:512].rearrange("p (k t) -> p k t", t=256), pk, [('mT', j)])

    out_keys = []

    def kv_loads(i):
        for c in range(2):
            n = 2 * i + c
            dma('sp', kvb[:, c, :], kvs[n * 128:(n + 1) * 128, :], [('kvs', n)], [('kvb', c)], ('kvb', c))

    def qg_pieces(i):
        hs_ = i % 2
        for p in range(4):
            ps, pk = tok_piece(i, p)
            for c in range(2):
                rotary(ps[:, c, :].rearrange("p (h e) -> p h e", e=128), 2, ropeq[:, hs_, c, 0:64], ropeq[:, hs_, c, 64:128],
                       q_rot[:, c, p * 256:(p + 1) * 256].rearrange("p (h e) -> p h e", e=128),
                       pk + [('ropeq', hs_)], [('q_rot', c)])
        for p in range(4):
            ps, pk = tok_piece(i, 4 + p)
            act(sgn[:, :, p * 256:(p + 1) * 256], ps, AF.Silu, pk, [('sgn', 0), ('sgn', 1)])
            tt('pool', sgn[:, :, p * 256:(p + 1) * 256], sgn[:, :, p * 256:(p + 1) * 256],
               gn_bc[:, p * 256:(p + 1) * 256].unsqueeze(1).broadcast_to([128, 2, 256]), ALU.mult,
               [('sgn', 0), ('sgn', 1), 'gn_bc'], [('sgn', 0), ('sgn', 1)])

    p2_prep_e(0)
    p2_prep_p(0)
    kv_loads(0)
    if REORDER:
        qg_pieces(0)
    for i in range(NT):
        hs_ = i % 2
        if not REORDER:
            qg_pieces(i)
        S = {}
        if i == 0:
            for s8 in range(8):
                conv_pair(i, s8)
            p2_prep_e(i + 1)
            g['finish_exchange'](xo.rearrange("p c f -> p (c f)").rearrange("p (r f) -> p r f", f=1024), [('xo', 0), ('xo', 1)])
            for s8 in range(8):
                ret_stage(i, s8 // 4, s8 % 4, S)
        else:
            for s8 in range(8):
                ret_stage(i, s8 // 4, s8 % 4, S)
                conv_pair(i, s8)
                if s8 == 3 and i + 1 < NT:
                    p2_prep_e(i + 1)
        if i + 1 < NT:
            p2_prep_p(i + 1)
            kv_loads(i + 1)
        dma('sp', xo, x[i * T:(i + 1) * T, :].rearrange("(c p) f -> p c f", p=128), [], [('xo', 0), ('xo', 1)], 'xo')
        for j in range(8):
            for br in range(2):
                ps, pk = tok_piece(i, 24 + 3 * j + br)
                act(gates[:, :, br, :], ps, AF.Sigmoid, pk, [('gates', br)])
            W, wk = wget(24 + 3 * j + 2)
            psu, uk = psalloc(2, 'b')
            psu4 = psu.rearrange("p (c b f) -> p c b f", b=2, f=256)
            for c in range(2):
                for br in range(2):
                    src = bretT if br == 0 else bconvT
                    for kf in range(8):
                        rk = ('bretT', c) if br == 0 else ('bconvT', kf)
                        mm(psu4[:, c, br, :], src[:, kf, c * 128:(c + 1) * 128], W[:, br * 8 + kf, :], kf == 0, kf == 7,
                           [rk, wk], [uk[c]])
            tt('dve', m12[:, :, 0, :], psu4[:, :, 0, :], gates[:, :, 0, :], ALU.mult, uk + [('gates', 0)], [('m12', 0)])
            tt('dve', m12[:, :, 1, :], psu4[:, :, 1, :], gates[:, :, 1, :], ALU.mult, uk + [('gates', 1)], [('m12', 1)])
            tt('pool', merged[:, j % 2], m12[:, :, 0, :], m12[:, :, 1, :], ALU.add, [('m12', 0), ('m12', 1)], [('merged', j % 2)])
            if j > 0:
                merged_T(i, j - 1)
        merged_T(i, 7)
        if REORDER and i + 1 < NT:
            qg_pieces(i + 1)
        for j in range(8):
            W, wk = wget(48 + j)
            ps, pk = psalloc(1, 'a')
            for c in range(2):
                for k in range(16):
                    mm(ps[:, c * 256:(c + 1) * 256], mT[:, k, c * 128:(c + 1) * 128], W[:, k, :], k == 0, k == 15,
                       [('mT', k // 2), wk], pk)
            xv = xo[:, :, j * 256:(j + 1) * 256]
            tt('dve', xv, ps.rearrange("p (c f) -> p c f", f=256), xv, ALU.add, pk + [('xo', 0), ('xo', 1)], [('xo', 0), ('xo', 1)])
        for c in range(2):
            n = 2 * i + c
            act(mT.rearrange("p k t -> p (k t)")[:, 0:D], xo[:, c, :], AF.Square, [('xo', c)],
                [('mT', j) for j in range(8)] + [('ssq2', c)], accum=ssq[:, 2 + c:3 + c])
            ts('dve', rstd[:, 2 + c:3 + c], ssq[:, 2 + c:3 + c], 1.0 / D, EPS, ALU.mult, ALU.add, [('ssq2', c)], [('rstd2', c)])
            rsq(rstd[:, 2 + c:3 + c], [('rstd2', c)])
            if c == 0:
                act(xo[:, c, :], xo[:, c, :], AF.Copy, [('xo', c), ('rstd2', c)], [('xo', c)], scale=rstd[:, 2 + c:3 + c])
                tt('pool', xo[:, c, :], xo[:, c, :], fg_bc, ALU.mult, [('xo', c), 'fg_bc'], [('xo', c)])
            else:
                stt('dve', xo[:, c, :], xo[:, c, :], rstd[:, 2 + c:3 + c], fg_bc, ALU.mult, ALU.mult,
                    [('xo', c), ('rstd2', c), 'fg_bc'], [('xo', c)])
            dma('sp', y[n * 128:(n + 1) * 128, :], xo[:, c, :], [('xo', c)], [('y', n)], ('yst', c))
            out_keys.append(('y', n))
    return out_keys


_NC_CACHE = {}


def _host_consts():
    c = np.zeros((128, 1184), np.float32)
    l = np.arange(128, dtype=np.float64)[:, None]
    cc = np.arange(128, dtype=np.float64)[None, :]
    c[:, 0:128] = np.eye(128)
    c[:, 128:256] = np.maximum(cc - l, 0)
    c[:, 256:384] = np.maximum(l - cc, 0)
    c[:, 384:512] = (cc >= l)
    c[:, 512:640] = (l > cc)
    c[:, 640:768] = l + 1
    c[:, 768:896] = 128 - l
    c[:, 896:1024] = cc
    c[:, 1024:1152] = 127 - cc
    c[:, 1152:1184] = 128.0 * (31 - np.arange(32))[None, :]
    return c


def _rope_tables(pos0):
    pos = (pos0 + np.arange(SEGT)).astype(np.float64)
    inv = 10000.0 ** (-np.arange(0, 128, 2, dtype=np.float64) / 128)
    ang = pos[:, None] * inv[None, :]
    cos, sin = np.cos(ang), np.sin(ang)
    sc = 128 ** -0.5

    def lay(a, b):
        t = np.concatenate([a, b], axis=1).reshape(NCH, 128, 128)
        return np.ascontiguousarray(t.transpose(1, 0, 2).reshape(128, NCH * 128)).astype(np.float32)

    return lay(cos, sin), lay(cos * sc, sin * sc)


def kernel(x, norm_gain, w_in, decay_logit_fwd, decay_logit_bwd, ret_gn_gain, conv_w, w_branch, w_out, final_gain):
    x = np.asarray(x, np.float32)
    Bn, S, _ = x.shape
    if 'nc' not in _NC_CACHE:
        _NC_CACHE['nc'] = build()
    nc, dbg_outs = _NC_CACHE['nc']
    w_in0 = np.ascontiguousarray(np.asarray(w_in, np.float32)[0])
    w_br0 = np.ascontiguousarray(np.asarray(w_branch, np.float32)[0].reshape(2 * 1024, D))
    w_out0 = np.ascontiguousarray(np.asarray(w_out, np.float32)[0])
    ng = np.asarray(norm_gain, np.float32)[0].reshape(1, D)
    fg = np.asarray(final_gain, np.float32).reshape(1, D)
    gng = np.asarray(ret_gn_gain, np.float32)[0].reshape(1, 1024)
    cw = np.ascontiguousarray(np.asarray(conv_w, np.float32)[0].reshape(3, 8, 128).transpose(2, 1, 0).reshape(128, 24))
    dl = np.concatenate([np.asarray(decay_logit_fwd, np.float32)[0], np.asarray(decay_logit_bwd, np.float32)[0]]).reshape(1, 16)
    consts = _host_consts()
    in_maps = []
    for c in range(NCORE):
        b, s = divmod(c, 4)
        t0 = s * SEGT
        xs = np.ascontiguousarray(x[b, t0:t0 + SEGT])
        xhal = np.zeros((64, D), np.float32)
        for i in range(NT + 1):
            for k, tok in enumerate((t0 + T * i - 1, t0 + T * i)):
                if 0 <= tok < S:
                    xhal[2 * i + k] = x[b, tok]
        rq, rk = _rope_tables(t0)
        seg = np.zeros((1, 16), np.float32)
        for r in range(4):
            if r < s:
                seg[0, r] = 4096.0 * (s - 1 - r)
                seg[0, 8 + r] = 1.0
            if r > s:
                seg[0, 4 + r] = 4096.0 * (r - s - 1)
                seg[0, 12 + r] = 1.0
        in_maps.append({"x": xs, "xh": xhal, "w_in": w_in0, "w_br": w_br0, "w_out": w_out0, "ng": ng, "fg": fg,
                        "gng": gng, "cw": cw, "dl": dl, "ropeq": rq, "ropek": rk, "consts": consts, "seg": seg})
    res = run_bass_kernel_spmd(nc, in_maps, core_ids=list(range(NCORE)))
    out = np.empty((Bn, S, D), np.float32)
    for c in range(NCORE):
        b, s = divmod(c, 4)
        out[b, s * SEGT:(s + 1) * SEGT] = res.results[c]["y"]
    if DEBUG:
        kernel.last_results = res.results
    return out
```

```python
import numpy as np
import concourse.bass as bass
import concourse.mybir as mybir
from concourse.bass_utils import run_bass_kernel_spmd

F32 = mybir.dt.float32
BF16 = mybir.dt.bfloat16
U8 = mybir.dt.uint8
AF = mybir.ActivationFunctionType
ALU = mybir.AluOpType

D = 2048
NH = 8
SEGT = 4096
NCH = 32
T = 256
NT = 16
EPS = 1e-6
NCORE = 8
NPIECE = 56
DEBUG = False
STAGE = 2
CONV_DVE = True
REORDER = True


class Prog:
    def __init__(self):
        self.ops = []
        self.lw = {}
        self.rd = {}
        self.chan_last = {}
        self.last_eng = {}
        self.fence = None
        self.fenced = set()

    def add(self, eng, fn, reads=(), writes=(), chan=None, inc=16):
        i = len(self.ops)
        deps = set()
        for k in reads:
            if k in self.lw:
                deps.add(self.lw[k])
            if isinstance(k, tuple) and k[0] == 'ps':
                me = ('c', chan) if chan is not None else ('e', eng)
                for wh, r in self.rd.get(k, {}).items():
                    if wh != me:
                        deps.add(r)
        for k in writes:
            if k in self.lw:
                deps.add(self.lw[k])
            for r in self.rd.get(k, {}).values():
                deps.add(r)
        if chan is not None and chan in self.chan_last:
            deps.add(self.chan_last[chan])
        if self.fence is not None and eng not in self.fenced:
            deps |= self.fence
            self.fenced.add(eng)
        deps.discard(i)
        who = ('c', chan) if chan is not None else ('e', eng)
        for k in reads:
            self.rd.setdefault(k, {})[who] = i
        for k in writes:
            self.lw[k] = i
            self.rd[k] = {}
        if chan is not None:
            self.chan_last[chan] = i
        else:
            self.last_eng[eng] = i
        self.ops.append((eng, fn, deps, chan, inc))
        return i

    def set_fence(self, skip_chans=()):
        self.fence = set(self.last_eng.values()) | {v for c, v in self.chan_last.items() if c not in skip_chans}
        self.fenced = set()

    def emit(self, nc, block):
        ops = self.ops
        n = len(ops)
        need = [False] * n
        for (eng, fn, deps, chan, inc) in ops:
            for d in deps:
                de = ops[d]
                if de[3] is None and de[0] == 'pe' and eng == 'pe' and chan is None:
                    continue
                need[d] = True
        engs = ['pe', 'act', 'dve', 'pool', 'sp']
        sem_e = {e: nc.semaphore('s_' + e).__enter__() for e in engs}
        chans = sorted({o[3] for o in ops if o[3] is not None}, key=str)
        sem_c = {c: nc.semaphore('c%d' % k).__enter__() for k, c in enumerate(chans)}
        cnt = {e: 0 for e in engs}
        ccnt = {c: 0 for c in chans}
        val = [None] * n
        for i, (eng, fn, deps, chan, inc) in enumerate(ops):
            if chan is not None:
                ccnt[chan] += inc
                val[i] = (('c', chan), ccnt[chan])
            elif need[i]:
                cnt[eng] += 1
                val[i] = (('e', eng), cnt[eng])

        def semof(key):
            return sem_c[key[1]] if key[0] == 'c' else sem_e[key[1]]

        per = {e: [] for e in engs}
        for i, o in enumerate(ops):
            per[o[0]].append(i)
        self.nwaits = 0

        def body(e, eo):
            waited = {}
            for i in per[e]:
                eng, fn, deps, chan, inc = ops[i]
                w = {}
                for d in deps:
                    de = ops[d]
                    if de[3] is None and de[0] == 'pe' and e == 'pe' and chan is None:
                        continue
                    k, v = val[d]
                    if w.get(k, 0) < v:
                        w[k] = v
                for k, v in w.items():
                    if waited.get(k, 0) < v:
                        eo.wait_ge(semof(k), v)
                        waited[k] = v
                        self.nwaits += 1
                ins = fn(eo)
                if val[i] is not None and ins is not None:
                    k, v = val[i]
                    ins.then_inc(semof(k), inc if chan is not None else 1)

        @block.tensor
        def _(eo):
            body('pe', eo)

        @block.scalar
        def _(eo):
            body('act', eo)

        @block.vector
        def _(eo):
            body('dve', eo)

        @block.gpsimd
        def _(eo):
            body('pool', eo)

        @block.sync
        def _(eo):
            body('sp', eo)


class Arena:
    def __init__(self, base_ap, size):
        self.base = base_ap
        self.size = size
        self.off = 0

    def mark(self):
        return self.off

    def reset(self, m):
        self.off = m

    def alloc(self, shape, dt):
        esz = 4 if dt == F32 else 2
        free = int(np.prod(shape[1:]))
        nb = free * esz
        nb_al = (nb + 63) // 64 * 64
        assert self.off + nb_al <= self.size, ("SBUF arena overflow", self.off, nb_al, self.size)
        v = self.base[:, self.off:self.off + nb].bitcast(dt)
        self.off += nb_al
        if len(shape) == 3:
            v = v.rearrange("p (a b) -> p a b", b=shape[2])
        elif len(shape) == 4:
            v = v.rearrange("p (a b c) -> p a b c", b=shape[2], c=shape[3])
        return v


def piece_srcs(idx):
    if idx < 4:
        return [('w_in', idx * 256, 256)]
    if idx < 8:
        return [('w_in', 3072 + (idx - 4) * 256, 256)]
    if idx < 24:
        ctp, t = divmod(idx - 8, 4)
        base = (5120, 6144, 4096, 7168)[t]
        return [('w_in', base + ctp * 256, 256)]
    if idx < 48:
        j, t = divmod(idx - 24, 3)
        if t == 0:
            return [('w_in', 8192 + j * 256, 256)]
        if t == 1:
            return [('w_in', 10240 + j * 256, 256)]
        return [('w_br', j * 256, 256)]
    return [('w_out', (idx - 48) * 256, 256)]


def build():
    nc = bass.Bass("TRN2", target_bir_lowering=False)
    P = Prog()
    dbg_outs = {}

    def dram_in(name, shape, dt=F32):
        return nc.dram_tensor(name, list(shape), dt, kind="ExternalInput").ap()

    x = dram_in("x", [SEGT, D])
    xh = dram_in("xh", [64, D])
    w_in = dram_in("w_in", [D, 12288])
    w_br = dram_in("w_br", [D, D])
    w_out = dram_in("w_out", [D, D])
    ng = dram_in("ng", [1, D])
    fg = dram_in("fg", [1, D])
    gng = dram_in("gng", [1, 1024])
    cw_d = dram_in("cw", [128, 24])
    dl = dram_in("dl", [1, 16])
    ropeq_d = dram_in("ropeq", [128, NCH * 128])
    ropek_d = dram_in("ropek", [128, NCH * 128])
    consts_d = dram_in("consts", [128, 1184])
    seg_d = dram_in("seg", [1, 16])
    y = nc.dram_tensor("y", [SEGT, D], F32, kind="ExternalOutput").ap()
    wsrc = {'w_in': w_in, 'w_br': w_br, 'w_out': w_out}

    skind = "ExternalOutput" if DEBUG else "Internal"
    kvs = nc.dram_tensor("kvs", [NCH * 128, 2048], BF16, kind=skind).ap()
    bst = nc.dram_tensor("bst", [NCH * 128, 1024], F32, kind=skind).ap()
    wsc = nc.dram_tensor("wsc", [NPIECE * 128, 16 * 256], BF16).ap()
    bounce = nc.dram_tensor("bounce", [128, 2048], F32).ap()
    gath = nc.dram_tensor("gath", [4 * 128, 2048], F32).ap()

    ARENA = 207 * 1024
    arena_t = nc.sbuf_tensor("arena", [128, ARENA], U8).__enter__()
    psum_t = nc.psum_tensor("psum", [128, 4096], F32).__enter__()
    A = Arena(arena_t, ARENA)
    ps_all = psum_t

    psptr = {'a': 0, 'b': 0}

    def psalloc(nb, pool):
        lo, n = (0, 4) if pool == 'a' else (4, 4)
        if pool == 'all':
            lo, n = 0, 8
        p = psptr.setdefault(pool, 0)
        if p % nb:
            p += nb - p % nb
        if p + nb > n:
            p = 0
        psptr[pool] = (p + nb) % n
        b0 = lo + p
        return ps_all[:, b0 * 512:(b0 + nb) * 512], [('ps', b0 + t) for t in range(nb)]

    def dma(q, out, in_, reads, writes, chan):
        P.add(q, lambda e: e.dma_start(out=out, in_=in_), reads, writes, chan=chan)

    def mm(out, lhsT, rhs, start, stop, reads, writes):
        P.add('pe', lambda e: e.matmul(out, lhsT, rhs, start=start, stop=stop), reads, writes)

    def tp(out, in_, ident, reads, writes):
        P.add('pe', lambda e: e.transpose(out, in_, ident), reads + ['ident'], writes)

    def act(out, in_, func, reads, writes, scale=None, accum=None):
        kw = {}
        if scale is not None:
            kw['scale'] = scale
        if accum is not None:
            kw['accum_out'] = accum
        P.add('act', lambda e: e.activation(out=out, in_=in_, func=func, **kw), reads, writes)

    def tt(eng, out, in0, in1, op, reads, writes):
        P.add(eng, lambda e: e.tensor_tensor(out=out, in0=in0, in1=in1, op=op), reads, writes)

    def ts(eng, out, in0, s1, s2, op0, op1, reads, writes):
        if s2 is None:
            P.add(eng, lambda e: e.tensor_scalar(out=out, in0=in0, scalar1=s1, scalar2=None, op0=op0), reads, writes)
        else:
            P.add(eng, lambda e: e.tensor_scalar(out=out, in0=in0, scalar1=s1, scalar2=s2, op0=op0, op1=op1), reads, writes)

    def stt(eng, out, in0, scalar, in1, op0, op1, reads, writes):
        P.add(eng, lambda e: e.scalar_tensor_tensor(out=out, in0=in0, scalar=scalar, in1=in1, op0=op0, op1=op1), reads, writes)

    def cp(eng, out, in_, reads, writes):
        if eng == 'act':
            act(out, in_, AF.Copy, reads, writes)
        else:
            P.add(eng, lambda e: e.tensor_copy(out=out, in_=in_), reads, writes)

    def xload(row0, xslot):
        dma('sp', xin[:, xslot, :], x[row0:row0 + 128, :], [], [('xin', xslot)], ('xin', xslot))

    def rsq(ap, keys):
        act(ap, ap, AF.Sqrt, keys, keys)
        P.add('dve', lambda e: e.reciprocal(out=ap, in_=ap), keys, keys)

    def dbg(name, ap, key, shape):
        if not DEBUG:
            return
        t = nc.dram_tensor("dbg_" + name, list(shape), F32, kind="ExternalOutput").ap()
        dbg_outs[name] = t
        dma('sp', t, ap, [key], [('dbg', name)], ('dbg', name))

    identb = A.alloc([128, 128], BF16)
    lg = A.alloc([128, 16], F32)
    DT = A.alloc([128, 1024], BF16)
    QWF = A.alloc([128, 1024], BF16)
    QWB = A.alloc([128, 1024], BF16)
    TF = A.alloc([128, 1024], BF16)
    g128 = A.alloc([128, 16], F32)
    coefB = A.alloc([128, NCH, 8], F32)
    ng_bc = A.alloc([128, D], F32)
    cwt = A.alloc([128, 24], F32)
    Fst = A.alloc([128, 2, 1024], F32)
    B_in = A.alloc([128, 1024], F32)
    ssq = A.alloc([128, 4], F32)
    rstd = A.alloc([128, 4], F32)
    xin = A.alloc([128, 2, D], F32)
    hbuf = A.alloc([128, 2, D], BF16)
    segc = A.alloc([128, 2, 4, 8], F32)
    pers_mark = A.mark()
    cst = A.alloc([128, 1184], F32)
    dlb = A.alloc([128, 16], F32)
    sm = A.alloc([128, 6, 16], F32)
    segb = A.alloc([128, 16], F32)
    TB = A.alloc([128, 1024], BF16)
    coefF = A.alloc([128, NCH, 8], F32)
    tmpA = A.alloc([128, 128], F32)
    tmpB = A.alloc([128, 128], F32)
    rt_box = [A.alloc([128, 4, 512], F32)]

    C_ID, C_EF, C_EB, C_MF, C_MB, C_EL1, C_E128L, C_EC, C_E127C, C_ENB = [k * 128 for k in range(10)]

    dma('sp', cst, consts_d, [], ['cst'], 'cst')
    dma('sp', dlb, dl.partition_broadcast(128), [], ['dlb'], 'dlb')
    dma('sp', segb, seg_d.partition_broadcast(128), [], ['segb'], 'segb')
    dma('sp', ng_bc, ng.partition_broadcast(128), [], ['ng_bc'], 'ng_bc')
    dma('sp', cwt, cw_d, [], ['cwt'], 'cwt')
    cp('dve', identb, cst[:, C_ID:C_ID + 128], ['cst'], ['ident'])
    s_e, s_w, s_l, s_d, s_r = [sm[:, k, :] for k in range(5)]
    act(s_e, dlb, AF.Exp, ['dlb'], ['s_e'], scale=-1.0)
    ts('dve', s_w, s_e, 1.0, None, ALU.add, None, ['s_e'], ['s_w'])
    act(s_l, s_w, AF.Ln, ['s_w'], ['s_l'])
    ts('dve', s_d, s_w, -1.0, 1e-30, ALU.add, ALU.max, ['s_w'], ['s_d'])
    P.add('dve', lambda e: e.reciprocal(out=s_d, in_=s_d), ['s_d'], ['s_d'])
    tt('dve', s_r, s_e, s_d, ALU.mult, ['s_e', 's_d'], ['s_r'])
    stt('dve', lg, s_l, -1.0, s_r, ALU.mult, ALU.mult, ['s_l', 's_r'], ['lg'])
    for h in range(NH):
        lf = lg[:, h:h + 1]
        lb = lg[:, 8 + h:9 + h]
        hs = slice(h * 128, (h + 1) * 128)
        act(tmpA, cst[:, C_EF:C_EF + 128], AF.Exp, ['cst', 'lg'], ['tmpA'], scale=lf)
        act(tmpB, cst[:, C_EB:C_EB + 128], AF.Exp, ['cst', 'lg'], ['tmpB'], scale=lb)
        tt('dve', tmpA, tmpA, cst[:, C_MF:C_MF + 128], ALU.mult, ['tmpA', 'cst'], ['tmpA'])
        tt('dve', tmpB, tmpB, cst[:, C_MB:C_MB + 128], ALU.mult, ['tmpB', 'cst'], ['tmpB'])
        tt('dve', DT[:, hs], tmpA, tmpB, ALU.add, ['tmpA', 'tmpB'], ['DT'])
        act(QWF[:, hs], cst[:, C_EC:C_EC + 128], AF.Exp, ['cst', 'lg'], ['QWF'], scale=lf)
        act(QWB[:, hs], cst[:, C_E127C:C_E127C + 128], AF.Exp, ['cst', 'lg'], ['QWB'], scale=lb)
        act(TB[:, hs], cst[:, C_EL1:C_EL1 + 128], AF.Exp, ['cst', 'lg'], ['TB'], scale=lb)
        act(TF[:, hs], cst[:, C_E128L:C_E128L + 128], AF.Exp, ['cst', 'lg'], ['TF'], scale=lf)
        act(coefF[:, :, h], cst[:, C_ENB:C_ENB + NCH], AF.Exp, ['cst', 'lg'], ['coefF'], scale=lf)
        act(coefB[:, :, h], cst[:, C_ENB:C_ENB + NCH], AF.Exp, ['cst', 'lg'], ['coefB'], scale=lb)
    act(g128, lg, AF.Exp, ['lg'], ['g128'], scale=128.0)
    for r in range(4):
        act(segc[:, 0, r, :], lg[:, 0:8], AF.Exp, ['lg', 'segb'], ['segc'], scale=segb[:, r:r + 1])
        act(segc[:, 1, r, :], lg[:, 8:16], AF.Exp, ['lg', 'segb'], ['segc'], scale=segb[:, 4 + r:5 + r])
        ts('dve', segc[:, 0, r, :], segc[:, 0, r, :], segb[:, 8 + r:9 + r], None, ALU.mult, None, ['segc', 'segb'], ['segc'])
        ts('dve', segc[:, 1, r, :], segc[:, 1, r, :], segb[:, 12 + r:13 + r], None, ALU.mult, None, ['segc', 'segb'], ['segc'])
    dbg('lg', lg, 'lg', [128, 16])
    dbg('segc', segc.rearrange("p a b c -> p (a b c)"), 'segc', [128, 64])

    def prep_elem(row0, nrows, xslot, src=None, hslot=0, noload=False):
        src = x if src is None else src
        sq, rs = ssq[:nrows, hslot:hslot + 1], rstd[:nrows, hslot:hslot + 1]
        ksq, krs = ('ssq', hslot), ('rstd', hslot)
        if not noload:
            dma('sp', xin[:nrows, xslot, :], src[row0:row0 + nrows, :], [], [('xin', xslot)], ('xin', xslot))
        act(hbuf[:nrows, hslot, :], xin[:nrows, xslot, :], AF.Square, [('xin', xslot)], [('hbuf', hslot), ksq], accum=sq)
        ts('dve', rs, sq, 1.0 / D, EPS, ALU.mult, ALU.add, [ksq], [krs])
        rsq(rs, [krs])
        stt('dve', hbuf[:nrows, hslot, :], xin[:nrows, xslot, :], rs, ng_bc[:nrows], ALU.mult, ALU.mult,
            [('xin', xslot), krs, 'ng_bc'], [('hbuf', hslot)])

    def prep_pe(nrows, dst, dkeys, hslot=0, bank=None):
        if bank is None:
            pst, pk = psalloc(2, 'b')
        else:
            pst, pk = ps_all[:, bank * 512:(bank + 2) * 512], [('ps', bank), ('ps', bank + 1)]
        pstb = pst.bitcast(BF16)
        for k in range(16):
            tp(pstb[:, k * nrows:(k + 1) * nrows], hbuf[:nrows, hslot, k * 128:(k + 1) * 128], identb[:nrows, :nrows],
               [('hbuf', hslot)], pk)
        v = pstb[:, 0:16 * nrows].rearrange("p (k t) -> p k t", t=nrows)
        cp('act', dst[:, 0:8, :], v[:, 0:8, :], pk, dkeys)
        cp('dve', dst[:, 8:16, :], v[:, 8:16, :], pk, dkeys)

    def rotary(ps3, nh, cos, sin, dst3, rkeys, wkeys, comb='pool'):
        cb = cos.unsqueeze(1).broadcast_to([128, nh, 64])
        sb = sin.unsqueeze(1).broadcast_to([128, nh, 64])
        x1 = ps3[:, :, 0:64]
        x2 = ps3[:, :, 64:128]
        t = [rt_box[0][:, k, 0:nh * 64].rearrange("p (h e) -> p h e", e=64) for k in range(4)]
        tt('dve', t[0], x1, cb, ALU.mult, rkeys, [('rt', 0)])
        tt('dve', t[1], x2, sb, ALU.mult, rkeys, [('rt', 1)])
        tt('dve', t[2], x2, cb, ALU.mult, rkeys, [('rt', 2)])
        tt('dve', t[3], x1, sb, ALU.mult, rkeys, [('rt', 3)])
        tt(comb, dst3[:, :, 0:64], t[0], t[1], ALU.subtract, [('rt', 0), ('rt', 1)], wkeys)
        tt(comb, dst3[:, :, 64:128], t[2], t[3], ALU.add, [('rt', 2), ('rt', 3)], wkeys)

    wkv = A.alloc([128, 16, 2048], BF16)
    ropek = A.alloc([128, NCH, 128], F32)
    hT1 = A.alloc([128, 2, 16, 128], BF16)
    kvo = A.alloc([128, 2, 2048], BF16)
    Bst = A.alloc([128, 2, 1024], F32)
    Fac = A.alloc([128, 2, 1024], F32)

    w_in_v = w_in.rearrange("(k p) f -> p k f", p=128)
    for q4 in range(4):
        P.add('pool', (lambda q4: lambda e: e.dma_start(out=wkv[:, :, q4 * 512:(q4 + 1) * 512],
                                                        in_=w_in_v[:, :, 1024 + q4 * 512:1024 + (q4 + 1) * 512]))(q4),
              [], [('wkv', q4)], chan=('wkvld', q4))
    dma('sp', ropek, ropek_d.rearrange("p (n c) -> p n c", c=128), [], ['ropek'], 'ropek')
    P.add('dve', lambda e: e.memset(Bst[:, 0, :], 0.0), [], [('Bst', 0)])
    P.add('dve', lambda e: e.memset(Fac[:, 0, :], 0.0), [], [('Fac', 0)])

    halo_list = [8 + 4 * ctp + t for ctp in range(4) for t in range(2)]
    conv_list = halo_list + [i for i in range(NPIECE) if i not in halo_list]
    conv_pos = [0]
    NCV = 32

    def issue_conv(nmax):
        for _ in range(nmax):
            if conv_pos[0] >= len(conv_list):
                return
            idx = conv_list[conv_pos[0]]
            cch = conv_pos[0] % NCV
            conv_pos[0] += 1
            dstp = wsc[idx * 128:(idx + 1) * 128, :].rearrange("p (k f) -> p k f", f=256)
            col = 0
            for (sn, c0, wd) in piece_srcs(idx):
                srcv = wsrc[sn].rearrange("(k p) f -> p k f", p=128)[:, :, c0:c0 + wd]
                P.add('pool', (lambda o, s: lambda e: e.dma_start(out=o, in_=s))(dstp[:, :, col:col + wd], srcv),
                      [], [('wsc', idx)], chan=('cv', cch))
                col += wd

    k32 = A.alloc([128, 1024], F32)
    vBF = A.alloc([128, 2, 2, 1024], BF16)

    def PSB(b, nb):
        return ps_all[:, b * 512:(b + nb) * 512], [('ps', b + t) for t in range(nb)]

    def p1_prep_e(n):
        s_ = n % 2
        prep_elem(n * 128, 128, s_, hslot=s_, noload=True)
        if n - 2 >= 0:
            xload((n - 2) * 128, s_)

    def p1_prep_p(n):
        s_ = n % 2
        prep_pe(128, hT1[:, s_], [('hT1', s_)], hslot=s_, bank=4)

    def p1_proj(n):
        hs_ = n % 2
        o = n % 2
        psk, kk = PSB(0, 2)
        psv, kvk = PSB(2, 2)
        for (psx, kx, cbase, q0) in ((psk, kk, 0, 0), (psv, kvk, 1024, 2)):
            for ft in range(2):
                for k in range(16):
                    mm(psx[:, ft * 512:(ft + 1) * 512], hT1[:, hs_, k, :], wkv[:, k, cbase + ft * 512:cbase + (ft + 1) * 512],
                       k == 0, k == 15, [('hT1', hs_), ('wkv', q0 + ft)], [kx[ft]])
            if cbase == 0:
                cp('act', k32, psk, kk, ['k32'])
            else:
                cp('act', kvo[:, o, 1024:2048], psv, kvk, [('kvo', o, 'v')])

    def p1_post_elem(n):
        o = n % 2
        rotary(k32.rearrange("p (h e) -> p h e", e=128), 8, ropek[:, n, 0:64], ropek[:, n, 64:128],
               kvo[:, o, 0:1024].rearrange("p (h e) -> p h e", e=128), ['k32', 'ropek'], [('kvo', o, 'k')], comb='dve')
        dma('sp', kvs[n * 128:(n + 1) * 128, :], kvo[:, o, :], [('kvo', o, 'k'), ('kvo', o, 'v')], [('kvs', n)], ('kvst', o))
        tt('dve', vBF[:, o, 0, :], kvo[:, o, 1024:2048], TB, ALU.mult, [('kvo', o, 'v'), 'TB'], [('vB', o)])
        tt('dve', vBF[:, o, 1, :], kvo[:, o, 1024:2048], TF, ALU.mult, [('kvo', o, 'v'), 'TF'], [('vF', o)])

    def p1_post_pe(n):
        o = n % 2
        cur = (NCH - 1 - n) % 2
        new = 1 - cur
        dma('sp', bst[n * 128:(n + 1) * 128, :], Bst[:, cur, :], [('Bst', cur)], [('bst', n)], ('bstst', cur))
        pkb, kb = PSB(6, 2)
        pkf, kf = PSB(4, 2)
        for h in range(NH):
            hs = slice(h * 128, (h + 1) * 128)
            mm(pkb[:, hs], kvo[:, o, hs], vBF[:, o, 0, hs], True, True, [('kvo', o, 'k'), ('vB', o)], [kb[h // 4]])
        for h in range(NH):
            hs = slice(h * 128, (h + 1) * 128)
            mm(pkf[:, hs], kvo[:, o, hs], vBF[:, o, 1, hs], True, True, [('kvo', o, 'k'), ('vF', o)], [kf[h // 4]])
        for h in range(NH):
            hs = slice(h * 128, (h + 1) * 128)
            stt('dve', Bst[:, new, hs], Bst[:, cur, hs], g128[:, 8 + h:9 + h], pkb[:, hs], ALU.mult, ALU.add,
                [('Bst', cur), 'g128', kb[h // 4]], [('Bst', new)])
        for h in range(NH):
            hs = slice(h * 128, (h + 1) * 128)
            stt('dve', Fac[:, new, hs], pkf[:, hs], coefF[:, n, h:h + 1], Fac[:, cur, hs], ALU.mult, ALU.add,
                [('Fac', cur), 'coefF', kf[h // 4]], [('Fac', new)])

    xload((NCH - 1) * 128, (NCH - 1) % 2)
    xload((NCH - 2) * 128, (NCH - 2) % 2)
    p1_prep_e(NCH - 1)
    p1_prep_p(NCH - 1)
    issue_conv(100)
    for n in range(NCH - 1, -1, -1):
        if n > 0:
            p1_prep_e(n - 1)
        p1_proj(n)
        if n > 0:
            p1_prep_p(n - 1)
        if n < NCH - 1:
            p1_post_pe(n + 1)
        p1_post_elem(n)
    p1_post_pe(0)
    dma('sp', xin[:64, 1, :], xh[0:64, :], [], [('xin', 1)], ('xin', 1))
    xload(0, 0)
    issue_conv(100)
    fin = NCH % 2
    dma('sp', bounce[:, 0:1024], Fac[:, fin, :], [('Fac', fin)], ['bounce'], 'bnc0')
    dma('sp', bounce[:, 1024:2048], Bst[:, fin, :], [('Bst', fin)], ['bounce2'], 'bnc1')
    P.add('pool', lambda e: e.collective_compute("AllGather", ALU.bypass, replica_groups=[[0, 1, 2, 3], [4, 5, 6, 7]],
                                                 ins=[bounce], outs=[gath]),
          ['bounce', 'bounce2'], ['gath'], chan='cc', inc=1)
    def finish_exchange(Gbuf, gkeys):
        gv = gath.rearrange("(r p) f -> p r f", p=128)
        dma('sp', Gbuf, gv[:, :, 0:1024], ['gath'], gkeys, 'gld')
        for h in range(NH):
            hs = slice(h * 128, (h + 1) * 128)
            ts('dve', Fst[:, 0, hs], Gbuf[:, 0, hs], segc[:, 0, 0, h:h + 1], None, ALU.mult, None, gkeys + ['segc'], [('Fst', 0)])
            for r in range(1, 4):
                stt('dve', Fst[:, 0, hs], Gbuf[:, r, hs], segc[:, 0, r, h:h + 1], Fst[:, 0, hs], ALU.mult, ALU.add,
                    gkeys + ['segc', ('Fst', 0)], [('Fst', 0)])
        dma('sp', Gbuf, gv[:, :, 1024:2048], ['gath'], gkeys, 'gld')
        for h in range(NH):
            hs = slice(h * 128, (h + 1) * 128)
            ts('dve', B_in[:, hs], Gbuf[:, 0, hs], segc[:, 1, 0, h:h + 1], None, ALU.mult, None, gkeys + ['segc'], ['B_in'])
            for r in range(1, 4):
                stt('dve', B_in[:, hs], Gbuf[:, r, hs], segc[:, 1, r, h:h + 1], B_in[:, hs], ALU.mult, ALU.add,
                    gkeys + ['segc', 'B_in'], ['B_in'])
        dbg('F_in', Fst[:, 0, :], ('Fst', 0), [128, 1024])
        dbg('B_in', B_in, 'B_in', [128, 1024])

    if STAGE < 2:
        Gb = wkv.rearrange("p k f -> p (k f)")[:, 0:8192].bitcast(F32).rearrange("p (r f) -> p r f", f=1024)
        finish_exchange(Gb, [('wkv', q) for q in range(4)])
    dbg('Fsum', Fac[:, fin, :], ('Fac', fin), [128, 1024])

    out_keys = []
    if STAGE >= 2:
        P.set_fence(skip_chans=('cc',) + tuple(('cv', k) for k in range(32)))
        A.reset(pers_mark)
        out_keys = phase2(nc, P, A, locals())

    P.add('sp', lambda e: None, reads=out_keys + [('dbg', k) for k in dbg_outs] + ['B_in', ('Fst', 0)] +
          ([('kvs', n) for n in range(NCH)] + [('bst', n) for n in range(NCH)] if STAGE < 2 else []), writes=[])
    if STAGE < 2:
        dma('sp', y[0:128, :], x[0:128, :], [], [('y', 0)], 'ycopy')
        P.add('sp', lambda e: None, reads=[('y', 0)], writes=[])

    with nc.allow_low_precision("bf16 matmul operands, fp32 accumulation"):
        with nc.Block() as block:
            P.emit(nc, block)
    return nc, dbg_outs


def phase2(nc, P, A, env):
    g = env
    x, xh, y, kvs, bst, wsc = g['x'], g['xh'], g['y'], g['kvs'], g['bst'], g['wsc']
    dma, mm, tp, act, tt, ts, stt, cp = g['dma'], g['mm'], g['tp'], g['act'], g['tt'], g['ts'], g['stt'], g['cp']
    psalloc, prep_elem, prep_pe, rotary, rsq = g['psalloc'], g['prep_elem'], g['prep_pe'], g['rotary'], g['rsq']
    identb, xin, hbuf, ssq, rstd = g['identb'], g['xin'], g['hbuf'], g['ssq'], g['rstd']
    DT, QWF, QWB, TF, g128, coefB, Fst, B_in, cwt = (g['DT'], g['QWF'], g['QWB'], g['TF'], g['g128'], g['coefB'],
                                                     g['Fst'], g['B_in'], g['cwt'])
    fg, gng, ropeq_d, dbg = g['fg'], g['gng'], g['ropeq_d'], g['dbg']

    g['rt_box'][0] = A.alloc([128, 4, 128], F32)
    fg_bc = A.alloc([128, D], F32)
    gn_bc = A.alloc([128, 1024], F32)
    NW = 4
    Wr = A.alloc([128, NW, 16, 256], BF16)
    ropeq = A.alloc([128, 2, 2, 128], F32)
    hT = A.alloc([128, 2, 16, T], BF16)
    uh = A.alloc([128, 8, 64], F32)
    q_rot = A.alloc([128, 2, 1024], BF16)
    sgn = A.alloc([128, 2, 1024], BF16)
    ccs = A.alloc([128, 2, T], F32)
    ubuf = A.alloc([128, 2, T + 2], F32)
    tacc = A.alloc([128, 2, T], F32)
    sgc = A.alloc([128, 2, T], F32)
    bconvT = A.alloc([128, 8, T], BF16)
    kvb = A.alloc([128, 2, 2048], BF16)
    Bl = A.alloc([128, 1024], F32)
    qT = A.alloc([128, 1024], BF16)
    kT = A.alloc([128, 1024], BF16)
    qTf = A.alloc([128, 1024], BF16)
    qTb = A.alloc([128, 1024], BF16)
    Bfull = A.alloc([128, 1024], BF16)
    Fbf = A.alloc([128, 1024], BF16)
    vF = A.alloc([128, 1024], BF16)
    STm = A.alloc([128, 1024], BF16)
    ssqo = A.alloc([128, 8], F32)
    rso = A.alloc([128, 8], F32)
    bret = A.alloc([128, 1024], BF16)
    hTh = bret.rearrange("p (k t) -> p k t", t=64)
    bretT = A.alloc([128, 8, T], BF16)
    gates = A.alloc([128, 2, 2, 256], BF16)
    m12 = A.alloc([128, 2, 2, 256], F32)
    merged = A.alloc([128, 2, 2, 256], BF16)
    mT = A.alloc([128, 16, T], BF16)
    xo = A.alloc([128, 2, D], F32)

    dma('sp', fg_bc, fg.partition_broadcast(128), [], ['fg_bc'], 'fg_bc')
    dma('sp', gn_bc, gng.partition_broadcast(128), [], ['gn_bc'], 'gn_bc')

    if REORDER:
        wseq = [8 + 4 * ctp + t for ctp in range(4) for t in range(2)] + list(range(8))
        for i in range(NT):
            wseq += list(range(8, 48)) + (list(range(8)) if i + 1 < NT else []) + list(range(48, 56))
    else:
        wseq = [8 + 4 * ctp + t for ctp in range(4) for t in range(2)]
        for i in range(NT):
            wseq += list(range(NPIECE))
    wstate = {'issued': 0, 'next': 0}

    def wget(expect_idx):
        gq = wstate['next']
        assert wseq[gq] == expect_idx, (gq, wseq[gq], expect_idx)
        while wstate['issued'] < min(gq + NW, len(wseq)):
            q = wstate['issued']
            s = q % NW
            idx = wseq[q]
            dma('sp', Wr[:, s], wsc[idx * 128:(idx + 1) * 128, :].rearrange("p (k f) -> p k f", f=256),
                [('wsc', idx)], [('W', s)], ('W', s))
            wstate['issued'] += 1
        wstate['next'] += 1
        s = gq % NW
        return Wr[:, s], ('W', s)

    prep_elem(0, 64, 1, src=xh, noload=True)
    prep_pe(64, hTh, ['bret'])
    g['xload'](128, 1)
    ccs_f = ccs.rearrange("p g t -> p (g t)")
    for ctp in range(4):
        pss = []
        for t in range(2):
            W, wk = wget(8 + 4 * ctp + t)
            psh, hk = psalloc(1, 'a')
            for gq in range(2):
                for k in range(16):
                    mm(psh[:, gq * 64:(gq + 1) * 64], W[:, k, gq * 128:(gq + 1) * 128], hTh[:, k, :], k == 0, k == 15,
                       [wk, 'bret'], hk)
            pss.append((psh, hk))
            if t == 0:
                cp('act', ccs_f[:, 0:128], psh[:, 0:128], hk, ['ccs'])
        psh, hk = pss[1]
        tt('dve', uh[:, 2 * ctp:2 * ctp + 2, :], psh[:, 0:128].rearrange("p (g t) -> p g t", t=64),
           ccs_f[:, 0:128].rearrange("p (g t) -> p g t", t=64), ALU.mult, hk + ['ccs'], ['uh'])
    dbg('uh', uh.rearrange("p a b -> p (a b)"), 'uh', [128, 512])

    xs = {'n': 0}

    def p2_prep_elem(i):
        slots = []
        for c in range(2):
            s = xs['n'] % 2
            xs['n'] += 1
            slots.append(s)
        return slots

    def p2_prep_e(i):
        hs_ = i % 2
        dma('sp', ropeq[:, hs_], ropeq_d.rearrange("p (n c) -> p n c", c=128)[:, 2 * i:2 * i + 2, :], [],
            [('ropeq', hs_)], ('ropeq', hs_))
        for c in range(2):
            prep_elem((2 * i + c) * 128, 128, c, hslot=c, noload=True)
            if i + 1 < NT:
                g['xload']((2 * (i + 1) + c) * 128, c)

    def p2_prep_p(i):
        hs_ = i % 2
        for c in range(2):
            pst, pk = psalloc(2, 'b')
            pstb = pst.bitcast(BF16)
            for k in range(16):
                tp(pstb[:, k * 128:(k + 1) * 128], hbuf[:, c, k * 128:(k + 1) * 128], identb, [('hbuf', c)], pk)
            v = pstb.rearrange("p (k t) -> p k t", t=128)
            cp('act', hT[:, hs_, 0:8, c * 128:(c + 1) * 128], v[:, 0:8, :], pk, [('hT', hs_, c)])
            cp('dve', hT[:, hs_, 8:16, c * 128:(c + 1) * 128], v[:, 8:16, :], pk, [('hT', hs_, c)])

    def tok_piece(i, idx):
        hs_ = i % 2
        W, wk = wget(idx)
        ps, pk = psalloc(1, 'a')
        for c in range(2):
            for k in range(16):
                mm(ps[:, c * 256:(c + 1) * 256], hT[:, hs_, k, c * 128:(c + 1) * 128], W[:, k, :], k == 0, k == 15,
                   [('hT', hs_, c), wk], pk)
        return ps.rearrange("p (c f) -> p c f", f=256), pk

    def feat_piece(i, idx):
        hs_ = i % 2
        W, wk = wget(idx)
        ps, pk = psalloc(1, 'a')
        for gq in range(2):
            for k in range(16):
                mm(ps[:, gq * 256:(gq + 1) * 256], W[:, k, gq * 128:(gq + 1) * 128], hT[:, hs_, k, :], k == 0, k == 15,
                   [('hT', hs_, 0), ('hT', hs_, 1), wk], pk)
        return ps.rearrange("p (g t) -> p g t", t=256), pk

    def conv_A(i, ctp):
        ps, pk = feat_piece(i, 8 + 4 * ctp)
        cp('act', ccs, ps, pk, ['ccs'])
        ps, pk = feat_piece(i, 9 + 4 * ctp)
        tt('dve', ubuf[:, :, 1:T + 1], ps, ccs, ALU.mult, pk + ['ccs'], ['ubuf'])
        for gq in range(2):
            ct = 2 * ctp + gq
            ta = tacc[:, gq, :]
            cp('pool', ubuf[:, gq, 0:1], uh[:, ct, 2 * i:2 * i + 1], ['uh'], ['ubuf'])
            cp('pool', ubuf[:, gq, T + 1:T + 2], uh[:, ct, 2 * i + 3:2 * i + 4], ['uh'], ['ubuf'])
            ts('dve', ta, ubuf[:, gq, 0:T], cwt[:, ct * 3:ct * 3 + 1], None, ALU.mult, None, ['ubuf', 'cwt'], ['tacc'])
            stt('dve', ta, ubuf[:, gq, 1:T + 1], cwt[:, ct * 3 + 1:ct * 3 + 2], ta, ALU.mult, ALU.add, ['ubuf', 'cwt', 'tacc'], ['tacc'])
            stt('dve', ta, ubuf[:, gq, 2:T + 2], cwt[:, ct * 3 + 2:ct * 3 + 3], ta, ALU.mult, ALU.add, ['ubuf', 'cwt', 'tacc'], ['tacc'])

    def conv_B(i, ctp):
        ps, pk = feat_piece(i, 10 + 4 * ctp)
        tt('dve', tacc, ps, tacc, ALU.mult, pk + ['tacc'], ['tacc'])
        ps, pk = feat_piece(i, 11 + 4 * ctp)
        act(sgc, ps, AF.Silu, pk, ['sgc'])
        tt('pool', bconvT[:, 2 * ctp:2 * ctp + 2, :], tacc, sgc, ALU.mult, ['tacc', 'sgc'],
           [('bconvT', 2 * ctp), ('bconvT', 2 * ctp + 1)])

    def conv_pair(i, s8):
        if s8 % 2 == 0:
            conv_A(i, s8 // 2)
        else:
            conv_B(i, s8 // 2)

    def ret_stage(i, c, st, S):
        n = 2 * i + c
        cur = n % 2
        new = 1 - cur
        H3 = lambda ap: ap.rearrange("p (h e) -> p h e", e=128)
        if st == 0:
            psq, qk = psalloc(1, 'b')
            psk, kk = psalloc(1, 'b')
            psqb, pskb = psq.bitcast(BF16), psk.bitcast(BF16)
            for h in range(NH):
                hs = slice(h * 128, (h + 1) * 128)
                tp(psqb[:, hs], q_rot[:, c, hs], identb, [('q_rot', c)], qk)
            for h in range(NH):
                hs = slice(h * 128, (h + 1) * 128)
                tp(pskb[:, hs], kvb[:, c, hs], identb, [('kvb', c)], kk)
            cp('act', qT, psqb, qk, ['qT'])
            cp('act', kT, pskb, kk, ['kT'])
            tt('dve', qTf, psqb, QWF, ALU.mult, qk + ['QWF'], ['qTf'])
            tt('dve', qTb, psqb, QWB, ALU.mult, qk + ['QWB'], ['qTb'])
            if n == 0:
                dma('sp', Bl, bst[0:128, :], [('bst', 0)], ['Bl'], 'Bl')
            for h in range(NH):
                hs = slice(h * 128, (h + 1) * 128)
                stt('dve', Bfull[:, hs], B_in[:, hs], coefB[:, n, h:h + 1], Bl[:, hs], ALU.mult, ALU.add,
                    ['B_in', 'coefB', 'Bl'], ['Bfull'])
            if n + 1 < NCH:
                dma('sp', Bl, bst[(n + 1) * 128:(n + 2) * 128, :], [('bst', n + 1)], ['Bl'], 'Bl')
            cp('act', Fbf, Fst[:, cur, :], [('Fst', cur)], ['Fbf'])
            tt('pool', vF, kvb[:, c, 1024:2048], TF, ALU.mult, [('kvb', c), 'TF'], ['vF'])
        elif st == 1:
            psS, sk = psalloc(2, 'b')
            for h in range(NH):
                hs = slice(h * 128, (h + 1) * 128)
                mm(psS[:, hs], kT[:, hs], qT[:, hs], True, True, ['kT', 'qT'], [sk[h // 4]])
            tt('dve', STm, psS, DT, ALU.mult, sk + ['DT'], ['STm'])
        elif st == 2:
            psO, ok = psalloc(2, 'b')
            S['psO'], S['ok'] = psO, ok
            for h in range(NH):
                hs = slice(h * 128, (h + 1) * 128)
                vs = slice(1024 + h * 128, 1024 + (h + 1) * 128)
                mm(psO[:, hs], STm[:, hs], kvb[:, c, vs], True, False, ['STm', ('kvb', c)], [ok[h // 4]])
                mm(psO[:, hs], qTf[:, hs], Fbf[:, hs], False, False, ['qTf', 'Fbf'], [ok[h // 4]])
                mm(psO[:, hs], qTb[:, hs], Bfull[:, hs], False, True, ['qTb', 'Bfull'], [ok[h // 4]])
            psK, kk = psalloc(2, 'b')
            for h in range(NH):
                hs = slice(h * 128, (h + 1) * 128)
                mm(psK[:, hs], kvb[:, c, hs], vF[:, hs], True, True, [('kvb', c), 'vF'], [kk[h // 4]])
            for h in range(NH):
                hs = slice(h * 128, (h + 1) * 128)
                act(bret[:, hs], psO[:, hs], AF.Square, [ok[h // 4]], ['bret', 'ssqo'], accum=ssqo[:, h:h + 1])
            ts('dve', rso, ssqo, 1.0 / 128, EPS, ALU.mult, ALU.add, ['ssqo'], ['rso'])
            rsq(rso, ['rso'])
            for h in range(NH):
                hs = slice(h * 128, (h + 1) * 128)
                stt('dve', bret[:, hs], psO[:, hs], rso[:, h:h + 1], sgn[:, c, hs], ALU.mult, ALU.mult,
                    [ok[h // 4], 'rso', ('sgn', c)], ['bret'])
            for h in range(NH):
                hs = slice(h * 128, (h + 1) * 128)
                stt('dve', Fst[:, new, hs], Fst[:, cur, hs], g128[:, h:h + 1], psK[:, hs], ALU.mult, ALU.add,
                    [('Fst', cur), 'g128', kk[h // 4]], [('Fst', new)])
        else:
            psb, bk = psalloc(1, 'b')
            psbb = psb.bitcast(BF16)
            for h in range(NH):
                hs = slice(h * 128, (h + 1) * 128)
                tp(psbb[:, hs], bret[:, hs], identb, ['bret'], bk)
            cp('act', bretT[:, :, c * 128:(c + 1) * 128], H3(psbb), bk, [('bretT', c)])

    def merged_T(i, j):
        sl = j % 2
        pst, pk = psalloc(1, 'b')
        pstb = pst.bitcast(BF16)
        for c in range(2):
            for kk in range(2):
                tp(pstb[:, kk * 256 + c * 128: kk * 256 + (c + 1) * 128], merged[:, sl, c, kk * 128:(kk + 1) * 128], identb,
                   [('merged', sl)], pk)
        eng = 'act' if j % 2 == 0 else 'dve'
        cp(eng, mT[:, 2 * j:2 * j + 2, :], pstb[:, 0:512].rearrange("p (k t) -> p k t", t=256), pk, [('mT', j)])

    out_keys = []

    def kv_loads(i):
        for c in range(2):
            n = 2 * i + c
            dma('sp', kvb[:, c, :], kvs[n * 128:(n + 1) * 128, :], [('kvs', n)], [('kvb', c)], ('kvb', c))

    def qg_pieces(i):
        hs_ = i % 2
        for p in range(4):
            ps, pk = tok_piece(i, p)
            for c in range(2):
                rotary(ps[:, c, :].rearrange("p (h e) -> p h e", e=128), 2, ropeq[:, hs_, c, 0:64], ropeq[:, hs_, c, 64:128],
                       q_rot[:, c, p * 256:(p + 1) * 256].rearrange("p (h e) -> p h e", e=128),
                       pk + [('ropeq', hs_)], [('q_rot', c)])
        for p in range(4):
            ps, pk = tok_piece(i, 4 + p)
            act(sgn[:, :, p * 256:(p + 1) * 256], ps, AF.Silu, pk, [('sgn', 0), ('sgn', 1)])
            tt('pool', sgn[:, :, p * 256:(p + 1) * 256], sgn[:, :, p * 256:(p + 1) * 256],
               gn_bc[:, p * 256:(p + 1) * 256].unsqueeze(1).broadcast_to([128, 2, 256]), ALU.mult,
               [('sgn', 0), ('sgn', 1), 'gn_bc'], [('sgn', 0), ('sgn', 1)])

    p2_prep_e(0)
    p2_prep_p(0)
    kv_loads(0)
    if REORDER:
        qg_pieces(0)
    for i in range(NT):
        hs_ = i % 2
        if not REORDER:
            qg_pieces(i)
        S = {}
        if i == 0:
            for s8 in range(8):
                conv_pair(i, s8)
            p2_prep_e(i + 1)
            g['finish_exchange'](xo.rearrange("p c f -> p (c f)").rearrange("p (r f) -> p r f", f=1024), [('xo', 0), ('xo', 1)])
            for s8 in range(8):
                ret_stage(i, s8 // 4, s8 % 4, S)
        else:
            for s8 in range(8):
                ret_stage(i, s8 // 4, s8 % 4, S)
                conv_pair(i, s8)
                if s8 == 0 and i + 1 < NT:
                    p2_prep_e(i + 1)
        if i + 1 < NT:
            p2_prep_p(i + 1)
            kv_loads(i + 1)
        dma('sp', xo, x[i * T:(i + 1) * T, :].rearrange("(c p) f -> p c f", p=128), [], [('xo', 0), ('xo', 1)], 'xo')
        for j in range(8):
            for br in range(2):
                ps, pk = feat_piece(i, 24 + 3 * j + br)
                act(gates[:, br, :, :], ps, AF.Sigmoid, pk, [('gates', br)])
            W, wk = wget(24 + 3 * j + 2)
            psu, uk = psalloc(2, 'b')
            psu4 = psu.rearrange("p (hf b t) -> p hf b t", b=2, t=256)
            for hf in range(2):
                for br in range(2):
                    src = bretT if br == 0 else bconvT
                    for kf in range(8):
                        rk = [('bretT', 0), ('bretT', 1)] if br == 0 else [('bconvT', kf)]
                        mm(psu4[:, hf, br, :], W[:, br * 8 + kf, hf * 128:(hf + 1) * 128], src[:, kf, :], kf == 0, kf == 7,
                           rk + [wk], [uk[hf]])
            tt('dve', m12[:, :, 0, :], psu4[:, :, 0, :], gates[:, 0, :, :], ALU.mult, uk + [('gates', 0)], [('m12', 0)])
            tt('dve', m12[:, :, 1, :], psu4[:, :, 1, :], gates[:, 1, :, :], ALU.mult, uk + [('gates', 1)], [('m12', 1)])
            tt('pool', mT[:, 2 * j:2 * j + 2, :], m12[:, :, 0, :], m12[:, :, 1, :], ALU.add, [('m12', 0), ('m12', 1)], [('mT', j)])
        if REORDER and i + 1 < NT:
            qg_pieces(i + 1)
        for j in range(8):
            W, wk = wget(48 + j)
            ps, pk = psalloc(1, 'a')
            for c in range(2):
                for k in range(16):
                    mm(ps[:, c * 256:(c + 1) * 256], mT[:, k, c * 128:(c + 1) * 128], W[:, k, :], k == 0, k == 15,
                       [('mT', k // 2), wk], pk)
            xv = xo[:, :, j * 256:(j + 1) * 256]
            tt('dve', xv, ps.rearrange("p (c f) -> p c f", f=256), xv, ALU.add, pk + [('xo', 0), ('xo', 1)], [('xo', 0), ('xo', 1)])
        for c in range(2):
            n = 2 * i + c
            act(mT.rearrange("p k t -> p (k t)")[:, 0:D], xo[:, c, :], AF.Square, [('xo', c)],
                [('mT', j) for j in range(8)] + [('ssq2', c)], accum=ssq[:, 2 + c:3 + c])
            ts('dve', rstd[:, 2 + c:3 + c], ssq[:, 2 + c:3 + c], 1.0 / D, EPS, ALU.mult, ALU.add, [('ssq2', c)], [('rstd2', c)])
            rsq(rstd[:, 2 + c:3 + c], [('rstd2', c)])
            if c == 0:
                act(xo[:, c, :], xo[:, c, :], AF.Copy, [('xo', c), ('rstd2', c)], [('xo', c)], scale=rstd[:, 2 + c:3 + c])
                tt('pool', xo[:, c, :], xo[:, c, :], fg_bc, ALU.mult, [('xo', c), 'fg_bc'], [('xo', c)])
            else:
                stt('dve', xo[:, c, :], xo[:, c, :], rstd[:, 2 + c:3 + c], fg_bc, ALU.mult, ALU.mult,
                    [('xo', c), ('rstd2', c), 'fg_bc'], [('xo', c)])
            dma('sp', y[n * 128:(n + 1) * 128, :], xo[:, c, :], [('xo', c)], [('y', n)], ('yst', c))
            out_keys.append(('y', n))
    return out_keys


_NC_CACHE = {}


def _host_consts():
    c = np.zeros((128, 1184), np.float32)
    l = np.arange(128, dtype=np.float64)[:, None]
    cc = np.arange(128, dtype=np.float64)[None, :]
    c[:, 0:128] = np.eye(128)
    c[:, 128:256] = np.maximum(cc - l, 0)
    c[:, 256:384] = np.maximum(l - cc, 0)
    c[:, 384:512] = (cc >= l)
    c[:, 512:640] = (l > cc)
    c[:, 640:768] = l + 1
    c[:, 768:896] = 128 - l
    c[:, 896:1024] = cc
    c[:, 1024:1152] = 127 - cc
    c[:, 1152:1184] = 128.0 * (31 - np.arange(32))[None, :]
    return c


def _rope_tables(pos0):
    pos = (pos0 + np.arange(SEGT)).astype(np.float64)
    inv = 10000.0 ** (-np.arange(0, 128, 2, dtype=np.float64) / 128)
    ang = pos[:, None] * inv[None, :]
    cos, sin = np.cos(ang), np.sin(ang)
    sc = 128 ** -0.5

    def lay(a, b):
        t = np.concatenate([a, b], axis=1).reshape(NCH, 128, 128)
        return np.ascontiguousarray(t.transpose(1, 0, 2).reshape(128, NCH * 128)).astype(np.float32)

    return lay(cos, sin), lay(cos * sc, sin * sc)


def kernel(x, norm_gain, w_in, decay_logit_fwd, decay_logit_bwd, ret_gn_gain, conv_w, w_branch, w_out, final_gain):
    x = np.asarray(x, np.float32)
    Bn, S, _ = x.shape
    if 'nc' not in _NC_CACHE:
        _NC_CACHE['nc'] = build()
    nc, dbg_outs = _NC_CACHE['nc']
    w_in0 = np.ascontiguousarray(np.asarray(w_in, np.float32)[0])
    w_br0 = np.ascontiguousarray(np.asarray(w_branch, np.float32)[0].reshape(2 * 1024, D))
    w_out0 = np.ascontiguousarray(np.asarray(w_out, np.float32)[0])
    ng = np.asarray(norm_gain, np.float32)[0].reshape(1, D)
    fg = np.asarray(final_gain, np.float32).reshape(1, D)
    gng = np.asarray(ret_gn_gain, np.float32)[0].reshape(1, 1024)
    cw = np.ascontiguousarray(np.asarray(conv_w, np.float32)[0].reshape(3, 8, 128).transpose(2, 1, 0).reshape(128, 24))
    dl = np.concatenate([np.asarray(decay_logit_fwd, np.float32)[0], np.asarray(decay_logit_bwd, np.float32)[0]]).reshape(1, 16)
    consts = _host_consts()
    in_maps = []
    for c in range(NCORE):
        b, s = divmod(c, 4)
        t0 = s * SEGT
        xs = np.ascontiguousarray(x[b, t0:t0 + SEGT])
        xhal = np.zeros((64, D), np.float32)
        for i in range(NT + 1):
            for k, tok in enumerate((t0 + T * i - 1, t0 + T * i)):
                if 0 <= tok < S:
                    xhal[2 * i + k] = x[b, tok]
        rq, rk = _rope_tables(t0)
        seg = np.zeros((1, 16), np.float32)
        for r in range(4):
            if r < s:
                seg[0, r] = 4096.0 * (s - 1 - r)
                seg[0, 8 + r] = 1.0
            if r > s:
                seg[0, 4 + r] = 4096.0 * (r - s - 1)
                seg[0, 12 + r] = 1.0
        in_maps.append({"x": xs, "xh": xhal, "w_in": w_in0, "w_br": w_br0, "w_out": w_out0, "ng": ng, "fg": fg,
                        "gng": gng, "cw": cw, "dl": dl, "ropeq": rq, "ropek": rk, "consts": consts, "seg": seg})
    res = run_bass_kernel_spmd(nc, in_maps, core_ids=list(range(NCORE)))
    out = np.empty((Bn, S, D), np.float32)
    for c in range(NCORE):
        b, s = divmod(c, 4)
        out[b, s * SEGT:(s + 1) * SEGT] = res.results[c]["y"]
    if DEBUG:
        kernel.last_results = res.results
    return out
```

```python
import numpy as np
import concourse.bass as bass
import concourse.mybir as mybir
from concourse.bass_utils import run_bass_kernel_spmd

F32 = mybir.dt.float32
BF16 = mybir.dt.bfloat16
U8 = mybir.dt.uint8
AF = mybir.ActivationFunctionType
ALU = mybir.AluOpType

D = 2048
NH = 8
SEGT = 4096
NCH = 32
T = 256
NT = 16
EPS = 1e-6
NCORE = 8
NPIECE = 56
DEBUG = False
STAGE = 2
CONV_DVE = True
REORDER = True


class Prog:
    def __init__(self):
        self.ops = []
        self.lw = {}
        self.rd = {}
        self.chan_last = {}
        self.last_eng = {}
        self.fence = None
        self.fenced = set()

    def add(self, eng, fn, reads=(), writes=(), chan=None, inc=16):
        i = len(self.ops)
        deps = set()
        for k in reads:
            if k in self.lw:
                deps.add(self.lw[k])
            if isinstance(k, tuple) and k[0] == 'ps':
                me = ('c', chan) if chan is not None else ('e', eng)
                for wh, r in self.rd.get(k, {}).items():
                    if wh != me:
                        deps.add(r)
        for k in writes:
            if k in self.lw:
                deps.add(self.lw[k])
            for r in self.rd.get(k, {}).values():
                deps.add(r)
        if chan is not None and chan in self.chan_last:
            deps.add(self.chan_last[chan])
        if self.fence is not None and eng not in self.fenced:
            deps |= self.fence
            self.fenced.add(eng)
        deps.discard(i)
        who = ('c', chan) if chan is not None else ('e', eng)
        for k in reads:
            self.rd.setdefault(k, {})[who] = i
        for k in writes:
            self.lw[k] = i
            self.rd[k] = {}
        if chan is not None:
            self.chan_last[chan] = i
        else:
            self.last_eng[eng] = i
        self.ops.append((eng, fn, deps, chan, inc))
        return i

    def set_fence(self, skip_chans=()):
        self.fence = set(self.last_eng.values()) | {v for c, v in self.chan_last.items() if c not in skip_chans}
        self.fenced = set()

    def emit(self, nc, block):
        ops = self.ops
        n = len(ops)
        need = [False] * n
        for (eng, fn, deps, chan, inc) in ops:
            for d in deps:
                de = ops[d]
                if de[3] is None and de[0] == 'pe' and eng == 'pe' and chan is None:
                    continue
                need[d] = True
        engs = ['pe', 'act', 'dve', 'pool', 'sp']
        sem_e = {e: nc.semaphore('s_' + e).__enter__() for e in engs}
        chans = sorted({o[3] for o in ops if o[3] is not None}, key=str)
        sem_c = {c: nc.semaphore('c%d' % k).__enter__() for k, c in enumerate(chans)}
        cnt = {e: 0 for e in engs}
        ccnt = {c: 0 for c in chans}
        val = [None] * n
        for i, (eng, fn, deps, chan, inc) in enumerate(ops):
            if chan is not None:
                ccnt[chan] += inc
                val[i] = (('c', chan), ccnt[chan])
            elif need[i]:
                cnt[eng] += 1
                val[i] = (('e', eng), cnt[eng])

        def semof(key):
            return sem_c[key[1]] if key[0] == 'c' else sem_e[key[1]]

        per = {e: [] for e in engs}
        for i, o in enumerate(ops):
            per[o[0]].append(i)
        self.nwaits = 0

        def body(e, eo):
            waited = {}
            for i in per[e]:
                eng, fn, deps, chan, inc = ops[i]
                w = {}
                for d in deps:
                    de = ops[d]
                    if de[3] is None and de[0] == 'pe' and e == 'pe' and chan is None:
                        continue
                    k, v = val[d]
                    if w.get(k, 0) < v:
                        w[k] = v
                for k, v in w.items():
                    if waited.get(k, 0) < v:
                        eo.wait_ge(semof(k), v)
                        waited[k] = v
                        self.nwaits += 1
                ins = fn(eo)
                if val[i] is not None and ins is not None:
                    k, v = val[i]
                    ins.then_inc(semof(k), inc if chan is not None else 1)

        @block.tensor
        def _(eo):
            body('pe', eo)

        @block.scalar
        def _(eo):
            body('act', eo)

        @block.vector
        def _(eo):
            body('dve', eo)

        @block.gpsimd
        def _(eo):
            body('pool', eo)

        @block.sync
        def _(eo):
            body('sp', eo)


class Arena:
    def __init__(self, base_ap, size):
        self.base = base_ap
        self.size = size
        self.off = 0

    def mark(self):
        return self.off

    def reset(self, m):
        self.off = m

    def alloc(self, shape, dt):
        esz = 4 if dt == F32 else 2
        free = int(np.prod(shape[1:]))
        nb = free * esz
        nb_al = (nb + 63) // 64 * 64
        assert self.off + nb_al <= self.size, ("SBUF arena overflow", self.off, nb_al, self.size)
        v = self.base[:, self.off:self.off + nb].bitcast(dt)
        self.off += nb_al
        if len(shape) == 3:
            v = v.rearrange("p (a b) -> p a b", b=shape[2])
        elif len(shape) == 4:
            v = v.rearrange("p (a b c) -> p a b c", b=shape[2], c=shape[3])
        return v


def piece_srcs(idx):
    if idx < 4:
        return [('w_in', idx * 256, 256)]
    if idx < 8:
        return [('w_in', 3072 + (idx - 4) * 256, 256)]
    if idx < 24:
        ctp, t = divmod(idx - 8, 4)
        base = (5120, 6144, 4096, 7168)[t]
        return [('w_in', base + ctp * 256, 256)]
    if idx < 48:
        j, t = divmod(idx - 24, 3)
        if t == 0:
            return [('w_in', 8192 + j * 256, 256)]
        if t == 1:
            return [('w_in', 10240 + j * 256, 256)]
        return [('w_br', j * 256, 256)]
    return [('w_out', (idx - 48) * 256, 256)]


def build():
    nc = bass.Bass("TRN2", target_bir_lowering=False)
    P = Prog()
    dbg_outs = {}

    def dram_in(name, shape, dt=F32):
        return nc.dram_tensor(name, list(shape), dt, kind="ExternalInput").ap()

    x = dram_in("x", [SEGT, D])
    xh = dram_in("xh", [64, D])
    w_in = dram_in("w_in", [D, 12288])
    w_br = dram_in("w_br", [D, D])
    w_out = dram_in("w_out", [D, D])
    ng = dram_in("ng", [1, D])
    fg = dram_in("fg", [1, D])
    gng = dram_in("gng", [1, 1024])
    cw_d = dram_in("cw", [128, 24])
    dl = dram_in("dl", [1, 16])
    ropeq_d = dram_in("ropeq", [128, NCH * 128])
    ropek_d = dram_in("ropek", [128, NCH * 128])
    consts_d = dram_in("consts", [128, 1184])
    seg_d = dram_in("seg", [1, 16])
    y = nc.dram_tensor("y", [SEGT, D], F32, kind="ExternalOutput").ap()
    wsrc = {'w_in': w_in, 'w_br': w_br, 'w_out': w_out}

    skind = "ExternalOutput" if DEBUG else "Internal"
    kvs = nc.dram_tensor("kvs", [NCH * 128, 2048], BF16, kind=skind).ap()
    bst = nc.dram_tensor("bst", [NCH * 128, 1024], F32, kind=skind).ap()
    wsc = nc.dram_tensor("wsc", [NPIECE * 128, 16 * 256], BF16).ap()
    bounce = nc.dram_tensor("bounce", [128, 2048], F32).ap()
    gath = nc.dram_tensor("gath", [4 * 128, 2048], F32).ap()

    ARENA = 207 * 1024
    arena_t = nc.sbuf_tensor("arena", [128, ARENA], U8).__enter__()
    psum_t = nc.psum_tensor("psum", [128, 4096], F32).__enter__()
    A = Arena(arena_t, ARENA)
    ps_all = psum_t

    psptr = {'a': 0, 'b': 0}

    def psalloc(nb, pool):
        lo, n = (0, 4) if pool == 'a' else (4, 4)
        if pool == 'all':
            lo, n = 0, 8
        p = psptr.setdefault(pool, 0)
        if p % nb:
            p += nb - p % nb
        if p + nb > n:
            p = 0
        psptr[pool] = (p + nb) % n
        b0 = lo + p
        return ps_all[:, b0 * 512:(b0 + nb) * 512], [('ps', b0 + t) for t in range(nb)]

    def dma(q, out, in_, reads, writes, chan):
        P.add(q, lambda e: e.dma_start(out=out, in_=in_), reads, writes, chan=chan)

    def mm(out, lhsT, rhs, start, stop, reads, writes):
        P.add('pe', lambda e: e.matmul(out, lhsT, rhs, start=start, stop=stop), reads, writes)

    def tp(out, in_, ident, reads, writes):
        P.add('pe', lambda e: e.transpose(out, in_, ident), reads + ['ident'], writes)

    def act(out, in_, func, reads, writes, scale=None, accum=None):
        kw = {}
        if scale is not None:
            kw['scale'] = scale
        if accum is not None:
            kw['accum_out'] = accum
        P.add('act', lambda e: e.activation(out=out, in_=in_, func=func, **kw), reads, writes)

    def tt(eng, out, in0, in1, op, reads, writes):
        P.add(eng, lambda e: e.tensor_tensor(out=out, in0=in0, in1=in1, op=op), reads, writes)

    def ts(eng, out, in0, s1, s2, op0, op1, reads, writes):
        if s2 is None:
            P.add(eng, lambda e: e.tensor_scalar(out=out, in0=in0, scalar1=s1, scalar2=None, op0=op0), reads, writes)
        else:
            P.add(eng, lambda e: e.tensor_scalar(out=out, in0=in0, scalar1=s1, scalar2=s2, op0=op0, op1=op1), reads, writes)

    def stt(eng, out, in0, scalar, in1, op0, op1, reads, writes):
        P.add(eng, lambda e: e.scalar_tensor_tensor(out=out, in0=in0, scalar=scalar, in1=in1, op0=op0, op1=op1), reads, writes)

    def cp(eng, out, in_, reads, writes):
        if eng == 'act':
            act(out, in_, AF.Copy, reads, writes)
        else:
            P.add(eng, lambda e: e.tensor_copy(out=out, in_=in_), reads, writes)

    def xload(row0, xslot):
        dma('sp', xin[:, xslot, :], x[row0:row0 + 128, :], [], [('xin', xslot)], ('xin', xslot))

    def rsq(ap, keys):
        act(ap, ap, AF.Sqrt, keys, keys)
        P.add('dve', lambda e: e.reciprocal(out=ap, in_=ap), keys, keys)

    def dbg(name, ap, key, shape):
        if not DEBUG:
            return
        t = nc.dram_tensor("dbg_" + name, list(shape), F32, kind="ExternalOutput").ap()
        dbg_outs[name] = t
        dma('sp', t, ap, [key], [('dbg', name)], ('dbg', name))

    identb = A.alloc([128, 128], BF16)
    lg = A.alloc([128, 16], F32)
    DT = A.alloc([128, 1024], BF16)
    QWF = A.alloc([128, 1024], BF16)
    QWB = A.alloc([128, 1024], BF16)
    TF = A.alloc([128, 1024], BF16)
    g128 = A.alloc([128, 16], F32)
    coefB = A.alloc([128, NCH, 8], F32)
    ng_bc = A.alloc([128, D], F32)
    cwt = A.alloc([128, 24], F32)
    Fst = A.alloc([128, 2, 1024], F32)
    B_in = A.alloc([128, 1024], F32)
    ssq = A.alloc([128, 4], F32)
    rstd = A.alloc([128, 4], F32)
    xin = A.alloc([128, 2, D], F32)
    hbuf = A.alloc([128, 2, D], BF16)
    segc = A.alloc([128, 2, 4, 8], F32)
    pers_mark = A.mark()
    cst = A.alloc([128, 1184], F32)
    dlb = A.alloc([128, 16], F32)
    sm = A.alloc([128, 6, 16], F32)
    segb = A.alloc([128, 16], F32)
    TB = A.alloc([128, 1024], BF16)
    coefF = A.alloc([128, NCH, 8], F32)
    tmpA = A.alloc([128, 128], F32)
    tmpB = A.alloc([128, 128], F32)
    rt_box = [A.alloc([128, 4, 512], F32)]

    C_ID, C_EF, C_EB, C_MF, C_MB, C_EL1, C_E128L, C_EC, C_E127C, C_ENB = [k * 128 for k in range(10)]

    dma('sp', cst, consts_d, [], ['cst'], 'cst')
    dma('sp', dlb, dl.partition_broadcast(128), [], ['dlb'], 'dlb')
    dma('sp', segb, seg_d.partition_broadcast(128), [], ['segb'], 'segb')
    dma('sp', ng_bc, ng.partition_broadcast(128), [], ['ng_bc'], 'ng_bc')
    dma('sp', cwt, cw_d, [], ['cwt'], 'cwt')
    cp('dve', identb, cst[:, C_ID:C_ID + 128], ['cst'], ['ident'])
    s_e, s_w, s_l, s_d, s_r = [sm[:, k, :] for k in range(5)]
    act(s_e, dlb, AF.Exp, ['dlb'], ['s_e'], scale=-1.0)
    ts('dve', s_w, s_e, 1.0, None, ALU.add, None, ['s_e'], ['s_w'])
    act(s_l, s_w, AF.Ln, ['s_w'], ['s_l'])
    ts('dve', s_d, s_w, -1.0, 1e-30, ALU.add, ALU.max, ['s_w'], ['s_d'])
    P.add('dve', lambda e: e.reciprocal(out=s_d, in_=s_d), ['s_d'], ['s_d'])
    tt('dve', s_r, s_e, s_d, ALU.mult, ['s_e', 's_d'], ['s_r'])
    stt('dve', lg, s_l, -1.0, s_r, ALU.mult, ALU.mult, ['s_l', 's_r'], ['lg'])
    for h in range(NH):
        lf = lg[:, h:h + 1]
        lb = lg[:, 8 + h:9 + h]
        hs = slice(h * 128, (h + 1) * 128)
        act(tmpA, cst[:, C_EF:C_EF + 128], AF.Exp, ['cst', 'lg'], ['tmpA'], scale=lf)
        act(tmpB, cst[:, C_EB:C_EB + 128], AF.Exp, ['cst', 'lg'], ['tmpB'], scale=lb)
        tt('dve', tmpA, tmpA, cst[:, C_MF:C_MF + 128], ALU.mult, ['tmpA', 'cst'], ['tmpA'])
        tt('dve', tmpB, tmpB, cst[:, C_MB:C_MB + 128], ALU.mult, ['tmpB', 'cst'], ['tmpB'])
        tt('dve', DT[:, hs], tmpA, tmpB, ALU.add, ['tmpA', 'tmpB'], ['DT'])
        act(QWF[:, hs], cst[:, C_EC:C_EC + 128], AF.Exp, ['cst', 'lg'], ['QWF'], scale=lf)
        act(QWB[:, hs], cst[:, C_E127C:C_E127C + 128], AF.Exp, ['cst', 'lg'], ['QWB'], scale=lb)
        act(TB[:, hs], cst[:, C_EL1:C_EL1 + 128], AF.Exp, ['cst', 'lg'], ['TB'], scale=lb)
        act(TF[:, hs], cst[:, C_E128L:C_E128L + 128], AF.Exp, ['cst', 'lg'], ['TF'], scale=lf)
        act(coefF[:, :, h], cst[:, C_ENB:C_ENB + NCH], AF.Exp, ['cst', 'lg'], ['coefF'], scale=lf)
        act(coefB[:, :, h], cst[:, C_ENB:C_ENB + NCH], AF.Exp, ['cst', 'lg'], ['coefB'], scale=lb)
    act(g128, lg, AF.Exp, ['lg'], ['g128'], scale=128.0)
    for r in range(4):
        act(segc[:, 0, r, :], lg[:, 0:8], AF.Exp, ['lg', 'segb'], ['segc'], scale=segb[:, r:r + 1])
        act(segc[:, 1, r, :], lg[:, 8:16], AF.Exp, ['lg', 'segb'], ['segc'], scale=segb[:, 4 + r:5 + r])
        ts('dve', segc[:, 0, r, :], segc[:, 0, r, :], segb[:, 8 + r:9 + r], None, ALU.mult, None, ['segc', 'segb'], ['segc'])
        ts('dve', segc[:, 1, r, :], segc[:, 1, r, :], segb[:, 12 + r:13 + r], None, ALU.mult, None, ['segc', 'segb'], ['segc'])
    dbg('lg', lg, 'lg', [128, 16])
    dbg('segc', segc.rearrange("p a b c -> p (a b c)"), 'segc', [128, 64])

    def prep_elem(row0, nrows, xslot, src=None, hslot=0, noload=False):
        src = x if src is None else src
        sq, rs = ssq[:nrows, hslot:hslot + 1], rstd[:nrows, hslot:hslot + 1]
        ksq, krs = ('ssq', hslot), ('rstd', hslot)
        if not noload:
            dma('sp', xin[:nrows, xslot, :], src[row0:row0 + nrows, :], [], [('xin', xslot)], ('xin', xslot))
        act(hbuf[:nrows, hslot, :], xin[:nrows, xslot, :], AF.Square, [('xin', xslot)], [('hbuf', hslot), ksq], accum=sq)
        ts('dve', rs, sq, 1.0 / D, EPS, ALU.mult, ALU.add, [ksq], [krs])
        rsq(rs, [krs])
        stt('dve', hbuf[:nrows, hslot, :], xin[:nrows, xslot, :], rs, ng_bc[:nrows], ALU.mult, ALU.mult,
            [('xin', xslot), krs, 'ng_bc'], [('hbuf', hslot)])

    def prep_pe(nrows, dst, dkeys, hslot=0, bank=None):
        if bank is None:
            pst, pk = psalloc(2, 'b')
        else:
            pst, pk = ps_all[:, bank * 512:(bank + 2) * 512], [('ps', bank), ('ps', bank + 1)]
        pstb = pst.bitcast(BF16)
        for k in range(16):
            tp(pstb[:, k * nrows:(k + 1) * nrows], hbuf[:nrows, hslot, k * 128:(k + 1) * 128], identb[:nrows, :nrows],
               [('hbuf', hslot)], pk)
        v = pstb[:, 0:16 * nrows].rearrange("p (k t) -> p k t", t=nrows)
        cp('act', dst[:, 0:8, :], v[:, 0:8, :], pk, dkeys)
        cp('dve', dst[:, 8:16, :], v[:, 8:16, :], pk, dkeys)

    def rotary(ps3, nh, cos, sin, dst3, rkeys, wkeys, comb='pool'):
        cb = cos.unsqueeze(1).broadcast_to([128, nh, 64])
        sb = sin.unsqueeze(1).broadcast_to([128, nh, 64])
        x1 = ps3[:, :, 0:64]
        x2 = ps3[:, :, 64:128]
        t = [rt_box[0][:, k, 0:nh * 64].rearrange("p (h e) -> p h e", e=64) for k in range(4)]
        tt('dve', t[0], x1, cb, ALU.mult, rkeys, [('rt', 0)])
        tt('dve', t[1], x2, sb, ALU.mult, rkeys, [('rt', 1)])
        tt('dve', t[2], x2, cb, ALU.mult, rkeys, [('rt', 2)])
        tt('dve', t[3], x1, sb, ALU.mult, rkeys, [('rt', 3)])
        tt(comb, dst3[:, :, 0:64], t[0], t[1], ALU.subtract, [('rt', 0), ('rt', 1)], wkeys)
        tt(comb, dst3[:, :, 64:128], t[2], t[3], ALU.add, [('rt', 2), ('rt', 3)], wkeys)

    wkv = A.alloc([128, 16, 2048], BF16)
    ropek = A.alloc([128, NCH, 128], F32)
    hT1 = A.alloc([128, 2, 16, 128], BF16)
    kvo = A.alloc([128, 2, 2048], BF16)
    Bst = A.alloc([128, 2, 1024], F32)
    Fac = A.alloc([128, 2, 1024], F32)

    w_in_v = w_in.rearrange("(k p) f -> p k f", p=128)
    for q4 in range(4):
        P.add('pool', (lambda q4: lambda e: e.dma_start(out=wkv[:, :, q4 * 512:(q4 + 1) * 512],
                                                        in_=w_in_v[:, :, 1024 + q4 * 512:1024 + (q4 + 1) * 512]))(q4),
              [], [('wkv', q4)], chan=('wkvld', q4))
    dma('sp', ropek, ropek_d.rearrange("p (n c) -> p n c", c=128), [], ['ropek'], 'ropek')
    P.add('dve', lambda e: e.memset(Bst[:, 0, :], 0.0), [], [('Bst', 0)])
    P.add('dve', lambda e: e.memset(Fac[:, 0, :], 0.0), [], [('Fac', 0)])

    halo_list = [8 + 4 * ctp + t for ctp in range(4) for t in range(2)]
    conv_list = halo_list + [i for i in range(NPIECE) if i not in halo_list]
    conv_pos = [0]
    NCV = 32

    def issue_conv(nmax):
        for _ in range(nmax):
            if conv_pos[0] >= len(conv_list):
                return
            idx = conv_list[conv_pos[0]]
            cch = conv_pos[0] % NCV
            conv_pos[0] += 1
            dstp = wsc[idx * 128:(idx + 1) * 128, :].rearrange("p (k f) -> p k f", f=256)
            col = 0
            for (sn, c0, wd) in piece_srcs(idx):
                srcv = wsrc[sn].rearrange("(k p) f -> p k f", p=128)[:, :, c0:c0 + wd]
                P.add('pool', (lambda o, s: lambda e: e.dma_start(out=o, in_=s))(dstp[:, :, col:col + wd], srcv),
                      [], [('wsc', idx)], chan=('cv', cch))
                col += wd

    k32 = A.alloc([128, 1024], F32)
    vBF = A.alloc([128, 2, 2, 1024], BF16)

    def PSB(b, nb):
        return ps_all[:, b * 512:(b + nb) * 512], [('ps', b + t) for t in range(nb)]

    def p1_prep_e(n):
        s_ = n % 2
        prep_elem(n * 128, 128, s_, hslot=s_, noload=True)
        if n - 2 >= 0:
            xload((n - 2) * 128, s_)

    def p1_prep_p(n):
        s_ = n % 2
        prep_pe(128, hT1[:, s_], [('hT1', s_)], hslot=s_, bank=4)

    def p1_proj(n):
        hs_ = n % 2
        o = n % 2
        psk, kk = PSB(0, 2)
        psv, kvk = PSB(2, 2)
        for (psx, kx, cbase, q0) in ((psk, kk, 0, 0), (psv, kvk, 1024, 2)):
            for ft in range(2):
                for k in range(16):
                    mm(psx[:, ft * 512:(ft + 1) * 512], hT1[:, hs_, k, :], wkv[:, k, cbase + ft * 512:cbase + (ft + 1) * 512],
                       k == 0, k == 15, [('hT1', hs_), ('wkv', q0 + ft)], [kx[ft]])
            if cbase == 0:
                cp('act', k32, psk, kk, ['k32'])
            else:
                cp('act', kvo[:, o, 1024:2048], psv, kvk, [('kvo', o, 'v')])

    def p1_post_elem(n):
        o = n % 2
        rotary(k32.rearrange("p (h e) -> p h e", e=128), 8, ropek[:, n, 0:64], ropek[:, n, 64:128],
               kvo[:, o, 0:1024].rearrange("p (h e) -> p h e", e=128), ['k32', 'ropek'], [('kvo', o, 'k')], comb='dve')
        dma('sp', kvs[n * 128:(n + 1) * 128, :], kvo[:, o, :], [('kvo', o, 'k'), ('kvo', o, 'v')], [('kvs', n)], ('kvst', o))
        tt('dve', vBF[:, o, 0, :], kvo[:, o, 1024:2048], TB, ALU.mult, [('kvo', o, 'v'), 'TB'], [('vB', o)])
        tt('dve', vBF[:, o, 1, :], kvo[:, o, 1024:2048], TF, ALU.mult, [('kvo', o, 'v'), 'TF'], [('vF', o)])

    def p1_post_pe(n):
        o = n % 2
        cur = (NCH - 1 - n) % 2
        new = 1 - cur
        dma('sp', bst[n * 128:(n + 1) * 128, :], Bst[:, cur, :], [('Bst', cur)], [('bst', n)], ('bstst', cur))
        pkb, kb = PSB(6, 2)
        pkf, kf = PSB(4, 2)
        for h in range(NH):
            hs = slice(h * 128, (h + 1) * 128)
            mm(pkb[:, hs], kvo[:, o, hs], vBF[:, o, 0, hs], True, True, [('kvo', o, 'k'), ('vB', o)], [kb[h // 4]])
        for h in range(NH):
            hs = slice(h * 128, (h + 1) * 128)
            mm(pkf[:, hs], kvo[:, o, hs], vBF[:, o, 1, hs], True, True, [('kvo', o, 'k'), ('vF', o)], [kf[h // 4]])
        for h in range(NH):
            hs = slice(h * 128, (h + 1) * 128)
            stt('dve', Bst[:, new, hs], Bst[:, cur, hs], g128[:, 8 + h:9 + h], pkb[:, hs], ALU.mult, ALU.add,
                [('Bst', cur), 'g128', kb[h // 4]], [('Bst', new)])
        for h in range(NH):
            hs = slice(h * 128, (h + 1) * 128)
            stt('dve', Fac[:, new, hs], pkf[:, hs], coefF[:, n, h:h + 1], Fac[:, cur, hs], ALU.mult, ALU.add,
                [('Fac', cur), 'coefF', kf[h // 4]], [('Fac', new)])

    xload((NCH - 1) * 128, (NCH - 1) % 2)
    xload((NCH - 2) * 128, (NCH - 2) % 2)
    p1_prep_e(NCH - 1)
    p1_prep_p(NCH - 1)
    issue_conv(100)
    for n in range(NCH - 1, -1, -1):
        if n > 0:
            p1_prep_e(n - 1)
        p1_proj(n)
        if n > 0:
            p1_prep_p(n - 1)
        if n < NCH - 1:
            p1_post_pe(n + 1)
        p1_post_elem(n)
    p1_post_pe(0)
    dma('sp', xin[:64, 1, :], xh[0:64, :], [], [('xin', 1)], ('xin', 1))
    xload(0, 0)
    issue_conv(100)
    fin = NCH % 2
    dma('sp', bounce[:, 0:1024], Fac[:, fin, :], [('Fac', fin)], ['bounce'], 'bnc0')
    dma('sp', bounce[:, 1024:2048], Bst[:, fin, :], [('Bst', fin)], ['bounce2'], 'bnc1')
    P.add('pool', lambda e: e.collective_compute("AllGather", ALU.bypass, replica_groups=[[0, 1, 2, 3], [4, 5, 6, 7]],
                                                 ins=[bounce], outs=[gath]),
          ['bounce', 'bounce2'], ['gath'], chan='cc', inc=1)
    def finish_exchange(Gbuf, gkeys):
        gv = gath.rearrange("(r p) f -> p r f", p=128)
        dma('sp', Gbuf, gv[:, :, 0:1024], ['gath'], gkeys, 'gld')
        for h in range(NH):
            hs = slice(h * 128, (h + 1) * 128)
            ts('dve', Fst[:, 0, hs], Gbuf[:, 0, hs], segc[:, 0, 0, h:h + 1], None, ALU.mult, None, gkeys + ['segc'], [('Fst', 0)])
            for r in range(1, 4):
                stt('dve', Fst[:, 0, hs], Gbuf[:, r, hs], segc[:, 0, r, h:h + 1], Fst[:, 0, hs], ALU.mult, ALU.add,
                    gkeys + ['segc', ('Fst', 0)], [('Fst', 0)])
        dma('sp', Gbuf, gv[:, :, 1024:2048], ['gath'], gkeys, 'gld')
        for h in range(NH):
            hs = slice(h * 128, (h + 1) * 128)
            ts('dve', B_in[:, hs], Gbuf[:, 0, hs], segc[:, 1, 0, h:h + 1], None, ALU.mult, None, gkeys + ['segc'], ['B_in'])
            for r in range(1, 4):
                stt('dve', B_in[:, hs], Gbuf[:, r, hs], segc[:, 1, r, h:h + 1], B_in[:, hs], ALU.mult, ALU.add,
                    gkeys + ['segc', 'B_in'], ['B_in'])
        dbg('F_in', Fst[:, 0, :], ('Fst', 0), [128, 1024])
        dbg('B_in', B_in, 'B_in', [128, 1024])

    if STAGE < 2:
        Gb = wkv.rearrange("p k f -> p (k f)")[:, 0:8192].bitcast(F32).rearrange("p (r f) -> p r f", f=1024)
        finish_exchange(Gb, [('wkv', q) for q in range(4)])
    dbg('Fsum', Fac[:, fin, :], ('Fac', fin), [128, 1024])

    out_keys = []
    if STAGE >= 2:
        P.set_fence(skip_chans=('cc',) + tuple(('cv', k) for k in range(32)))
        A.reset(pers_mark)
        out_keys = phase2(nc, P, A, locals())

    P.add('sp', lambda e: None, reads=out_keys + [('dbg', k) for k in dbg_outs] + ['B_in', ('Fst', 0)] +
          ([('kvs', n) for n in range(NCH)] + [('bst', n) for n in range(NCH)] if STAGE < 2 else []), writes=[])
    if STAGE < 2:
        dma('sp', y[0:128, :], x[0:128, :], [], [('y', 0)], 'ycopy')
        P.add('sp', lambda e: None, reads=[('y', 0)], writes=[])

    with nc.allow_low_precision("bf16 matmul operands, fp32 accumulation"):
        with nc.Block() as block:
            P.emit(nc, block)
    return nc, dbg_outs


def phase2(nc, P, A, env):
    g = env
    x, xh, y, kvs, bst, wsc = g['x'], g['xh'], g['y'], g['kvs'], g['bst'], g['wsc']
    dma, mm, tp, act, tt, ts, stt, cp = g['dma'], g['mm'], g['tp'], g['act'], g['tt'], g['ts'], g['stt'], g['cp']
    psalloc, prep_elem, prep_pe, rotary, rsq = g['psalloc'], g['prep_elem'], g['prep_pe'], g['rotary'], g['rsq']
    identb, xin, hbuf, ssq, rstd = g['identb'], g['xin'], g['hbuf'], g['ssq'], g['rstd']
    DT, QWF, QWB, TF, g128, coefB, Fst, B_in, cwt = (g['DT'], g['QWF'], g['QWB'], g['TF'], g['g128'], g['coefB'],
                                                     g['Fst'], g['B_in'], g['cwt'])
    fg, gng, ropeq_d, dbg = g['fg'], g['gng'], g['ropeq_d'], g['dbg']

    g['rt_box'][0] = A.alloc([128, 4, 128], F32)
    fg_bc = A.alloc([128, D], F32)
    gn_bc = A.alloc([128, 1024], F32)
    NW = 4
    Wr = A.alloc([128, NW, 16, 256], BF16)
    ropeq = A.alloc([128, 2, 2, 128], F32)
    hT = A.alloc([128, 2, 16, T], BF16)
    uh = A.alloc([128, 8, 64], F32)
    q_rot = A.alloc([128, 2, 1024], BF16)
    sgn = A.alloc([128, 2, 1024], BF16)
    ccs = A.alloc([128, 2, T], F32)
    ubuf = A.alloc([128, 2, T + 2], F32)
    tacc = A.alloc([128, 2, T], F32)
    sgc = A.alloc([128, 2, T], F32)
    bconvT = A.alloc([128, 8, T], BF16)
    kvb = A.alloc([128, 2, 2048], BF16)
    Bl = A.alloc([128, 1024], F32)
    qT = A.alloc([128, 1024], BF16)
    kT = A.alloc([128, 1024], BF16)
    qTf = A.alloc([128, 1024], BF16)
    qTb = A.alloc([128, 1024], BF16)
    Bfull = A.alloc([128, 1024], BF16)
    Fbf = A.alloc([128, 1024], BF16)
    vF = A.alloc([128, 1024], BF16)
    STm = A.alloc([128, 1024], BF16)
    ssqo = A.alloc([128, 8], F32)
    rso = A.alloc([128, 8], F32)
    bret = A.alloc([128, 1024], BF16)
    hTh = bret.rearrange("p (k t) -> p k t", t=64)
    bretT = A.alloc([128, 8, T], BF16)
    gates = A.alloc([128, 2, 2, 256], BF16)
    m12 = A.alloc([128, 2, 2, 256], F32)
    merged = A.alloc([128, 2, 2, 256], BF16)
    mT = A.alloc([128, 16, T], BF16)
    xo = A.alloc([128, 2, D], F32)

    dma('sp', fg_bc, fg.partition_broadcast(128), [], ['fg_bc'], 'fg_bc')
    dma('sp', gn_bc, gng.partition_broadcast(128), [], ['gn_bc'], 'gn_bc')

    if REORDER:
        wseq = [8 + 4 * ctp + t for ctp in range(4) for t in range(2)] + list(range(8))
        for i in range(NT):
            wseq += list(range(8, 48)) + (list(range(8)) if i + 1 < NT else []) + list(range(48, 56))
    else:
        wseq = [8 + 4 * ctp + t for ctp in range(4) for t in range(2)]
        for i in range(NT):
            wseq += list(range(NPIECE))
    wstate = {'issued': 0, 'next': 0}

    def wget(expect_idx):
        gq = wstate['next']
        assert wseq[gq] == expect_idx, (gq, wseq[gq], expect_idx)
        while wstate['issued'] < min(gq + NW, len(wseq)):
            q = wstate['issued']
            s = q % NW
            idx = wseq[q]
            dma('sp', Wr[:, s], wsc[idx * 128:(idx + 1) * 128, :].rearrange("p (k f) -> p k f", f=256),
                [('wsc', idx)], [('W', s)], ('W', s))
            wstate['issued'] += 1
        wstate['next'] += 1
        s = gq % NW
        return Wr[:, s], ('W', s)

    prep_elem(0, 64, 1, src=xh, noload=True)
    prep_pe(64, hTh, ['bret'])
    g['xload'](128, 1)
    ccs_f = ccs.rearrange("p g t -> p (g t)")
    for ctp in range(4):
        pss = []
        for t in range(2):
            W, wk = wget(8 + 4 * ctp + t)
            psh, hk = psalloc(1, 'a')
            for gq in range(2):
                for k in range(16):
                    mm(psh[:, gq * 64:(gq + 1) * 64], W[:, k, gq * 128:(gq + 1) * 128], hTh[:, k, :], k == 0, k == 15,
                       [wk, 'bret'], hk)
            pss.append((psh, hk))
            if t == 0:
                cp('act', ccs_f[:, 0:128], psh[:, 0:128], hk, ['ccs'])
        psh, hk = pss[1]
        tt('dve', uh[:, 2 * ctp:2 * ctp + 2, :], psh[:, 0:128].rearrange("p (g t) -> p g t", t=64),
           ccs_f[:, 0:128].rearrange("p (g t) -> p g t", t=64), ALU.mult, hk + ['ccs'], ['uh'])
    dbg('uh', uh.rearrange("p a b -> p (a b)"), 'uh', [128, 512])

    xs = {'n': 0}

    def p2_prep_elem(i):
        slots = []
        for c in range(2):
            s = xs['n'] % 2
            xs['n'] += 1
            slots.append(s)
        return slots

    def p2_prep_e(i):
        hs_ = i % 2
        dma('sp', ropeq[:, hs_], ropeq_d.rearrange("p (n c) -> p n c", c=128)[:, 2 * i:2 * i + 2, :], [],
            [('ropeq', hs_)], ('ropeq', hs_))
        for c in range(2):
            prep_elem((2 * i + c) * 128, 128, c, hslot=c, noload=True)
            if i + 1 < NT:
                g['xload']((2 * (i + 1) + c) * 128, c)

    def p2_prep_p(i):
        hs_ = i % 2
        for c in range(2):
            pst, pk = psalloc(2, 'b')
            pstb = pst.bitcast(BF16)
            for k in range(16):
                tp(pstb[:, k * 128:(k + 1) * 128], hbuf[:, c, k * 128:(k + 1) * 128], identb, [('hbuf', c)], pk)
            v = pstb.rearrange("p (k t) -> p k t", t=128)
            cp('act', hT[:, hs_, 0:8, c * 128:(c + 1) * 128], v[:, 0:8, :], pk, [('hT', hs_, c)])
            cp('dve', hT[:, hs_, 8:16, c * 128:(c + 1) * 128], v[:, 8:16, :], pk, [('hT', hs_, c)])

    def tok_piece(i, idx):
        hs_ = i % 2
        W, wk = wget(idx)
        ps, pk = psalloc(1, 'a')
        for c in range(2):
            for k in range(16):
                mm(ps[:, c * 256:(c + 1) * 256], hT[:, hs_, k, c * 128:(c + 1) * 128], W[:, k, :], k == 0, k == 15,
                   [('hT', hs_, c), wk], pk)
        return ps.rearrange("p (c f) -> p c f", f=256), pk

    def feat_piece(i, idx):
        hs_ = i % 2
        W, wk = wget(idx)
        ps, pk = psalloc(1, 'a')
        for gq in range(2):
            for k in range(16):
                mm(ps[:, gq * 256:(gq + 1) * 256], W[:, k, gq * 128:(gq + 1) * 128], hT[:, hs_, k, :], k == 0, k == 15,
                   [('hT', hs_, 0), ('hT', hs_, 1), wk], pk)
        return ps.rearrange("p (g t) -> p g t", t=256), pk

    def conv_A(i, ctp):
        ps, pk = feat_piece(i, 8 + 4 * ctp)
        cp('act', ccs, ps, pk, ['ccs'])
        ps, pk = feat_piece(i, 9 + 4 * ctp)
        tt('dve', ubuf[:, :, 1:T + 1], ps, ccs, ALU.mult, pk + ['ccs'], ['ubuf'])
        for gq in range(2):
            ct = 2 * ctp + gq
            ta = tacc[:, gq, :]
            cp('pool', ubuf[:, gq, 0:1], uh[:, ct, 2 * i:2 * i + 1], ['uh'], ['ubuf'])
            cp('pool', ubuf[:, gq, T + 1:T + 2], uh[:, ct, 2 * i + 3:2 * i + 4], ['uh'], ['ubuf'])
            ts('dve', ta, ubuf[:, gq, 0:T], cwt[:, ct * 3:ct * 3 + 1], None, ALU.mult, None, ['ubuf', 'cwt'], ['tacc'])
            stt('dve', ta, ubuf[:, gq, 1:T + 1], cwt[:, ct * 3 + 1:ct * 3 + 2], ta, ALU.mult, ALU.add, ['ubuf', 'cwt', 'tacc'], ['tacc'])
            stt('dve', ta, ubuf[:, gq, 2:T + 2], cwt[:, ct * 3 + 2:ct * 3 + 3], ta, ALU.mult, ALU.add, ['ubuf', 'cwt', 'tacc'], ['tacc'])

    def conv_B(i, ctp):
        ps, pk = feat_piece(i, 10 + 4 * ctp)
        tt('dve', tacc, ps, tacc, ALU.mult, pk + ['tacc'], ['tacc'])
        ps, pk = feat_piece(i, 11 + 4 * ctp)
        act(sgc, ps, AF.Silu, pk, ['sgc'])
        tt('pool', bconvT[:, 2 * ctp:2 * ctp + 2, :], tacc, sgc, ALU.mult, ['tacc', 'sgc'],
           [('bconvT', 2 * ctp), ('bconvT', 2 * ctp + 1)])

    def conv_pair(i, s8):
        if s8 % 2 == 0:
            conv_A(i, s8 // 2)
        else:
            conv_B(i, s8 // 2)

    def ret_stage(i, c, st, S):
        n = 2 * i + c
        cur = n % 2
        new = 1 - cur
        H3 = lambda ap: ap.rearrange("p (h e) -> p h e", e=128)
        if st == 0:
            psq, qk = psalloc(1, 'b')
            psk, kk = psalloc(1, 'b')
            psqb, pskb = psq.bitcast(BF16), psk.bitcast(BF16)
            for h in range(NH):
                hs = slice(h * 128, (h + 1) * 128)
                tp(psqb[:, hs], q_rot[:, c, hs], identb, [('q_rot', c)], qk)
            for h in range(NH):
                hs = slice(h * 128, (h + 1) * 128)
                tp(pskb[:, hs], kvb[:, c, hs], identb, [('kvb', c)], kk)
            cp('act', qT, psqb, qk, ['qT'])
            cp('act', kT, pskb, kk, ['kT'])
            tt('dve', qTf, psqb, QWF, ALU.mult, qk + ['QWF'], ['qTf'])
            tt('dve', qTb, psqb, QWB, ALU.mult, qk + ['QWB'], ['qTb'])
            if n == 0:
                dma('sp', Bl, bst[0:128, :], [('bst', 0)], ['Bl'], 'Bl')
            for h in range(NH):
                hs = slice(h * 128, (h + 1) * 128)
                stt('dve', Bfull[:, hs], B_in[:, hs], coefB[:, n, h:h + 1], Bl[:, hs], ALU.mult, ALU.add,
                    ['B_in', 'coefB', 'Bl'], ['Bfull'])
            if n + 1 < NCH:
                dma('sp', Bl, bst[(n + 1) * 128:(n + 2) * 128, :], [('bst', n + 1)], ['Bl'], 'Bl')
            cp('act', Fbf, Fst[:, cur, :], [('Fst', cur)], ['Fbf'])
            tt('pool', vF, kvb[:, c, 1024:2048], TF, ALU.mult, [('kvb', c), 'TF'], ['vF'])
        elif st == 1:
            psS, sk = psalloc(2, 'b')
            for h in range(NH):
                hs = slice(h * 128, (h + 1) * 128)
                mm(psS[:, hs], kT[:, hs], qT[:, hs], True, True, ['kT', 'qT'], [sk[h // 4]])
            tt('dve', STm, psS, DT, ALU.mult, sk + ['DT'], ['STm'])
        elif st == 2:
            psO, ok = psalloc(2, 'b')
            S['psO'], S['ok'] = psO, ok
            for h in range(NH):
                hs = slice(h * 128, (h + 1) * 128)
                vs = slice(1024 + h * 128, 1024 + (h + 1) * 128)
                mm(psO[:, hs], STm[:, hs], kvb[:, c, vs], True, False, ['STm', ('kvb', c)], [ok[h // 4]])
                mm(psO[:, hs], qTf[:, hs], Fbf[:, hs], False, False, ['qTf', 'Fbf'], [ok[h // 4]])
                mm(psO[:, hs], qTb[:, hs], Bfull[:, hs], False, True, ['qTb', 'Bfull'], [ok[h // 4]])
            psK, kk = psalloc(2, 'b')
            for h in range(NH):
                hs = slice(h * 128, (h + 1) * 128)
                mm(psK[:, hs], kvb[:, c, hs], vF[:, hs], True, True, [('kvb', c), 'vF'], [kk[h // 4]])
            for h in range(NH):
                hs = slice(h * 128, (h + 1) * 128)
                act(bret[:, hs], psO[:, hs], AF.Square, [ok[h // 4]], ['bret', 'ssqo'], accum=ssqo[:, h:h + 1])
            ts('dve', rso, ssqo, 1.0 / 128, EPS, ALU.mult, ALU.add, ['ssqo'], ['rso'])
            rsq(rso, ['rso'])
            for h in range(NH):
                hs = slice(h * 128, (h + 1) * 128)
                stt('dve', bret[:, hs], psO[:, hs], rso[:, h:h + 1], sgn[:, c, hs], ALU.mult, ALU.mult,
                    [ok[h // 4], 'rso', ('sgn', c)], ['bret'])
            for h in range(NH):
                hs = slice(h * 128, (h + 1) * 128)
                stt('dve', Fst[:, new, hs], Fst[:, cur, hs], g128[:, h:h + 1], psK[:, hs], ALU.mult, ALU.add,
                    [('Fst', cur), 'g128', kk[h // 4]], [('Fst', new)])
        else:
            psb, bk = psalloc(1, 'b')
            psbb = psb.bitcast(BF16)
            for h in range(NH):
                hs = slice(h * 128, (h + 1) * 128)
                tp(psbb[:, hs], bret[:, hs], identb, ['bret'], bk)
            cp('act', bretT[:, :, c * 128:(c + 1) * 128], H3(psbb), bk, [('bretT', c)])

    def merged_T(i, j):
        sl = j % 2
        pst, pk = psalloc(1, 'b')
        pstb = pst.bitcast(BF16)
        for c in range(2):
            for kk in range(2):
                tp(pstb[:, kk * 256 + c * 128: kk * 256 + (c + 1) * 128], merged[:, sl, c, kk * 128:(kk + 1) * 128], identb,
                   [('merged', sl)], pk)
        eng = 'act' if j % 2 == 0 else 'dve'
        cp(eng, mT[:, 2 * j:2 * j + 2, :], pstb[:, 0:512].rearrange("p (k t) -> p k t", t=256), pk, [('mT', j)])

    out_keys = []

    def kv_loads(i):
        for c in range(2):
            n = 2 * i + c
            dma('sp', kvb[:, c, :], kvs[n * 128:(n + 1) * 128, :], [('kvs', n)], [('kvb', c)], ('kvb', c))

    def qg_pieces(i):
        hs_ = i % 2
        for p in range(4):
            ps, pk = tok_piece(i, p)
            for c in range(2):
                rotary(ps[:, c, :].rearrange("p (h e) -> p h e", e=128), 2, ropeq[:, hs_, c, 0:64], ropeq[:, hs_, c, 64:128],
                       q_rot[:, c, p * 256:(p + 1) * 256].rearrange("p (h e) -> p h e", e=128),
                       pk + [('ropeq', hs_)], [('q_rot', c)])
        for p in range(4):
            ps, pk = tok_piece(i, 4 + p)
            act(sgn[:, :, p * 256:(p + 1) * 256], ps, AF.Silu, pk, [('sgn', 0), ('sgn', 1)])
            tt('pool', sgn[:, :, p * 256:(p + 1) * 256], sgn[:, :, p * 256:(p + 1) * 256],
               gn_bc[:, p * 256:(p + 1) * 256].unsqueeze(1).broadcast_to([128, 2, 256]), ALU.mult,
               [('sgn', 0), ('sgn', 1), 'gn_bc'], [('sgn', 0), ('sgn', 1)])

    def final_norm(i):
        for c in range(2):
            n = 2 * i + c
            act(mT.rearrange("p k t -> p (k t)")[:, 0:D], xo[:, c, :], AF.Square, [('xo', c)],
                [('mT', j) for j in range(8)] + [('ssq2', c)], accum=ssq[:, 2 + c:3 + c])
            ts('dve', rstd[:, 2 + c:3 + c], ssq[:, 2 + c:3 + c], 1.0 / D, EPS, ALU.mult, ALU.add, [('ssq2', c)], [('rstd2', c)])
            rsq(rstd[:, 2 + c:3 + c], [('rstd2', c)])
            if c == 0:
                act(xo[:, c, :], xo[:, c, :], AF.Copy, [('xo', c), ('rstd2', c)], [('xo', c)], scale=rstd[:, 2 + c:3 + c])
                tt('pool', xo[:, c, :], xo[:, c, :], fg_bc, ALU.mult, [('xo', c), 'fg_bc'], [('xo', c)])
            else:
                stt('dve', xo[:, c, :], xo[:, c, :], rstd[:, 2 + c:3 + c], fg_bc, ALU.mult, ALU.mult,
                    [('xo', c), ('rstd2', c), 'fg_bc'], [('xo', c)])
            dma('sp', y[n * 128:(n + 1) * 128, :], xo[:, c, :], [('xo', c)], [('y', n)], ('yst', c))
            out_keys.append(('y', n))

    p2_prep_e(0)
    p2_prep_p(0)
    kv_loads(0)
    if REORDER:
        qg_pieces(0)
    for i in range(NT):
        hs_ = i % 2
        if not REORDER:
            qg_pieces(i)
        S = {}
        if i == 0:
            for s8 in range(8):
                conv_pair(i, s8)
            p2_prep_e(i + 1)
            g['finish_exchange'](xo.rearrange("p c f -> p (c f)").rearrange("p (r f) -> p r f", f=1024), [('xo', 0), ('xo', 1)])
            for s8 in range(8):
                ret_stage(i, s8 // 4, s8 % 4, S)
        else:
            for s8 in range(8):
                ret_stage(i, s8 // 4, s8 % 4, S)
                conv_pair(i, s8)
                if s8 == 0 and i + 1 < NT:
                    p2_prep_e(i + 1)
                if s8 == 1:
                    final_norm(i - 1)
        if i + 1 < NT:
            p2_prep_p(i + 1)
            kv_loads(i + 1)
        dma('sp', xo, x[i * T:(i + 1) * T, :].rearrange("(c p) f -> p c f", p=128), [], [('xo', 0), ('xo', 1)], 'xo')
        for j in range(8):
            for br in range(2):
                ps, pk = feat_piece(i, 24 + 3 * j + br)
                act(gates[:, br, :, :], ps, AF.Sigmoid, pk, [('gates', br)])
            W, wk = wget(24 + 3 * j + 2)
            psu, uk = psalloc(2, 'b')
            psu4 = psu.rearrange("p (hf b t) -> p hf b t", b=2, t=256)
            for hf in range(2):
                for br in range(2):
                    src = bretT if br == 0 else bconvT
                    for kf in range(8):
                        rk = [('bretT', 0), ('bretT', 1)] if br == 0 else [('bconvT', kf)]
                        mm(psu4[:, hf, br, :], W[:, br * 8 + kf, hf * 128:(hf + 1) * 128], src[:, kf, :], kf == 0, kf == 7,
                           rk + [wk], [uk[hf]])
            tt('dve', m12[:, :, 0, :], psu4[:, :, 0, :], gates[:, 0, :, :], ALU.mult, uk + [('gates', 0)], [('m12', 0)])
            tt('dve', m12[:, :, 1, :], psu4[:, :, 1, :], gates[:, 1, :, :], ALU.mult, uk + [('gates', 1)], [('m12', 1)])
            tt('pool', mT[:, 2 * j:2 * j + 2, :], m12[:, :, 0, :], m12[:, :, 1, :], ALU.add, [('m12', 0), ('m12', 1)], [('mT', j)])
        if REORDER and i + 1 < NT:
            qg_pieces(i + 1)
        for j in range(8):
            W, wk = wget(48 + j)
            ps, pk = psalloc(1, 'a')
            for c in range(2):
                for k in range(16):
                    mm(ps[:, c * 256:(c + 1) * 256], mT[:, k, c * 128:(c + 1) * 128], W[:, k, :], k == 0, k == 15,
                       [('mT', k // 2), wk], pk)
            xv = xo[:, :, j * 256:(j + 1) * 256]
            tt('dve', xv, ps.rearrange("p (c f) -> p c f", f=256), xv, ALU.add, pk + [('xo', 0), ('xo', 1)], [('xo', 0), ('xo', 1)])
        if i == NT - 1:
            final_norm(i)
    return out_keys


_NC_CACHE = {}


def _host_consts():
    c = np.zeros((128, 1184), np.float32)
    l = np.arange(128, dtype=np.float64)[:, None]
    cc = np.arange(128, dtype=np.float64)[None, :]
    c[:, 0:128] = np.eye(128)
    c[:, 128:256] = np.maximum(cc - l, 0)
    c[:, 256:384] = np.maximum(l - cc, 0)
    c[:, 384:512] = (cc >= l)
    c[:, 512:640] = (l > cc)
    c[:, 640:768] = l + 1
    c[:, 768:896] = 128 - l
    c[:, 896:1024] = cc
    c[:, 1024:1152] = 127 - cc
    c[:, 1152:1184] = 128.0 * (31 - np.arange(32))[None, :]
    return c


def _rope_tables(pos0):
    pos = (pos0 + np.arange(SEGT)).astype(np.float64)
    inv = 10000.0 ** (-np.arange(0, 128, 2, dtype=np.float64) / 128)
    ang = pos[:, None] * inv[None, :]
    cos, sin = np.cos(ang), np.sin(ang)
    sc = 128 ** -0.5

    def lay(a, b):
        t = np.concatenate([a, b], axis=1).reshape(NCH, 128, 128)
        return np.ascontiguousarray(t.transpose(1, 0, 2).reshape(128, NCH * 128)).astype(np.float32)

    return lay(cos, sin), lay(cos * sc, sin * sc)


def kernel(x, norm_gain, w_in, decay_logit_fwd, decay_logit_bwd, ret_gn_gain, conv_w, w_branch, w_out, final_gain):
    x = np.asarray(x, np.float32)
    Bn, S, _ = x.shape
    if 'nc' not in _NC_CACHE:
        _NC_CACHE['nc'] = build()
    nc, dbg_outs = _NC_CACHE['nc']
    w_in0 = np.ascontiguousarray(np.asarray(w_in, np.float32)[0])
    w_br0 = np.ascontiguousarray(np.asarray(w_branch, np.float32)[0].reshape(2 * 1024, D))
    w_out0 = np.ascontiguousarray(np.asarray(w_out, np.float32)[0])
    ng = np.asarray(norm_gain, np.float32)[0].reshape(1, D)
    fg = np.asarray(final_gain, np.float32).reshape(1, D)
    gng = np.asarray(ret_gn_gain, np.float32)[0].reshape(1, 1024)
    cw = np.ascontiguousarray(np.asarray(conv_w, np.float32)[0].reshape(3, 8, 128).transpose(2, 1, 0).reshape(128, 24))
    dl = np.concatenate([np.asarray(decay_logit_fwd, np.float32)[0], np.asarray(decay_logit_bwd, np.float32)[0]]).reshape(1, 16)
    consts = _host_consts()
    in_maps = []
    for c in range(NCORE):
        b, s = divmod(c, 4)
        t0 = s * SEGT
        xs = np.ascontiguousarray(x[b, t0:t0 + SEGT])
        xhal = np.zeros((64, D), np.float32)
        for i in range(NT + 1):
            for k, tok in enumerate((t0 + T * i - 1, t0 + T * i)):
                if 0 <= tok < S:
                    xhal[2 * i + k] = x[b, tok]
        rq, rk = _rope_tables(t0)
        seg = np.zeros((1, 16), np.float32)
        for r in range(4):
            if r < s:
                seg[0, r] = 4096.0 * (s - 1 - r)
                seg[0, 8 + r] = 1.0
            if r > s:
                seg[0, 4 + r] = 4096.0 * (r - s - 1)
                seg[0, 12 + r] = 1.0
        in_maps.append({"x": xs, "xh": xhal, "w_in": w_in0, "w_br": w_br0, "w_out": w_out0, "ng": ng, "fg": fg,
                        "gng": gng, "cw": cw, "dl": dl, "ropeq": rq, "ropek": rk, "consts": consts, "seg": seg})
    res = run_bass_kernel_spmd(nc, in_maps, core_ids=list(range(NCORE)))
    out = np.empty((Bn, S, D), np.float32)
    for c in range(NCORE):
        b, s = divmod(c, 4)
        out[b, s * SEGT:(s + 1) * SEGT] = res.results[c]["y"]
    if DEBUG:
        kernel.last_results = res.results
    return out
```

```python
import numpy as np
import concourse.bass as bass
import concourse.mybir as mybir
from concourse.bass_utils import run_bass_kernel_spmd

F32 = mybir.dt.float32
BF16 = mybir.dt.bfloat16
U8 = mybir.dt.uint8
AF = mybir.ActivationFunctionType
ALU = mybir.AluOpType

D = 2048
NH = 8
SEGT = 4096
NCH = 32
T = 256
NT = 16
EPS = 1e-6
NCORE = 8
NPIECE = 56
DEBUG = False
STAGE = 2
CONV_DVE = True
REORDER = True


class Prog:
    def __init__(self):
        self.ops = []
        self.lw = {}
        self.rd = {}
        self.chan_last = {}
        self.last_eng = {}
        self.fence = None
        self.fenced = set()

    def add(self, eng, fn, reads=(), writes=(), chan=None, inc=16):
        i = len(self.ops)
        deps = set()
        for k in reads:
            if k in self.lw:
                deps.add(self.lw[k])
            if isinstance(k, tuple) and k[0] == 'ps':
                me = ('c', chan) if chan is not None else ('e', eng)
                for wh, r in self.rd.get(k, {}).items():
                    if wh != me:
                        deps.add(r)
        for k in writes:
            if k in self.lw:
                deps.add(self.lw[k])
            for r in self.rd.get(k, {}).values():
                deps.add(r)
        if chan is not None and chan in self.chan_last:
            deps.add(self.chan_last[chan])
        if self.fence is not None and eng not in self.fenced:
            deps |= self.fence
            self.fenced.add(eng)
        deps.discard(i)
        who = ('c', chan) if chan is not None else ('e', eng)
        for k in reads:
            self.rd.setdefault(k, {})[who] = i
        for k in writes:
            self.lw[k] = i
            self.rd[k] = {}
        if chan is not None:
            self.chan_last[chan] = i
        else:
            self.last_eng[eng] = i
        self.ops.append((eng, fn, deps, chan, inc))
        return i

    def set_fence(self, skip_chans=()):
        self.fence = set(self.last_eng.values()) | {v for c, v in self.chan_last.items() if c not in skip_chans}
        self.fenced = set()

    def emit(self, nc, block):
        ops = self.ops
        n = len(ops)
        need = [False] * n
        for (eng, fn, deps, chan, inc) in ops:
            for d in deps:
                de = ops[d]
                if de[3] is None and de[0] == 'pe' and eng == 'pe' and chan is None:
                    continue
                need[d] = True
        engs = ['pe', 'act', 'dve', 'pool', 'sp']
        sem_e = {e: nc.semaphore('s_' + e).__enter__() for e in engs}
        chans = sorted({o[3] for o in ops if o[3] is not None}, key=str)
        sem_c = {c: nc.semaphore('c%d' % k).__enter__() for k, c in enumerate(chans)}
        cnt = {e: 0 for e in engs}
        ccnt = {c: 0 for c in chans}
        val = [None] * n
        for i, (eng, fn, deps, chan, inc) in enumerate(ops):
            if chan is not None:
                ccnt[chan] += inc
                val[i] = (('c', chan), ccnt[chan])
            elif need[i]:
                cnt[eng] += 1
                val[i] = (('e', eng), cnt[eng])

        def semof(key):
            return sem_c[key[1]] if key[0] == 'c' else sem_e[key[1]]

        per = {e: [] for e in engs}
        for i, o in enumerate(ops):
            per[o[0]].append(i)
        self.nwaits = 0

        def body(e, eo):
            waited = {}
            for i in per[e]:
                eng, fn, deps, chan, inc = ops[i]
                w = {}
                for d in deps:
                    de = ops[d]
                    if de[3] is None and de[0] == 'pe' and e == 'pe' and chan is None:
                        continue
                    k, v = val[d]
                    if w.get(k, 0) < v:
                        w[k] = v
                for k, v in w.items():
                    if waited.get(k, 0) < v:
                        eo.wait_ge(semof(k), v)
                        waited[k] = v
                        self.nwaits += 1
                ins = fn(eo)
                if val[i] is not None and ins is not None:
                    k, v = val[i]
                    ins.then_inc(semof(k), inc if chan is not None else 1)

        @block.tensor
        def _(eo):
            body('pe', eo)

        @block.scalar
        def _(eo):
            body('act', eo)

        @block.vector
        def _(eo):
            body('dve', eo)

        @block.gpsimd
        def _(eo):
            body('pool', eo)

        @block.sync
        def _(eo):
            body('sp', eo)


class Arena:
    def __init__(self, base_ap, size):
        self.base = base_ap
        self.size = size
        self.off = 0

    def mark(self):
        return self.off

    def reset(self, m):
        self.off = m

    def alloc(self, shape, dt):
        esz = 4 if dt == F32 else 2
        free = int(np.prod(shape[1:]))
        nb = free * esz
        nb_al = (nb + 63) // 64 * 64
        assert self.off + nb_al <= self.size, ("SBUF arena overflow", self.off, nb_al, self.size)
        v = self.base[:, self.off:self.off + nb].bitcast(dt)
        self.off += nb_al
        if len(shape) == 3:
            v = v.rearrange("p (a b) -> p a b", b=shape[2])
        elif len(shape) == 4:
            v = v.rearrange("p (a b c) -> p a b c", b=shape[2], c=shape[3])
        return v


def piece_srcs(idx):
    if idx < 4:
        return [('w_in', idx * 256, 256)]
    if idx < 8:
        return [('w_in', 3072 + (idx - 4) * 256, 256)]
    if idx < 24:
        ctp, t = divmod(idx - 8, 4)
        base = (5120, 6144, 4096, 7168)[t]
        return [('w_in', base + ctp * 256, 256)]
    if idx < 48:
        j, t = divmod(idx - 24, 3)
        if t == 0:
            return [('w_in', 8192 + j * 256, 256)]
        if t == 1:
            return [('w_in', 10240 + j * 256, 256)]
        return [('w_br', j * 256, 256)]
    return [('w_out', (idx - 48) * 256, 256)]


def build():
    nc = bass.Bass("TRN2", target_bir_lowering=False)
    P = Prog()
    dbg_outs = {}

    def dram_in(name, shape, dt=F32):
        return nc.dram_tensor(name, list(shape), dt, kind="ExternalInput").ap()

    x = dram_in("x", [SEGT, D])
    xh = dram_in("xh", [64, D])
    w_in = dram_in("w_in", [D, 12288])
    w_br = dram_in("w_br", [D, D])
    w_out = dram_in("w_out", [D, D])
    ng = dram_in("ng", [1, D])
    fg = dram_in("fg", [1, D])
    gng = dram_in("gng", [1, 1024])
    cw_d = dram_in("cw", [128, 24])
    dl = dram_in("dl", [1, 16])
    ropeq_d = dram_in("ropeq", [128, NCH * 128])
    ropek_d = dram_in("ropek", [128, NCH * 128])
    consts_d = dram_in("consts", [128, 1184])
    seg_d = dram_in("seg", [1, 16])
    y = nc.dram_tensor("y", [SEGT, D], F32, kind="ExternalOutput").ap()
    wsrc = {'w_in': w_in, 'w_br': w_br, 'w_out': w_out}

    skind = "ExternalOutput" if DEBUG else "Internal"
    kvs = nc.dram_tensor("kvs", [NCH * 128, 2048], BF16, kind=skind).ap()
    bst = nc.dram_tensor("bst", [NCH * 128, 1024], F32, kind=skind).ap()
    wsc = nc.dram_tensor("wsc", [NPIECE * 128, 16 * 256], BF16).ap()
    bounce = nc.dram_tensor("bounce", [128, 2048], F32).ap()
    gath = nc.dram_tensor("gath", [4 * 128, 2048], F32).ap()

    ARENA = 207 * 1024
    arena_t = nc.sbuf_tensor("arena", [128, ARENA], U8).__enter__()
    psum_t = nc.psum_tensor("psum", [128, 4096], F32).__enter__()
    A = Arena(arena_t, ARENA)
    ps_all = psum_t

    psptr = {'a': 0, 'b': 0}

    def psalloc(nb, pool):
        lo, n = (0, 4) if pool == 'a' else (4, 4)
        if pool == 'all':
            lo, n = 0, 8
        p = psptr.setdefault(pool, 0)
        if p % nb:
            p += nb - p % nb
        if p + nb > n:
            p = 0
        psptr[pool] = (p + nb) % n
        b0 = lo + p
        return ps_all[:, b0 * 512:(b0 + nb) * 512], [('ps', b0 + t) for t in range(nb)]

    def dma(q, out, in_, reads, writes, chan):
        P.add(q, lambda e: e.dma_start(out=out, in_=in_), reads, writes, chan=chan)

    def mm(out, lhsT, rhs, start, stop, reads, writes):
        P.add('pe', lambda e: e.matmul(out, lhsT, rhs, start=start, stop=stop), reads, writes)

    def tp(out, in_, ident, reads, writes):
        P.add('pe', lambda e: e.transpose(out, in_, ident), reads + ['ident'], writes)

    def act(out, in_, func, reads, writes, scale=None, accum=None):
        kw = {}
        if scale is not None:
            kw['scale'] = scale
        if accum is not None:
            kw['accum_out'] = accum
        P.add('act', lambda e: e.activation(out=out, in_=in_, func=func, **kw), reads, writes)

    def tt(eng, out, in0, in1, op, reads, writes):
        P.add(eng, lambda e: e.tensor_tensor(out=out, in0=in0, in1=in1, op=op), reads, writes)

    def ts(eng, out, in0, s1, s2, op0, op1, reads, writes):
        if s2 is None:
            P.add(eng, lambda e: e.tensor_scalar(out=out, in0=in0, scalar1=s1, scalar2=None, op0=op0), reads, writes)
        else:
            P.add(eng, lambda e: e.tensor_scalar(out=out, in0=in0, scalar1=s1, scalar2=s2, op0=op0, op1=op1), reads, writes)

    def stt(eng, out, in0, scalar, in1, op0, op1, reads, writes):
        P.add(eng, lambda e: e.scalar_tensor_tensor(out=out, in0=in0, scalar=scalar, in1=in1, op0=op0, op1=op1), reads, writes)

    def cp(eng, out, in_, reads, writes):
        if eng == 'act':
            act(out, in_, AF.Copy, reads, writes)
        else:
            P.add(eng, lambda e: e.tensor_copy(out=out, in_=in_), reads, writes)

    def xload(row0, xslot):
        dma('sp', xin[:, xslot, :], x[row0:row0 + 128, :], [], [('xin', xslot)], ('xin', xslot))

    def rsq(ap, keys):
        act(ap, ap, AF.Sqrt, keys, keys)
        P.add('dve', lambda e: e.reciprocal(out=ap, in_=ap), keys, keys)

    def dbg(name, ap, key, shape):
        if not DEBUG:
            return
        t = nc.dram_tensor("dbg_" + name, list(shape), F32, kind="ExternalOutput").ap()
        dbg_outs[name] = t
        dma('sp', t, ap, [key], [('dbg', name)], ('dbg', name))

    identb = A.alloc([128, 128], BF16)
    lg = A.alloc([128, 16], F32)
    DT = A.alloc([128, 1024], BF16)
    QWF = A.alloc([128, 1024], BF16)
    QWB = A.alloc([128, 1024], BF16)
    TF = A.alloc([128, 1024], BF16)
    g128 = A.alloc([128, 16], F32)
    coefB = A.alloc([128, NCH, 8], F32)
    ng_bc = A.alloc([128, D], F32)
    cwt = A.alloc([128, 24], F32)
    Fst = A.alloc([128, 2, 1024], F32)
    B_in = A.alloc([128, 1024], F32)
    ssq = A.alloc([128, 4], F32)
    rstd = A.alloc([128, 4], F32)
    xin = A.alloc([128, 2, D], F32)
    hbuf = A.alloc([128, 2, D], BF16)
    segc = A.alloc([128, 2, 4, 8], F32)
    pers_mark = A.mark()
    cst = A.alloc([128, 1184], F32)
    dlb = A.alloc([128, 16], F32)
    sm = A.alloc([128, 6, 16], F32)
    segb = A.alloc([128, 16], F32)
    TB = A.alloc([128, 1024], BF16)
    coefF = A.alloc([128, NCH, 8], F32)
    tmpA = A.alloc([128, 128], F32)
    tmpB = A.alloc([128, 128], F32)
    rt_box = [A.alloc([128, 4, 512], F32)]

    C_ID, C_EF, C_EB, C_MF, C_MB, C_EL1, C_E128L, C_EC, C_E127C, C_ENB = [k * 128 for k in range(10)]

    dma('sp', cst, consts_d, [], ['cst'], 'cst')
    dma('sp', dlb, dl.partition_broadcast(128), [], ['dlb'], 'dlb')
    dma('sp', segb, seg_d.partition_broadcast(128), [], ['segb'], 'segb')
    dma('sp', ng_bc, ng.partition_broadcast(128), [], ['ng_bc'], 'ng_bc')
    dma('sp', cwt, cw_d, [], ['cwt'], 'cwt')
    cp('dve', identb, cst[:, C_ID:C_ID + 128], ['cst'], ['ident'])
    s_e, s_w, s_l, s_d, s_r = [sm[:, k, :] for k in range(5)]
    act(s_e, dlb, AF.Exp, ['dlb'], ['s_e'], scale=-1.0)
    ts('dve', s_w, s_e, 1.0, None, ALU.add, None, ['s_e'], ['s_w'])
    act(s_l, s_w, AF.Ln, ['s_w'], ['s_l'])
    ts('dve', s_d, s_w, -1.0, 1e-30, ALU.add, ALU.max, ['s_w'], ['s_d'])
    P.add('dve', lambda e: e.reciprocal(out=s_d, in_=s_d), ['s_d'], ['s_d'])
    tt('dve', s_r, s_e, s_d, ALU.mult, ['s_e', 's_d'], ['s_r'])
    stt('dve', lg, s_l, -1.0, s_r, ALU.mult, ALU.mult, ['s_l', 's_r'], ['lg'])
    for h in range(NH):
        lf = lg[:, h:h + 1]
        lb = lg[:, 8 + h:9 + h]
        hs = slice(h * 128, (h + 1) * 128)
        act(tmpA, cst[:, C_EF:C_EF + 128], AF.Exp, ['cst', 'lg'], ['tmpA'], scale=lf)
        act(tmpB, cst[:, C_EB:C_EB + 128], AF.Exp, ['cst', 'lg'], ['tmpB'], scale=lb)
        tt('dve', tmpA, tmpA, cst[:, C_MF:C_MF + 128], ALU.mult, ['tmpA', 'cst'], ['tmpA'])
        tt('dve', tmpB, tmpB, cst[:, C_MB:C_MB + 128], ALU.mult, ['tmpB', 'cst'], ['tmpB'])
        tt('dve', DT[:, hs], tmpA, tmpB, ALU.add, ['tmpA', 'tmpB'], ['DT'])
        act(QWF[:, hs], cst[:, C_EC:C_EC + 128], AF.Exp, ['cst', 'lg'], ['QWF'], scale=lf)
        act(QWB[:, hs], cst[:, C_E127C:C_E127C + 128], AF.Exp, ['cst', 'lg'], ['QWB'], scale=lb)
        act(TB[:, hs], cst[:, C_EL1:C_EL1 + 128], AF.Exp, ['cst', 'lg'], ['TB'], scale=lb)
        act(TF[:, hs], cst[:, C_E128L:C_E128L + 128], AF.Exp, ['cst', 'lg'], ['TF'], scale=lf)
        act(coefF[:, :, h], cst[:, C_ENB:C_ENB + NCH], AF.Exp, ['cst', 'lg'], ['coefF'], scale=lf)
        act(coefB[:, :, h], cst[:, C_ENB:C_ENB + NCH], AF.Exp, ['cst', 'lg'], ['coefB'], scale=lb)
    act(g128, lg, AF.Exp, ['lg'], ['g128'], scale=128.0)
    for r in range(4):
        act(segc[:, 0, r, :], lg[:, 0:8], AF.Exp, ['lg', 'segb'], ['segc'], scale=segb[:, r:r + 1])
        act(segc[:, 1, r, :], lg[:, 8:16], AF.Exp, ['lg', 'segb'], ['segc'], scale=segb[:, 4 + r:5 + r])
        ts('dve', segc[:, 0, r, :], segc[:, 0, r, :], segb[:, 8 + r:9 + r], None, ALU.mult, None, ['segc', 'segb'], ['segc'])
        ts('dve', segc[:, 1, r, :], segc[:, 1, r, :], segb[:, 12 + r:13 + r], None, ALU.mult, None, ['segc', 'segb'], ['segc'])
    dbg('lg', lg, 'lg', [128, 16])
    dbg('segc', segc.rearrange("p a b c -> p (a b c)"), 'segc', [128, 64])

    def prep_elem(row0, nrows, xslot, src=None, hslot=0, noload=False):
        src = x if src is None else src
        sq, rs = ssq[:nrows, hslot:hslot + 1], rstd[:nrows, hslot:hslot + 1]
        ksq, krs = ('ssq', hslot), ('rstd', hslot)
        if not noload:
            dma('sp', xin[:nrows, xslot, :], src[row0:row0 + nrows, :], [], [('xin', xslot)], ('xin', xslot))
        act(hbuf[:nrows, hslot, :], xin[:nrows, xslot, :], AF.Square, [('xin', xslot)], [('hbuf', hslot), ksq], accum=sq)
        ts('dve', rs, sq, 1.0 / D, EPS, ALU.mult, ALU.add, [ksq], [krs])
        rsq(rs, [krs])
        stt('dve', hbuf[:nrows, hslot, :], xin[:nrows, xslot, :], rs, ng_bc[:nrows], ALU.mult, ALU.mult,
            [('xin', xslot), krs, 'ng_bc'], [('hbuf', hslot)])

    def prep_pe(nrows, dst, dkeys, hslot=0, bank=None):
        if bank is None:
            pst, pk = psalloc(2, 'b')
        else:
            pst, pk = ps_all[:, bank * 512:(bank + 2) * 512], [('ps', bank), ('ps', bank + 1)]
        pstb = pst.bitcast(BF16)
        for k in range(16):
            tp(pstb[:, k * nrows:(k + 1) * nrows], hbuf[:nrows, hslot, k * 128:(k + 1) * 128], identb[:nrows, :nrows],
               [('hbuf', hslot)], pk)
        v = pstb[:, 0:16 * nrows].rearrange("p (k t) -> p k t", t=nrows)
        cp('act', dst[:, 0:8, :], v[:, 0:8, :], pk, dkeys)
        cp('dve', dst[:, 8:16, :], v[:, 8:16, :], pk, dkeys)

    def rotary(ps3, nh, cos, sin, dst3, rkeys, wkeys, comb='pool'):
        cb = cos.unsqueeze(1).broadcast_to([128, nh, 64])
        sb = sin.unsqueeze(1).broadcast_to([128, nh, 64])
        x1 = ps3[:, :, 0:64]
        x2 = ps3[:, :, 64:128]
        t = [rt_box[0][:, k, 0:nh * 64].rearrange("p (h e) -> p h e", e=64) for k in range(4)]
        tt('dve', t[0], x1, cb, ALU.mult, rkeys, [('rt', 0)])
        tt('dve', t[1], x2, sb, ALU.mult, rkeys, [('rt', 1)])
        tt('dve', t[2], x2, cb, ALU.mult, rkeys, [('rt', 2)])
        tt('dve', t[3], x1, sb, ALU.mult, rkeys, [('rt', 3)])
        tt(comb, dst3[:, :, 0:64], t[0], t[1], ALU.subtract, [('rt', 0), ('rt', 1)], wkeys)
        tt(comb, dst3[:, :, 64:128], t[2], t[3], ALU.add, [('rt', 2), ('rt', 3)], wkeys)

    wkv = A.alloc([128, 16, 2048], BF16)
    ropek = A.alloc([128, NCH, 128], F32)
    hT1 = A.alloc([128, 2, 16, 128], BF16)
    kvo = A.alloc([128, 2, 2048], BF16)
    Bst = A.alloc([128, 2, 1024], F32)
    Fac = A.alloc([128, 2, 1024], F32)

    w_in_v = w_in.rearrange("(k p) f -> p k f", p=128)
    for q4 in range(4):
        P.add('pool', (lambda q4: lambda e: e.dma_start(out=wkv[:, :, q4 * 512:(q4 + 1) * 512],
                                                        in_=w_in_v[:, :, 1024 + q4 * 512:1024 + (q4 + 1) * 512]))(q4),
              [], [('wkv', q4)], chan=('wkvld', q4))
    dma('sp', ropek, ropek_d.rearrange("p (n c) -> p n c", c=128), [], ['ropek'], 'ropek')
    P.add('dve', lambda e: e.memset(Bst[:, 0, :], 0.0), [], [('Bst', 0)])
    P.add('dve', lambda e: e.memset(Fac[:, 0, :], 0.0), [], [('Fac', 0)])

    halo_list = [8 + 4 * ctp + t for ctp in range(4) for t in range(2)]
    conv_list = halo_list + [i for i in range(NPIECE) if i not in halo_list]
    conv_pos = [0]
    NCV = 32

    def issue_conv(nmax):
        for _ in range(nmax):
            if conv_pos[0] >= len(conv_list):
                return
            idx = conv_list[conv_pos[0]]
            cch = conv_pos[0] % NCV
            conv_pos[0] += 1
            dstp = wsc[idx * 128:(idx + 1) * 128, :].rearrange("p (k f) -> p k f", f=256)
            col = 0
            for (sn, c0, wd) in piece_srcs(idx):
                srcv = wsrc[sn].rearrange("(k p) f -> p k f", p=128)[:, :, c0:c0 + wd]
                P.add('pool', (lambda o, s: lambda e: e.dma_start(out=o, in_=s))(dstp[:, :, col:col + wd], srcv),
                      [], [('wsc', idx)], chan=('cv', cch))
                col += wd

    k32 = A.alloc([128, 1024], F32)
    vBF = A.alloc([128, 2, 2, 1024], BF16)

    def PSB(b, nb):
        return ps_all[:, b * 512:(b + nb) * 512], [('ps', b + t) for t in range(nb)]

    def p1_prep_e(n):
        s_ = n % 2
        prep_elem(n * 128, 128, s_, hslot=s_, noload=True)
        if n - 2 >= 0:
            xload((n - 2) * 128, s_)

    def p1_prep_p(n):
        s_ = n % 2
        prep_pe(128, hT1[:, s_], [('hT1', s_)], hslot=s_, bank=4)

    def p1_proj(n):
        hs_ = n % 2
        o = n % 2
        psk, kk = PSB(0, 2)
        psv, kvk = PSB(2, 2)
        for (psx, kx, cbase, q0) in ((psk, kk, 0, 0), (psv, kvk, 1024, 2)):
            for ft in range(2):
                for k in range(16):
                    mm(psx[:, ft * 512:(ft + 1) * 512], hT1[:, hs_, k, :], wkv[:, k, cbase + ft * 512:cbase + (ft + 1) * 512],
                       k == 0, k == 15, [('hT1', hs_), ('wkv', q0 + ft)], [kx[ft]])
            if cbase == 0:
                cp('act', k32, psk, kk, ['k32'])
            else:
                cp('act', kvo[:, o, 1024:2048], psv, kvk, [('kvo', o, 'v')])

    def p1_post_elem(n):
        o = n % 2
        rotary(k32.rearrange("p (h e) -> p h e", e=128), 8, ropek[:, n, 0:64], ropek[:, n, 64:128],
               kvo[:, o, 0:1024].rearrange("p (h e) -> p h e", e=128), ['k32', 'ropek'], [('kvo', o, 'k')], comb='dve')
        dma('sp', kvs[n * 128:(n + 1) * 128, :], kvo[:, o, :], [('kvo', o, 'k'), ('kvo', o, 'v')], [('kvs', n)], ('kvst', o))
        tt('dve', vBF[:, o, 0, :], kvo[:, o, 1024:2048], TB, ALU.mult, [('kvo', o, 'v'), 'TB'], [('vB', o)])
        tt('dve', vBF[:, o, 1, :], kvo[:, o, 1024:2048], TF, ALU.mult, [('kvo', o, 'v'), 'TF'], [('vF', o)])

    def p1_post_pe(n):
        o = n % 2
        cur = (NCH - 1 - n) % 2
        new = 1 - cur
        dma('sp', bst[n * 128:(n + 1) * 128, :], Bst[:, cur, :], [('Bst', cur)], [('bst', n)], ('bstst', cur))
        pkb, kb = PSB(6, 2)
        pkf, kf = PSB(4, 2)
        for h in range(NH):
            hs = slice(h * 128, (h + 1) * 128)
            mm(pkb[:, hs], kvo[:, o, hs], vBF[:, o, 0, hs], True, True, [('kvo', o, 'k'), ('vB', o)], [kb[h // 4]])
        for h in range(NH):
            hs = slice(h * 128, (h + 1) * 128)
            mm(pkf[:, hs], kvo[:, o, hs], vBF[:, o, 1, hs], True, True, [('kvo', o, 'k'), ('vF', o)], [kf[h // 4]])
        for h in range(NH):
            hs = slice(h * 128, (h + 1) * 128)
            stt('dve', Bst[:, new, hs], Bst[:, cur, hs], g128[:, 8 + h:9 + h], pkb[:, hs], ALU.mult, ALU.add,
                [('Bst', cur), 'g128', kb[h // 4]], [('Bst', new)])
        for h in range(NH):
            hs = slice(h * 128, (h + 1) * 128)
            stt('dve', Fac[:, new, hs], pkf[:, hs], coefF[:, n, h:h + 1], Fac[:, cur, hs], ALU.mult, ALU.add,
                [('Fac', cur), 'coefF', kf[h // 4]], [('Fac', new)])

    xload((NCH - 1) * 128, (NCH - 1) % 2)
    xload((NCH - 2) * 128, (NCH - 2) % 2)
    p1_prep_e(NCH - 1)
    p1_prep_p(NCH - 1)
    issue_conv(100)
    for n in range(NCH - 1, -1, -1):
        if n > 0:
            p1_prep_e(n - 1)
        p1_proj(n)
        if n > 0:
            p1_prep_p(n - 1)
        if n < NCH - 1:
            p1_post_pe(n + 1)
        p1_post_elem(n)
    p1_post_pe(0)
    dma('sp', xin[:64, 1, :], xh[0:64, :], [], [('xin', 1)], ('xin', 1))
    xload(0, 0)
    issue_conv(100)
    fin = NCH % 2
    dma('sp', bounce[:, 0:1024], Fac[:, fin, :], [('Fac', fin)], ['bounce'], 'bnc0')
    dma('sp', bounce[:, 1024:2048], Bst[:, fin, :], [('Bst', fin)], ['bounce2'], 'bnc1')
    P.add('pool', lambda e: e.collective_compute("AllGather", ALU.bypass, replica_groups=[[0, 1, 2, 3], [4, 5, 6, 7]],
                                                 ins=[bounce], outs=[gath]),
          ['bounce', 'bounce2'], ['gath'], chan='cc', inc=1)
    def finish_exchange(Gbuf, gkeys):
        gv = gath.rearrange("(r p) f -> p r f", p=128)
        dma('sp', Gbuf, gv[:, :, 0:1024], ['gath'], gkeys, 'gld')
        for h in range(NH):
            hs = slice(h * 128, (h + 1) * 128)
            ts('dve', Fst[:, 0, hs], Gbuf[:, 0, hs], segc[:, 0, 0, h:h + 1], None, ALU.mult, None, gkeys + ['segc'], [('Fst', 0)])
            for r in range(1, 4):
                stt('dve', Fst[:, 0, hs], Gbuf[:, r, hs], segc[:, 0, r, h:h + 1], Fst[:, 0, hs], ALU.mult, ALU.add,
                    gkeys + ['segc', ('Fst', 0)], [('Fst', 0)])
        dma('sp', Gbuf, gv[:, :, 1024:2048], ['gath'], gkeys, 'gld')
        for h in range(NH):
            hs = slice(h * 128, (h + 1) * 128)
            ts('dve', B_in[:, hs], Gbuf[:, 0, hs], segc[:, 1, 0, h:h + 1], None, ALU.mult, None, gkeys + ['segc'], ['B_in'])
            for r in range(1, 4):
                stt('dve', B_in[:, hs], Gbuf[:, r, hs], segc[:, 1, r, h:h + 1], B_in[:, hs], ALU.mult, ALU.add,
                    gkeys + ['segc', 'B_in'], ['B_in'])
        dbg('F_in', Fst[:, 0, :], ('Fst', 0), [128, 1024])
        dbg('B_in', B_in, 'B_in', [128, 1024])

    if STAGE < 2:
        Gb = wkv.rearrange("p k f -> p (k f)")[:, 0:8192].bitcast(F32).rearrange("p (r f) -> p r f", f=1024)
        finish_exchange(Gb, [('wkv', q) for q in range(4)])
    dbg('Fsum', Fac[:, fin, :], ('Fac', fin), [128, 1024])

    out_keys = []
    if STAGE >= 2:
        P.set_fence(skip_chans=('cc',) + tuple(('cv', k) for k in range(32)))
        A.reset(pers_mark)
        out_keys = phase2(nc, P, A, locals())

    P.add('sp', lambda e: None, reads=out_keys + [('dbg', k) for k in dbg_outs] + ['B_in', ('Fst', 0)] +
          ([('kvs', n) for n in range(NCH)] + [('bst', n) for n in range(NCH)] if STAGE < 2 else []), writes=[])
    if STAGE < 2:
        dma('sp', y[0:128, :], x[0:128, :], [], [('y', 0)], 'ycopy')
        P.add('sp', lambda e: None, reads=[('y', 0)], writes=[])

    with nc.allow_low_precision("bf16 matmul operands, fp32 accumulation"):
        with nc.Block() as block:
            P.emit(nc, block)
    return nc, dbg_outs


def phase2(nc, P, A, env):
    g = env
    x, xh, y, kvs, bst, wsc = g['x'], g['xh'], g['y'], g['kvs'], g['bst'], g['wsc']
    dma, mm, tp, act, tt, ts, stt, cp = g['dma'], g['mm'], g['tp'], g['act'], g['tt'], g['ts'], g['stt'], g['cp']
    psalloc, prep_elem, prep_pe, rotary, rsq = g['psalloc'], g['prep_elem'], g['prep_pe'], g['rotary'], g['rsq']
    identb, xin, hbuf, ssq, rstd = g['identb'], g['xin'], g['hbuf'], g['ssq'], g['rstd']
    DT, QWF, QWB, TF, g128, coefB, Fst, B_in, cwt = (g['DT'], g['QWF'], g['QWB'], g['TF'], g['g128'], g['coefB'],
                                                     g['Fst'], g['B_in'], g['cwt'])
    fg, gng, ropeq_d, dbg = g['fg'], g['gng'], g['ropeq_d'], g['dbg']

    g['rt_box'][0] = A.alloc([128, 4, 128], F32)
    fg_bc = A.alloc([128, D], F32)
    gn_bc = A.alloc([128, 1024], F32)
    NW = 4
    Wr = A.alloc([128, NW, 16, 256], BF16)
    ropeq = A.alloc([128, 2, 2, 128], F32)
    hT = A.alloc([128, 2, 16, T], BF16)
    uh = A.alloc([128, 8, 64], F32)
    q_rot = A.alloc([128, 2, 1024], BF16)
    sgn = A.alloc([128, 2, 1024], BF16)
    ccs = A.alloc([128, 2, T], F32)
    ubuf = A.alloc([128, 2, T + 2], F32)
    tacc = A.alloc([128, 2, T], F32)
    sgc = A.alloc([128, 2, T], F32)
    bconvT = A.alloc([128, 8, T], BF16)
    kvb = A.alloc([128, 2, 2048], BF16)
    Bl = A.alloc([128, 1024], F32)
    qT = A.alloc([128, 1024], BF16)
    kT = A.alloc([128, 1024], BF16)
    qTf = A.alloc([128, 1024], BF16)
    qTb = A.alloc([128, 1024], BF16)
    Bfull = A.alloc([128, 1024], BF16)
    Fbf = A.alloc([128, 1024], BF16)
    vF = A.alloc([128, 1024], BF16)
    STm = A.alloc([128, 1024], BF16)
    ssqo = A.alloc([128, 8], F32)
    rso = A.alloc([128, 8], F32)
    bret = A.alloc([128, 1024], BF16)
    hTh = bret.rearrange("p (k t) -> p k t", t=64)
    bretT = A.alloc([128, 8, T], BF16)
    gates = A.alloc([128, 2, 2, 256], BF16)
    m12 = A.alloc([128, 2, 2, 256], F32)
    merged = A.alloc([128, 2, 2, 256], BF16)
    mT = A.alloc([128, 16, T], BF16)
    xo = A.alloc([128, 2, D], F32)

    dma('sp', fg_bc, fg.partition_broadcast(128), [], ['fg_bc'], 'fg_bc')
    dma('sp', gn_bc, gng.partition_broadcast(128), [], ['gn_bc'], 'gn_bc')

    if REORDER:
        wseq = [8 + 4 * ctp + t for ctp in range(4) for t in range(2)] + list(range(8))
        for i in range(NT):
            wseq += list(range(8, 48)) + (list(range(8)) if i + 1 < NT else []) + list(range(48, 56))
    else:
        wseq = [8 + 4 * ctp + t for ctp in range(4) for t in range(2)]
        for i in range(NT):
            wseq += list(range(NPIECE))
    wstate = {'issued': 0, 'next': 0}

    def wget(expect_idx):
        gq = wstate['next']
        assert wseq[gq] == expect_idx, (gq, wseq[gq], expect_idx)
        while wstate['issued'] < min(gq + NW, len(wseq)):
            q = wstate['issued']
            s = q % NW
            idx = wseq[q]
            dma('sp', Wr[:, s], wsc[idx * 128:(idx + 1) * 128, :].rearrange("p (k f) -> p k f", f=256),
                [('wsc', idx)], [('W', s)], ('W', s))
            wstate['issued'] += 1
        wstate['next'] += 1
        s = gq % NW
        return Wr[:, s], ('W', s)

    prep_elem(0, 64, 1, src=xh, noload=True)
    prep_pe(64, hTh, ['bret'])
    g['xload'](128, 1)
    ccs_f = ccs.rearrange("p g t -> p (g t)")
    for ctp in range(4):
        pss = []
        for t in range(2):
            W, wk = wget(8 + 4 * ctp + t)
            psh, hk = psalloc(1, 'a')
            for gq in range(2):
                for k in range(16):
                    mm(psh[:, gq * 64:(gq + 1) * 64], W[:, k, gq * 128:(gq + 1) * 128], hTh[:, k, :], k == 0, k == 15,
                       [wk, 'bret'], hk)
            pss.append((psh, hk))
            if t == 0:
                cp('act', ccs_f[:, 0:128], psh[:, 0:128], hk, ['ccs'])
        psh, hk = pss[1]
        tt('dve', uh[:, 2 * ctp:2 * ctp + 2, :], psh[:, 0:128].rearrange("p (g t) -> p g t", t=64),
           ccs_f[:, 0:128].rearrange("p (g t) -> p g t", t=64), ALU.mult, hk + ['ccs'], ['uh'])
    dbg('uh', uh.rearrange("p a b -> p (a b)"), 'uh', [128, 512])

    xs = {'n': 0}

    def p2_prep_elem(i):
        slots = []
        for c in range(2):
            s = xs['n'] % 2
            xs['n'] += 1
            slots.append(s)
        return slots

    def p2_prep_e(i):
        hs_ = i % 2
        dma('sp', ropeq[:, hs_], ropeq_d.rearrange("p (n c) -> p n c", c=128)[:, 2 * i:2 * i + 2, :], [],
            [('ropeq', hs_)], ('ropeq', hs_))
        for c in range(2):
            prep_elem((2 * i + c) * 128, 128, c, hslot=c, noload=True)
            if i + 1 < NT:
                g['xload']((2 * (i + 1) + c) * 128, c)

    def p2_prep_p(i):
        hs_ = i % 2
        for c in range(2):
            pst, pk = psalloc(2, 'b')
            pstb = pst.bitcast(BF16)
            for k in range(16):
                tp(pstb[:, k * 128:(k + 1) * 128], hbuf[:, c, k * 128:(k + 1) * 128], identb, [('hbuf', c)], pk)
            v = pstb.rearrange("p (k t) -> p k t", t=128)
            cp('act', hT[:, hs_, 0:8, c * 128:(c + 1) * 128], v[:, 0:8, :], pk, [('hT', hs_, c)])
            cp('dve', hT[:, hs_, 8:16, c * 128:(c + 1) * 128], v[:, 8:16, :], pk, [('hT', hs_, c)])

    def tok_piece(i, idx):
        hs_ = i % 2
        W, wk = wget(idx)
        ps, pk = psalloc(1, 'a')
        for c in range(2):
            for k in range(16):
                mm(ps[:, c * 256:(c + 1) * 256], hT[:, hs_, k, c * 128:(c + 1) * 128], W[:, k, :], k == 0, k == 15,
                   [('hT', hs_, c), wk], pk)
        return ps.rearrange("p (c f) -> p c f", f=256), pk

    def feat_piece(i, idx):
        hs_ = i % 2
        W, wk = wget(idx)
        ps, pk = psalloc(1, 'a')
        for gq in range(2):
            for k in range(16):
                mm(ps[:, gq * 256:(gq + 1) * 256], W[:, k, gq * 128:(gq + 1) * 128], hT[:, hs_, k, :], k == 0, k == 15,
                   [('hT', hs_, 0), ('hT', hs_, 1), wk], pk)
        return ps.rearrange("p (g t) -> p g t", t=256), pk

    def conv_A(i, ctp):
        ps, pk = feat_piece(i, 8 + 4 * ctp)
        cp('act', ccs, ps, pk, ['ccs'])
        ps, pk = feat_piece(i, 9 + 4 * ctp)
        tt('dve', ubuf[:, :, 1:T + 1], ps, ccs, ALU.mult, pk + ['ccs'], ['ubuf'])
        for gq in range(2):
            ct = 2 * ctp + gq
            ta = tacc[:, gq, :]
            cp('pool', ubuf[:, gq, 0:1], uh[:, ct, 2 * i:2 * i + 1], ['uh'], ['ubuf'])
            cp('pool', ubuf[:, gq, T + 1:T + 2], uh[:, ct, 2 * i + 3:2 * i + 4], ['uh'], ['ubuf'])
            ts('dve', ta, ubuf[:, gq, 0:T], cwt[:, ct * 3:ct * 3 + 1], None, ALU.mult, None, ['ubuf', 'cwt'], ['tacc'])
            stt('dve', ta, ubuf[:, gq, 1:T + 1], cwt[:, ct * 3 + 1:ct * 3 + 2], ta, ALU.mult, ALU.add, ['ubuf', 'cwt', 'tacc'], ['tacc'])
            stt('dve', ta, ubuf[:, gq, 2:T + 2], cwt[:, ct * 3 + 2:ct * 3 + 3], ta, ALU.mult, ALU.add, ['ubuf', 'cwt', 'tacc'], ['tacc'])

    def conv_B(i, ctp):
        ps, pk = feat_piece(i, 10 + 4 * ctp)
        tt('dve', tacc, ps, tacc, ALU.mult, pk + ['tacc'], ['tacc'])
        ps, pk = feat_piece(i, 11 + 4 * ctp)
        act(sgc, ps, AF.Silu, pk, ['sgc'])
        tt('pool', bconvT[:, 2 * ctp:2 * ctp + 2, :], tacc, sgc, ALU.mult, ['tacc', 'sgc'],
           [('bconvT', 2 * ctp), ('bconvT', 2 * ctp + 1)])

    def conv_pair(i, s8):
        if s8 % 2 == 0:
            conv_A(i, s8 // 2)
        else:
            conv_B(i, s8 // 2)

    def ret_stage(i, c, st, S):
        n = 2 * i + c
        cur = n % 2
        new = 1 - cur
        H3 = lambda ap: ap.rearrange("p (h e) -> p h e", e=128)
        if st == 0:
            psq, qk = psalloc(1, 'b')
            psk, kk = psalloc(1, 'b')
            psqb, pskb = psq.bitcast(BF16), psk.bitcast(BF16)
            for h in range(NH):
                hs = slice(h * 128, (h + 1) * 128)
                tp(psqb[:, hs], q_rot[:, c, hs], identb, [('q_rot', c)], qk)
            for h in range(NH):
                hs = slice(h * 128, (h + 1) * 128)
                tp(pskb[:, hs], kvb[:, c, hs], identb, [('kvb', c)], kk)
            cp('act', qT, psqb, qk, ['qT'])
            cp('act', kT, pskb, kk, ['kT'])
            tt('dve', qTf, psqb, QWF, ALU.mult, qk + ['QWF'], ['qTf'])
            tt('dve', qTb, psqb, QWB, ALU.mult, qk + ['QWB'], ['qTb'])
            if n == 0:
                dma('sp', Bl, bst[0:128, :], [('bst', 0)], ['Bl'], 'Bl')
            for h in range(NH):
                hs = slice(h * 128, (h + 1) * 128)
                stt('dve', Bfull[:, hs], B_in[:, hs], coefB[:, n, h:h + 1], Bl[:, hs], ALU.mult, ALU.add,
                    ['B_in', 'coefB', 'Bl'], ['Bfull'])
            if n + 1 < NCH:
                dma('sp', Bl, bst[(n + 1) * 128:(n + 2) * 128, :], [('bst', n + 1)], ['Bl'], 'Bl')
            cp('act', Fbf, Fst[:, cur, :], [('Fst', cur)], ['Fbf'])
            tt('pool', vF, kvb[:, c, 1024:2048], TF, ALU.mult, [('kvb', c), 'TF'], ['vF'])
        elif st == 1:
            psS, sk = psalloc(2, 'b')
            for h in range(NH):
                hs = slice(h * 128, (h + 1) * 128)
                mm(psS[:, hs], kT[:, hs], qT[:, hs], True, True, ['kT', 'qT'], [sk[h // 4]])
            tt('dve', STm, psS, DT, ALU.mult, sk + ['DT'], ['STm'])
        elif st == 2:
            psO, ok = psalloc(2, 'b')
            S['psO'], S['ok'] = psO, ok
            for h in range(NH):
                hs = slice(h * 128, (h + 1) * 128)
                vs = slice(1024 + h * 128, 1024 + (h + 1) * 128)
                mm(psO[:, hs], STm[:, hs], kvb[:, c, vs], True, False, ['STm', ('kvb', c)], [ok[h // 4]])
                mm(psO[:, hs], qTf[:, hs], Fbf[:, hs], False, False, ['qTf', 'Fbf'], [ok[h // 4]])
                mm(psO[:, hs], qTb[:, hs], Bfull[:, hs], False, True, ['qTb', 'Bfull'], [ok[h // 4]])
            psK, kk = psalloc(2, 'b')
            for h in range(NH):
                hs = slice(h * 128, (h + 1) * 128)
                mm(psK[:, hs], kvb[:, c, hs], vF[:, hs], True, True, [('kvb', c), 'vF'], [kk[h // 4]])
            for h in range(NH):
                hs = slice(h * 128, (h + 1) * 128)
                act(bret[:, hs], psO[:, hs], AF.Square, [ok[h // 4]], ['bret', 'ssqo'], accum=ssqo[:, h:h + 1])
            ts('dve', rso, ssqo, 1.0 / 128, EPS, ALU.mult, ALU.add, ['ssqo'], ['rso'])
            rsq(rso, ['rso'])
            for h in range(NH):
                hs = slice(h * 128, (h + 1) * 128)
                stt('dve', bret[:, hs], psO[:, hs], rso[:, h:h + 1], sgn[:, c, hs], ALU.mult, ALU.mult,
                    [ok[h // 4], 'rso', ('sgn', c)], ['bret'])
            for h in range(NH):
                hs = slice(h * 128, (h + 1) * 128)
                stt('dve', Fst[:, new, hs], Fst[:, cur, hs], g128[:, h:h + 1], psK[:, hs], ALU.mult, ALU.add,
                    [('Fst', cur), 'g128', kk[h // 4]], [('Fst', new)])
        else:
            psb, bk = psalloc(1, 'b')
            psbb = psb.bitcast(BF16)
            for h in range(NH):
                hs = slice(h * 128, (h + 1) * 128)
                tp(psbb[:, hs], bret[:, hs], identb, ['bret'], bk)
            cp('act', bretT[:, :, c * 128:(c + 1) * 128], H3(psbb), bk, [('bretT', c)])

    def merged_T(i, j):
        sl = j % 2
        pst, pk = psalloc(1, 'b')
        pstb = pst.bitcast(BF16)
        for c in range(2):
            for kk in range(2):
                tp(pstb[:, kk * 256 + c * 128: kk * 256 + (c + 1) * 128], merged[:, sl, c, kk * 128:(kk + 1) * 128], identb,
                   [('merged', sl)], pk)
        eng = 'act' if j % 2 == 0 else 'dve'
        cp(eng, mT[:, 2 * j:2 * j + 2, :], pstb[:, 0:512].rearrange("p (k t) -> p k t", t=256), pk, [('mT', j)])

    out_keys = []

    def kv_loads(i):
        for c in range(2):
            n = 2 * i + c
            dma('sp', kvb[:, c, :], kvs[n * 128:(n + 1) * 128, :], [('kvs', n)], [('kvb', c)], ('kvb', c))

    def qg_pieces(i):
        hs_ = i % 2
        for p in range(4):
            ps, pk = tok_piece(i, p)
            for c in range(2):
                rotary(ps[:, c, :].rearrange("p (h e) -> p h e", e=128), 2, ropeq[:, hs_, c, 0:64], ropeq[:, hs_, c, 64:128],
                       q_rot[:, c, p * 256:(p + 1) * 256].rearrange("p (h e) -> p h e", e=128),
                       pk + [('ropeq', hs_)], [('q_rot', c)], comb='dve')
        for p in range(4):
            ps, pk = tok_piece(i, 4 + p)
            act(sgn[:, :, p * 256:(p + 1) * 256], ps, AF.Silu, pk, [('sgn', 0), ('sgn', 1)])
            tt('pool', sgn[:, :, p * 256:(p + 1) * 256], sgn[:, :, p * 256:(p + 1) * 256],
               gn_bc[:, p * 256:(p + 1) * 256].unsqueeze(1).broadcast_to([128, 2, 256]), ALU.mult,
               [('sgn', 0), ('sgn', 1), 'gn_bc'], [('sgn', 0), ('sgn', 1)])

    def final_norm(i):
        for c in range(2):
            n = 2 * i + c
            act(mT.rearrange("p k t -> p (k t)")[:, 0:D], xo[:, c, :], AF.Square, [('xo', c)],
                [('mT', j) for j in range(8)] + [('ssq2', c)], accum=ssq[:, 2 + c:3 + c])
            ts('dve', rstd[:, 2 + c:3 + c], ssq[:, 2 + c:3 + c], 1.0 / D, EPS, ALU.mult, ALU.add, [('ssq2', c)], [('rstd2', c)])
            rsq(rstd[:, 2 + c:3 + c], [('rstd2', c)])
            if c == 0:
                act(xo[:, c, :], xo[:, c, :], AF.Copy, [('xo', c), ('rstd2', c)], [('xo', c)], scale=rstd[:, 2 + c:3 + c])
                tt('pool', xo[:, c, :], xo[:, c, :], fg_bc, ALU.mult, [('xo', c), 'fg_bc'], [('xo', c)])
            else:
                stt('dve', xo[:, c, :], xo[:, c, :], rstd[:, 2 + c:3 + c], fg_bc, ALU.mult, ALU.mult,
                    [('xo', c), ('rstd2', c), 'fg_bc'], [('xo', c)])
            dma('sp', y[n * 128:(n + 1) * 128, :], xo[:, c, :], [('xo', c)], [('y', n)], ('yst', c))
            out_keys.append(('y', n))

    p2_prep_e(0)
    p2_prep_p(0)
    kv_loads(0)
    if REORDER:
        qg_pieces(0)
    for i in range(NT):
        hs_ = i % 2
        if not REORDER:
            qg_pieces(i)
        S = {}
        if i == 0:
            for s8 in range(8):
                conv_pair(i, s8)
            p2_prep_e(i + 1)
            g['finish_exchange'](xo.rearrange("p c f -> p (c f)").rearrange("p (r f) -> p r f", f=1024), [('xo', 0), ('xo', 1)])
            for s8 in range(8):
                ret_stage(i, s8 // 4, s8 % 4, S)
        else:
            for s8 in range(8):
                ret_stage(i, s8 // 4, s8 % 4, S)
                conv_pair(i, s8)
                if s8 == 0 and i + 1 < NT:
                    p2_prep_e(i + 1)
                if s8 == 1:
                    final_norm(i - 1)
        if i + 1 < NT:
            p2_prep_p(i + 1)
            kv_loads(i + 1)
        dma('sp', xo, x[i * T:(i + 1) * T, :].rearrange("(c p) f -> p c f", p=128), [], [('xo', 0), ('xo', 1)], 'xo')
        for j in range(8):
            for br in range(2):
                ps, pk = feat_piece(i, 24 + 3 * j + br)
                act(gates[:, br, :, :], ps, AF.Sigmoid, pk, [('gates', br)])
            W, wk = wget(24 + 3 * j + 2)
            psu, uk = psalloc(2, 'b')
            psu4 = psu.rearrange("p (hf b t) -> p hf b t", b=2, t=256)
            for hf in range(2):
                for br in range(2):
                    src = bretT if br == 0 else bconvT
                    for kf in range(8):
                        rk = [('bretT', 0), ('bretT', 1)] if br == 0 else [('bconvT', kf)]
                        mm(psu4[:, hf, br, :], W[:, br * 8 + kf, hf * 128:(hf + 1) * 128], src[:, kf, :], kf == 0, kf == 7,
                           rk + [wk], [uk[hf]])
            tt('dve', m12[:, :, 0, :], psu4[:, :, 0, :], gates[:, 0, :, :], ALU.mult, uk + [('gates', 0)], [('m12', 0)])
            tt('dve', m12[:, :, 1, :], psu4[:, :, 1, :], gates[:, 1, :, :], ALU.mult, uk + [('gates', 1)], [('m12', 1)])
            tt('pool', mT[:, 2 * j:2 * j + 2, :], m12[:, :, 0, :], m12[:, :, 1, :], ALU.add, [('m12', 0), ('m12', 1)], [('mT', j)])
        if REORDER and i + 1 < NT:
            qg_pieces(i + 1)
        for j in range(8):
            W, wk = wget(48 + j)
            ps, pk = psalloc(1, 'a')
            for c in range(2):
                for k in range(16):
                    mm(ps[:, c * 256:(c + 1) * 256], mT[:, k, c * 128:(c + 1) * 128], W[:, k, :], k == 0, k == 15,
                       [('mT', k // 2), wk], pk)
            xv = xo[:, :, j * 256:(j + 1) * 256]
            tt('dve', xv, ps.rearrange("p (c f) -> p c f", f=256), xv, ALU.add, pk + [('xo', 0), ('xo', 1)], [('xo', 0), ('xo', 1)])
        if i == NT - 1:
            final_norm(i)
    return out_keys


_NC_CACHE = {}


def _host_consts():
    c = np.zeros((128, 1184), np.float32)
    l = np.arange(128, dtype=np.float64)[:, None]
    cc = np.arange(128, dtype=np.float64)[None, :]
    c[:, 0:128] = np.eye(128)
    c[:, 128:256] = np.maximum(cc - l, 0)
    c[:, 256:384] = np.maximum(l - cc, 0)
    c[:, 384:512] = (cc >= l)
    c[:, 512:640] = (l > cc)
    c[:, 640:768] = l + 1
    c[:, 768:896] = 128 - l
    c[:, 896:1024] = cc
    c[:, 1024:1152] = 127 - cc
    c[:, 1152:1184] = 128.0 * (31 - np.arange(32))[None, :]
    return c


def _rope_tables(pos0):
    pos = (pos0 + np.arange(SEGT)).astype(np.float64)
    inv = 10000.0 ** (-np.arange(0, 128, 2, dtype=np.float64) / 128)
    ang = pos[:, None] * inv[None, :]
    cos, sin = np.cos(ang), np.sin(ang)
    sc = 128 ** -0.5

    def lay(a, b):
        t = np.concatenate([a, b], axis=1).reshape(NCH, 128, 128)
        return np.ascontiguousarray(t.transpose(1, 0, 2).reshape(128, NCH * 128)).astype(np.float32)

    return lay(cos, sin), lay(cos * sc, sin * sc)


def kernel(x, norm_gain, w_in, decay_logit_fwd, decay_logit_bwd, ret_gn_gain, conv_w, w_branch, w_out, final_gain):
    x = np.asarray(x, np.float32)
    Bn, S, _ = x.shape
    if 'nc' not in _NC_CACHE:
        _NC_CACHE['nc'] = build()
    nc, dbg_outs = _NC_CACHE['nc']
    w_in0 = np.ascontiguousarray(np.asarray(w_in, np.float32)[0])
    w_br0 = np.ascontiguousarray(np.asarray(w_branch, np.float32)[0].reshape(2 * 1024, D))
    w_out0 = np.ascontiguousarray(np.asarray(w_out, np.float32)[0])
    ng = np.asarray(norm_gain, np.float32)[0].reshape(1, D)
    fg = np.asarray(final_gain, np.float32).reshape(1, D)
    gng = np.asarray(ret_gn_gain, np.float32)[0].reshape(1, 1024)
    cw = np.ascontiguousarray(np.asarray(conv_w, np.float32)[0].reshape(3, 8, 128).transpose(2, 1, 0).reshape(128, 24))
    dl = np.concatenate([np.asarray(decay_logit_fwd, np.float32)[0], np.asarray(decay_logit_bwd, np.float32)[0]]).reshape(1, 16)
    consts = _host_consts()
    in_maps = []
    for c in range(NCORE):
        b, s = divmod(c, 4)
        t0 = s * SEGT
        xs = np.ascontiguousarray(x[b, t0:t0 + SEGT])
        xhal = np.zeros((64, D), np.float32)
        for i in range(NT + 1):
            for k, tok in enumerate((t0 + T * i - 1, t0 + T * i)):
                if 0 <= tok < S:
                    xhal[2 * i + k] = x[b, tok]
        rq, rk = _rope_tables(t0)
        seg = np.zeros((1, 16), np.float32)
        for r in range(4):
            if r < s:
                seg[0, r] = 4096.0 * (s - 1 - r)
                seg[0, 8 + r] = 1.0
            if r > s:
                seg[0, 4 + r] = 4096.0 * (r - s - 1)
                seg[0, 12 + r] = 1.0
        in_maps.append({"x": xs, "xh": xhal, "w_in": w_in0, "w_br": w_br0, "w_out": w_out0, "ng": ng, "fg": fg,
                        "gng": gng, "cw": cw, "dl": dl, "ropeq": rq, "ropek": rk, "consts": consts, "seg": seg})
    res = run_bass_kernel_spmd(nc, in_maps, core_ids=list(range(NCORE)))
    out = np.empty((Bn, S, D), np.float32)
    for c in range(NCORE):
        b, s = divmod(c, 4)
        out[b, s * SEGT:(s + 1) * SEGT] = res.results[c]["y"]
    if DEBUG:
        kernel.last_results = res.results
    return out
```

```python
import numpy as np
import concourse.bass as bass
import concourse.mybir as mybir
from concourse.bass_utils import run_bass_kernel_spmd

F32 = mybir.dt.float32
BF16 = mybir.dt.bfloat16
U8 = mybir.dt.uint8
AF = mybir.ActivationFunctionType
ALU = mybir.AluOpType

D = 2048
NH = 8
SEGT = 4096
NCH = 32
T = 256
NT = 16
EPS = 1e-6
NCORE = 8
NPIECE = 56
DEBUG = False
STAGE = 2
CONV_DVE = True
REORDER = True


class Prog:
    def __init__(self):
        self.ops = []
        self.lw = {}
        self.rd = {}
        self.chan_last = {}
        self.last_eng = {}
        self.fence = None
        self.fenced = set()

    def add(self, eng, fn, reads=(), writes=(), chan=None, inc=16):
        i = len(self.ops)
        deps = set()
        for k in reads:
            if k in self.lw:
                deps.add(self.lw[k])
            if isinstance(k, tuple) and k[0] == 'ps':
                me = ('c', chan) if chan is not None else ('e', eng)
                for wh, r in self.rd.get(k, {}).items():
                    if wh != me:
                        deps.add(r)
        for k in writes:
            if k in self.lw:
                deps.add(self.lw[k])
            for r in self.rd.get(k, {}).values():
                deps.add(r)
        if chan is not None and chan in self.chan_last:
            deps.add(self.chan_last[chan])
        if self.fence is not None and eng not in self.fenced:
            deps |= self.fence
            self.fenced.add(eng)
        deps.discard(i)
        who = ('c', chan) if chan is not None else ('e', eng)
        for k in reads:
            self.rd.setdefault(k, {})[who] = i
        for k in writes:
            self.lw[k] = i
            self.rd[k] = {}
        if chan is not None:
            self.chan_last[chan] = i
        else:
            self.last_eng[eng] = i
        self.ops.append((eng, fn, deps, chan, inc))
        return i

    def set_fence(self, skip_chans=()):
        self.fence = set(self.last_eng.values()) | {v for c, v in self.chan_last.items() if c not in skip_chans}
        self.fenced = set()

    def emit(self, nc, block):
        ops = self.ops
        n = len(ops)
        need = [False] * n
        for (eng, fn, deps, chan, inc) in ops:
            for d in deps:
                de = ops[d]
                if de[3] is None and de[0] == 'pe' and eng == 'pe' and chan is None:
                    continue
                need[d] = True
        engs = ['pe', 'act', 'dve', 'pool', 'sp']
        sem_e = {e: nc.semaphore('s_' + e).__enter__() for e in engs}
        chans = sorted({o[3] for o in ops if o[3] is not None}, key=str)
        sem_c = {c: nc.semaphore('c%d' % k).__enter__() for k, c in enumerate(chans)}
        cnt = {e: 0 for e in engs}
        ccnt = {c: 0 for c in chans}
        val = [None] * n
        for i, (eng, fn, deps, chan, inc) in enumerate(ops):
            if chan is not None:
                ccnt[chan] += inc
                val[i] = (('c', chan), ccnt[chan])
            elif need[i]:
                cnt[eng] += 1
                val[i] = (('e', eng), cnt[eng])

        def semof(key):
            return sem_c[key[1]] if key[0] == 'c' else sem_e[key[1]]

        per = {e: [] for e in engs}
        for i, o in enumerate(ops):
            per[o[0]].append(i)
        self.nwaits = 0

        def body(e, eo):
            waited = {}
            for i in per[e]:
                eng, fn, deps, chan, inc = ops[i]
                w = {}
                for d in deps:
                    de = ops[d]
                    if de[3] is None and de[0] == 'pe' and e == 'pe' and chan is None:
                        continue
                    k, v = val[d]
                    if w.get(k, 0) < v:
                        w[k] = v
                for k, v in w.items():
                    if waited.get(k, 0) < v:
                        eo.wait_ge(semof(k), v)
                        waited[k] = v
                        self.nwaits += 1
                ins = fn(eo)
                if val[i] is not None and ins is not None:
                    k, v = val[i]
                    ins.then_inc(semof(k), inc if chan is not None else 1)

        @block.tensor
        def _(eo):
            body('pe', eo)

        @block.scalar
        def _(eo):
            body('act', eo)

        @block.vector
        def _(eo):
            body('dve', eo)

        @block.gpsimd
        def _(eo):
            body('pool', eo)

        @block.sync
        def _(eo):
            body('sp', eo)


class Arena:
    def __init__(self, base_ap, size):
        self.base = base_ap
        self.size = size
        self.off = 0

    def mark(self):
        return self.off

    def reset(self, m):
        self.off = m

    def alloc(self, shape, dt):
        esz = 4 if dt == F32 else 2
        free = int(np.prod(shape[1:]))
        nb = free * esz
        nb_al = (nb + 63) // 64 * 64
        assert self.off + nb_al <= self.size, ("SBUF arena overflow", self.off, nb_al, self.size)
        v = self.base[:, self.off:self.off + nb].bitcast(dt)
        self.off += nb_al
        if len(shape) == 3:
            v = v.rearrange("p (a b) -> p a b", b=shape[2])
        elif len(shape) == 4:
            v = v.rearrange("p (a b c) -> p a b c", b=shape[2], c=shape[3])
        return v


def piece_srcs(idx):
    if idx < 4:
        return [('w_in', idx * 256, 256)]
    if idx < 8:
        return [('w_in', 3072 + (idx - 4) * 256, 256)]
    if idx < 24:
        ctp, t = divmod(idx - 8, 4)
        base = (5120, 6144, 4096, 7168)[t]
        return [('w_in', base + ctp * 256, 256)]
    if idx < 48:
        j, t = divmod(idx - 24, 3)
        if t == 0:
            return [('w_in', 8192 + j * 256, 256)]
        if t == 1:
            return [('w_in', 10240 + j * 256, 256)]
        return [('w_br', j * 256, 256)]
    return [('w_out', (idx - 48) * 256, 256)]


def build():
    nc = bass.Bass("TRN2", target_bir_lowering=False)
    P = Prog()
    dbg_outs = {}

    def dram_in(name, shape, dt=F32):
        return nc.dram_tensor(name, list(shape), dt, kind="ExternalInput").ap()

    x = dram_in("x", [SEGT, D])
    xh = dram_in("xh", [64, D])
    w_in = dram_in("w_in", [D, 12288])
    w_br = dram_in("w_br", [D, D])
    w_out = dram_in("w_out", [D, D])
    ng = dram_in("ng", [1, D])
    fg = dram_in("fg", [1, D])
    gng = dram_in("gng", [1, 1024])
    cw_d = dram_in("cw", [128, 24])
    dl = dram_in("dl", [1, 16])
    ropeq_d = dram_in("ropeq", [128, NCH * 128])
    ropek_d = dram_in("ropek", [128, NCH * 128])
    consts_d = dram_in("consts", [128, 1184])
    seg_d = dram_in("seg", [1, 16])
    y = nc.dram_tensor("y", [SEGT, D], F32, kind="ExternalOutput").ap()
    wsrc = {'w_in': w_in, 'w_br': w_br, 'w_out': w_out}

    skind = "ExternalOutput" if DEBUG else "Internal"
    kvs = nc.dram_tensor("kvs", [NCH * 128, 2048], BF16, kind=skind).ap()
    bst = nc.dram_tensor("bst", [NCH * 128, 1024], F32, kind=skind).ap()
    wsc = nc.dram_tensor("wsc", [NPIECE * 128, 16 * 256], BF16).ap()
    bounce = nc.dram_tensor("bounce", [128, 2048], F32).ap()
    gath = nc.dram_tensor("gath", [4 * 128, 2048], F32).ap()

    ARENA = 207 * 1024
    arena_t = nc.sbuf_tensor("arena", [128, ARENA], U8).__enter__()
    psum_t = nc.psum_tensor("psum", [128, 4096], F32).__enter__()
    A = Arena(arena_t, ARENA)
    ps_all = psum_t

    psptr = {'a': 0, 'b': 0}

    def psalloc(nb, pool):
        lo, n = (0, 4) if pool == 'a' else (4, 4)
        if pool == 'all':
            lo, n = 0, 8
        p = psptr.setdefault(pool, 0)
        if p % nb:
            p += nb - p % nb
        if p + nb > n:
            p = 0
        psptr[pool] = (p + nb) % n
        b0 = lo + p
        return ps_all[:, b0 * 512:(b0 + nb) * 512], [('ps', b0 + t) for t in range(nb)]

    def dma(q, out, in_, reads, writes, chan):
        P.add(q, lambda e: e.dma_start(out=out, in_=in_), reads, writes, chan=chan)

    def mm(out, lhsT, rhs, start, stop, reads, writes):
        P.add('pe', lambda e: e.matmul(out, lhsT, rhs, start=start, stop=stop), reads, writes)

    def tp(out, in_, ident, reads, writes):
        P.add('pe', lambda e: e.transpose(out, in_, ident), reads + ['ident'], writes)

    def act(out, in_, func, reads, writes, scale=None, accum=None):
        kw = {}
        if scale is not None:
            kw['scale'] = scale
        if accum is not None:
            kw['accum_out'] = accum
        P.add('act', lambda e: e.activation(out=out, in_=in_, func=func, **kw), reads, writes)

    def tt(eng, out, in0, in1, op, reads, writes):
        P.add(eng, lambda e: e.tensor_tensor(out=out, in0=in0, in1=in1, op=op), reads, writes)

    def ts(eng, out, in0, s1, s2, op0, op1, reads, writes):
        if s2 is None:
            P.add(eng, lambda e: e.tensor_scalar(out=out, in0=in0, scalar1=s1, scalar2=None, op0=op0), reads, writes)
        else:
            P.add(eng, lambda e: e.tensor_scalar(out=out, in0=in0, scalar1=s1, scalar2=s2, op0=op0, op1=op1), reads, writes)

    def stt(eng, out, in0, scalar, in1, op0, op1, reads, writes):
        P.add(eng, lambda e: e.scalar_tensor_tensor(out=out, in0=in0, scalar=scalar, in1=in1, op0=op0, op1=op1), reads, writes)

    def cp(eng, out, in_, reads, writes):
        if eng == 'act':
            act(out, in_, AF.Copy, reads, writes)
        else:
            P.add(eng, lambda e: e.tensor_copy(out=out, in_=in_), reads, writes)

    def xload(row0, xslot):
        dma('sp', xin[:, xslot, :], x[row0:row0 + 128, :], [], [('xin', xslot)], ('xin', xslot))

    def rsq(ap, keys):
        act(ap, ap, AF.Sqrt, keys, keys)
        P.add('dve', lambda e: e.reciprocal(out=ap, in_=ap), keys, keys)

    def dbg(name, ap, key, shape):
        if not DEBUG:
            return
        t = nc.dram_tensor("dbg_" + name, list(shape), F32, kind="ExternalOutput").ap()
        dbg_outs[name] = t
        dma('sp', t, ap, [key], [('dbg', name)], ('dbg', name))

    identb = A.alloc([128, 128], BF16)
    lg = A.alloc([128, 16], F32)
    DT = A.alloc([128, 1024], BF16)
    QWF = A.alloc([128, 1024], BF16)
    QWB = A.alloc([128, 1024], BF16)
    TF = A.alloc([128, 1024], BF16)
    g128 = A.alloc([128, 16], F32)
    coefB = A.alloc([128, NCH, 8], F32)
    ng_bc = A.alloc([128, D], F32)
    cwt = A.alloc([128, 24], F32)
    Fst = A.alloc([128, 2, 1024], F32)
    B_in = A.alloc([128, 1024], F32)
    ssq = A.alloc([128, 4], F32)
    rstd = A.alloc([128, 4], F32)
    xin = A.alloc([128, 2, D], F32)
    hbuf = A.alloc([128, 2, D], BF16)
    segc = A.alloc([128, 2, 4, 8], F32)
    pers_mark = A.mark()
    cst = A.alloc([128, 1184], F32)
    dlb = A.alloc([128, 16], F32)
    sm = A.alloc([128, 6, 16], F32)
    segb = A.alloc([128, 16], F32)
    TB = A.alloc([128, 1024], BF16)
    coefF = A.alloc([128, NCH, 8], F32)
    tmpA = A.alloc([128, 128], F32)
    tmpB = A.alloc([128, 128], F32)
    rt_box = [A.alloc([128, 4, 512], F32)]

    C_ID, C_EF, C_EB, C_MF, C_MB, C_EL1, C_E128L, C_EC, C_E127C, C_ENB = [k * 128 for k in range(10)]

    dma('sp', cst, consts_d, [], ['cst'], 'cst')
    dma('sp', dlb, dl.partition_broadcast(128), [], ['dlb'], 'dlb')
    dma('sp', segb, seg_d.partition_broadcast(128), [], ['segb'], 'segb')
    dma('sp', ng_bc, ng.partition_broadcast(128), [], ['ng_bc'], 'ng_bc')
    dma('sp', cwt, cw_d, [], ['cwt'], 'cwt')
    cp('dve', identb, cst[:, C_ID:C_ID + 128], ['cst'], ['ident'])
    s_e, s_w, s_l, s_d, s_r = [sm[:, k, :] for k in range(5)]
    act(s_e, dlb, AF.Exp, ['dlb'], ['s_e'], scale=-1.0)
    ts('dve', s_w, s_e, 1.0, None, ALU.add, None, ['s_e'], ['s_w'])
    act(s_l, s_w, AF.Ln, ['s_w'], ['s_l'])
    ts('dve', s_d, s_w, -1.0, 1e-30, ALU.add, ALU.max, ['s_w'], ['s_d'])
    P.add('dve', lambda e: e.reciprocal(out=s_d, in_=s_d), ['s_d'], ['s_d'])
    tt('dve', s_r, s_e, s_d, ALU.mult, ['s_e', 's_d'], ['s_r'])
    stt('dve', lg, s_l, -1.0, s_r, ALU.mult, ALU.mult, ['s_l', 's_r'], ['lg'])
    for h in range(NH):
        lf = lg[:, h:h + 1]
        lb = lg[:, 8 + h:9 + h]
        hs = slice(h * 128, (h + 1) * 128)
        act(tmpA, cst[:, C_EF:C_EF + 128], AF.Exp, ['cst', 'lg'], ['tmpA'], scale=lf)
        act(tmpB, cst[:, C_EB:C_EB + 128], AF.Exp, ['cst', 'lg'], ['tmpB'], scale=lb)
        tt('dve', tmpA, tmpA, cst[:, C_MF:C_MF + 128], ALU.mult, ['tmpA', 'cst'], ['tmpA'])
        tt('dve', tmpB, tmpB, cst[:, C_MB:C_MB + 128], ALU.mult, ['tmpB', 'cst'], ['tmpB'])
        tt('dve', DT[:, hs], tmpA, tmpB, ALU.add, ['tmpA', 'tmpB'], ['DT'])
        act(QWF[:, hs], cst[:, C_EC:C_EC + 128], AF.Exp, ['cst', 'lg'], ['QWF'], scale=lf)
        act(QWB[:, hs], cst[:, C_E127C:C_E127C + 128], AF.Exp, ['cst', 'lg'], ['QWB'], scale=lb)
        act(TB[:, hs], cst[:, C_EL1:C_EL1 + 128], AF.Exp, ['cst', 'lg'], ['TB'], scale=lb)
        act(TF[:, hs], cst[:, C_E128L:C_E128L + 128], AF.Exp, ['cst', 'lg'], ['TF'], scale=lf)
        act(coefF[:, :, h], cst[:, C_ENB:C_ENB + NCH], AF.Exp, ['cst', 'lg'], ['coefF'], scale=lf)
        act(coefB[:, :, h], cst[:, C_ENB:C_ENB + NCH], AF.Exp, ['cst', 'lg'], ['coefB'], scale=lb)
    act(g128, lg, AF.Exp, ['lg'], ['g128'], scale=128.0)
    for r in range(4):
        act(segc[:, 0, r, :], lg[:, 0:8], AF.Exp, ['lg', 'segb'], ['segc'], scale=segb[:, r:r + 1])
        act(segc[:, 1, r, :], lg[:, 8:16], AF.Exp, ['lg', 'segb'], ['segc'], scale=segb[:, 4 + r:5 + r])
        ts('dve', segc[:, 0, r, :], segc[:, 0, r, :], segb[:, 8 + r:9 + r], None, ALU.mult, None, ['segc', 'segb'], ['segc'])
        ts('dve', segc[:, 1, r, :], segc[:, 1, r, :], segb[:, 12 + r:13 + r], None, ALU.mult, None, ['segc', 'segb'], ['segc'])
    dbg('lg', lg, 'lg', [128, 16])
    dbg('segc', segc.rearrange("p a b c -> p (a b c)"), 'segc', [128, 64])

    def prep_elem(row0, nrows, xslot, src=None, hslot=0, noload=False):
        src = x if src is None else src
        sq, rs = ssq[:nrows, hslot:hslot + 1], rstd[:nrows, hslot:hslot + 1]
        ksq, krs = ('ssq', hslot), ('rstd', hslot)
        if not noload:
            dma('sp', xin[:nrows, xslot, :], src[row0:row0 + nrows, :], [], [('xin', xslot)], ('xin', xslot))
        act(hbuf[:nrows, hslot, :], xin[:nrows, xslot, :], AF.Square, [('xin', xslot)], [('hbuf', hslot), ksq], accum=sq)
        ts('dve', rs, sq, 1.0 / D, EPS, ALU.mult, ALU.add, [ksq], [krs])
        rsq(rs, [krs])
        stt('dve', hbuf[:nrows, hslot, :], xin[:nrows, xslot, :], rs, ng_bc[:nrows], ALU.mult, ALU.mult,
            [('xin', xslot), krs, 'ng_bc'], [('hbuf', hslot)])

    def prep_pe(nrows, dst, dkeys, hslot=0, bank=None):
        if bank is None:
            pst, pk = psalloc(2, 'b')
        else:
            pst, pk = ps_all[:, bank * 512:(bank + 2) * 512], [('ps', bank), ('ps', bank + 1)]
        pstb = pst.bitcast(BF16)
        for k in range(16):
            tp(pstb[:, k * nrows:(k + 1) * nrows], hbuf[:nrows, hslot, k * 128:(k + 1) * 128], identb[:nrows, :nrows],
               [('hbuf', hslot)], pk)
        v = pstb[:, 0:16 * nrows].rearrange("p (k t) -> p k t", t=nrows)
        cp('act', dst[:, 0:8, :], v[:, 0:8, :], pk, dkeys)
        cp('dve', dst[:, 8:16, :], v[:, 8:16, :], pk, dkeys)

    def rotary(ps3, nh, cos, sin, dst3, rkeys, wkeys, comb='pool'):
        cb = cos.unsqueeze(1).broadcast_to([128, nh, 64])
        sb = sin.unsqueeze(1).broadcast_to([128, nh, 64])
        x1 = ps3[:, :, 0:64]
        x2 = ps3[:, :, 64:128]
        t = [rt_box[0][:, k, 0:nh * 64].rearrange("p (h e) -> p h e", e=64) for k in range(4)]
        tt('dve', t[0], x1, cb, ALU.mult, rkeys, [('rt', 0)])
        tt('dve', t[1], x2, sb, ALU.mult, rkeys, [('rt', 1)])
        tt('dve', t[2], x2, cb, ALU.mult, rkeys, [('rt', 2)])
        tt('dve', t[3], x1, sb, ALU.mult, rkeys, [('rt', 3)])
        tt(comb, dst3[:, :, 0:64], t[0], t[1], ALU.subtract, [('rt', 0), ('rt', 1)], wkeys)
        tt(comb, dst3[:, :, 64:128], t[2], t[3], ALU.add, [('rt', 2), ('rt', 3)], wkeys)

    wkv = A.alloc([128, 16, 2048], BF16)
    ropek = A.alloc([128, NCH, 128], F32)
    hT1 = A.alloc([128, 2, 16, 128], BF16)
    kvo = A.alloc([128, 2, 2048], BF16)
    Bst = A.alloc([128, 2, 1024], F32)
    Fac = A.alloc([128, 2, 1024], F32)

    w_in_v = w_in.rearrange("(k p) f -> p k f", p=128)
    for q4 in range(4):
        P.add('pool', (lambda q4: lambda e: e.dma_start(out=wkv[:, :, q4 * 512:(q4 + 1) * 512],
                                                        in_=w_in_v[:, :, 1024 + q4 * 512:1024 + (q4 + 1) * 512]))(q4),
              [], [('wkv', q4)], chan=('wkvld', q4))
    dma('sp', ropek, ropek_d.rearrange("p (n c) -> p n c", c=128), [], ['ropek'], 'ropek')
    P.add('dve', lambda e: e.memset(Bst[:, 0, :], 0.0), [], [('Bst', 0)])
    P.add('dve', lambda e: e.memset(Fac[:, 0, :], 0.0), [], [('Fac', 0)])

    halo_list = [8 + 4 * ctp + t for ctp in range(4) for t in range(2)]
    conv_list = halo_list + [i for i in range(NPIECE) if i not in halo_list]
    conv_pos = [0]
    NCV = 32

    def issue_conv(nmax):
        for _ in range(nmax):
            if conv_pos[0] >= len(conv_list):
                return
            idx = conv_list[conv_pos[0]]
            cch = conv_pos[0] % NCV
            conv_pos[0] += 1
            dstp = wsc[idx * 128:(idx + 1) * 128, :].rearrange("p (k f) -> p k f", f=256)
            col = 0
            for (sn, c0, wd) in piece_srcs(idx):
                srcv = wsrc[sn].rearrange("(k p) f -> p k f", p=128)[:, :, c0:c0 + wd]
                P.add('pool', (lambda o, s: lambda e: e.dma_start(out=o, in_=s))(dstp[:, :, col:col + wd], srcv),
                      [], [('wsc', idx)], chan=('cv', cch))
                col += wd

    k32 = A.alloc([128, 1024], F32)
    vBF = A.alloc([128, 2, 2, 1024], BF16)

    def PSB(b, nb):
        return ps_all[:, b * 512:(b + nb) * 512], [('ps', b + t) for t in range(nb)]

    def p1_prep_e(n):
        s_ = n % 2
        prep_elem(n * 128, 128, s_, hslot=s_, noload=True)
        if n - 2 >= 0:
            xload((n - 2) * 128, s_)

    def p1_prep_p(n):
        s_ = n % 2
        prep_pe(128, hT1[:, s_], [('hT1', s_)], hslot=s_, bank=4)

    def p1_proj(n):
        hs_ = n % 2
        o = n % 2
        psk, kk = PSB(0, 2)
        psv, kvk = PSB(2, 2)
        for (psx, kx, cbase, q0) in ((psk, kk, 0, 0), (psv, kvk, 1024, 2)):
            for ft in range(2):
                for k in range(16):
                    mm(psx[:, ft * 512:(ft + 1) * 512], hT1[:, hs_, k, :], wkv[:, k, cbase + ft * 512:cbase + (ft + 1) * 512],
                       k == 0, k == 15, [('hT1', hs_), ('wkv', q0 + ft)], [kx[ft]])
            if cbase == 0:
                cp('act', k32, psk, kk, ['k32'])
            else:
                cp('act', kvo[:, o, 1024:2048], psv, kvk, [('kvo', o, 'v')])

    def p1_post_elem(n):
        o = n % 2
        rotary(k32.rearrange("p (h e) -> p h e", e=128), 8, ropek[:, n, 0:64], ropek[:, n, 64:128],
               kvo[:, o, 0:1024].rearrange("p (h e) -> p h e", e=128), ['k32', 'ropek'], [('kvo', o, 'k')], comb='dve')
        dma('sp', kvs[n * 128:(n + 1) * 128, :], kvo[:, o, :], [('kvo', o, 'k'), ('kvo', o, 'v')], [('kvs', n)], ('kvst', o))
        tt('dve', vBF[:, o, 0, :], kvo[:, o, 1024:2048], TB, ALU.mult, [('kvo', o, 'v'), 'TB'], [('vB', o)])
        tt('dve', vBF[:, o, 1, :], kvo[:, o, 1024:2048], TF, ALU.mult, [('kvo', o, 'v'), 'TF'], [('vF', o)])

    def p1_post_pe(n):
        o = n % 2
        cur = (NCH - 1 - n) % 2
        new = 1 - cur
        dma('sp', bst[n * 128:(n + 1) * 128, :], Bst[:, cur, :], [('Bst', cur)], [('bst', n)], ('bstst', cur))
        pkb, kb = PSB(6, 2)
        pkf, kf = PSB(4, 2)
        for h in range(NH):
            hs = slice(h * 128, (h + 1) * 128)
            mm(pkb[:, hs], kvo[:, o, hs], vBF[:, o, 0, hs], True, True, [('kvo', o, 'k'), ('vB', o)], [kb[h // 4]])
        for h in range(NH):
            hs = slice(h * 128, (h + 1) * 128)
            mm(pkf[:, hs], kvo[:, o, hs], vBF[:, o, 1, hs], True, True, [('kvo', o, 'k'), ('vF', o)], [kf[h // 4]])
        for h in range(NH):
            hs = slice(h * 128, (h + 1) * 128)
            stt('dve', Bst[:, new, hs], Bst[:, cur, hs], g128[:, 8 + h:9 + h], pkb[:, hs], ALU.mult, ALU.add,
                [('Bst', cur), 'g128', kb[h // 4]], [('Bst', new)])
        for h in range(NH):
            hs = slice(h * 128, (h + 1) * 128)
            stt('dve', Fac[:, new, hs], pkf[:, hs], coefF[:, n, h:h + 1], Fac[:, cur, hs], ALU.mult, ALU.add,
                [('Fac', cur), 'coefF', kf[h // 4]], [('Fac', new)])

    xload((NCH - 1) * 128, (NCH - 1) % 2)
    xload((NCH - 2) * 128, (NCH - 2) % 2)
    p1_prep_e(NCH - 1)
    p1_prep_p(NCH - 1)
    issue_conv(100)
    for n in range(NCH - 1, -1, -1):
        if n > 0:
            p1_prep_e(n - 1)
        p1_proj(n)
        if n > 0:
            p1_prep_p(n - 1)
        if n < NCH - 1:
            p1_post_pe(n + 1)
        p1_post_elem(n)
    p1_post_pe(0)
    dma('sp', xin[:64, 1, :], xh[0:64, :], [], [('xin', 1)], ('xin', 1))
    xload(0, 0)
    issue_conv(100)
    fin = NCH % 2
    dma('sp', bounce[:, 0:1024], Fac[:, fin, :], [('Fac', fin)], ['bounce'], 'bnc0')
    dma('sp', bounce[:, 1024:2048], Bst[:, fin, :], [('Bst', fin)], ['bounce2'], 'bnc1')
    P.add('pool', lambda e: e.collective_compute("AllGather", ALU.bypass, replica_groups=[[0, 1, 2, 3], [4, 5, 6, 7]],
                                                 ins=[bounce], outs=[gath]),
          ['bounce', 'bounce2'], ['gath'], chan='cc', inc=1)
    def finish_exchange(Gbuf, gkeys):
        gv = gath.rearrange("(r p) f -> p r f", p=128)
        dma('sp', Gbuf, gv[:, :, 0:1024], ['gath'], gkeys, 'gld')
        for h in range(NH):
            hs = slice(h * 128, (h + 1) * 128)
            ts('dve', Fst[:, 0, hs], Gbuf[:, 0, hs], segc[:, 0, 0, h:h + 1], None, ALU.mult, None, gkeys + ['segc'], [('Fst', 0)])
            for r in range(1, 4):
                stt('dve', Fst[:, 0, hs], Gbuf[:, r, hs], segc[:, 0, r, h:h + 1], Fst[:, 0, hs], ALU.mult, ALU.add,
                    gkeys + ['segc', ('Fst', 0)], [('Fst', 0)])
        dma('sp', Gbuf, gv[:, :, 1024:2048], ['gath'], gkeys, 'gld')
        for h in range(NH):
            hs = slice(h * 128, (h + 1) * 128)
            ts('dve', B_in[:, hs], Gbuf[:, 0, hs], segc[:, 1, 0, h:h + 1], None, ALU.mult, None, gkeys + ['segc'], ['B_in'])
            for r in range(1, 4):
                stt('dve', B_in[:, hs], Gbuf[:, r, hs], segc[:, 1, r, h:h + 1], B_in[:, hs], ALU.mult, ALU.add,
                    gkeys + ['segc', 'B_in'], ['B_in'])
        dbg('F_in', Fst[:, 0, :], ('Fst', 0), [128, 1024])
        dbg('B_in', B_in, 'B_in', [128, 1024])

    if STAGE < 2:
        Gb = wkv.rearrange("p k f -> p (k f)")[:, 0:8192].bitcast(F32).rearrange("p (r f) -> p r f", f=1024)
        finish_exchange(Gb, [('wkv', q) for q in range(4)])
    dbg('Fsum', Fac[:, fin, :], ('Fac', fin), [128, 1024])

    out_keys = []
    if STAGE >= 2:
        P.set_fence(skip_chans=('cc',) + tuple(('cv', k) for k in range(32)))
        A.reset(pers_mark)
        out_keys = phase2(nc, P, A, locals())

    P.add('sp', lambda e: None, reads=out_keys + [('dbg', k) for k in dbg_outs] + ['B_in', ('Fst', 0)] +
          ([('kvs', n) for n in range(NCH)] + [('bst', n) for n in range(NCH)] if STAGE < 2 else []), writes=[])
    if STAGE < 2:
        dma('sp', y[0:128, :], x[0:128, :], [], [('y', 0)], 'ycopy')
        P.add('sp', lambda e: None, reads=[('y', 0)], writes=[])

    with nc.allow_low_precision("bf16 matmul operands, fp32 accumulation"):
        with nc.Block() as block:
            P.emit(nc, block)
    return nc, dbg_outs


def phase2(nc, P, A, env):
    g = env
    x, xh, y, kvs, bst, wsc = g['x'], g['xh'], g['y'], g['kvs'], g['bst'], g['wsc']
    dma, mm, tp, act, tt, ts, stt, cp = g['dma'], g['mm'], g['tp'], g['act'], g['tt'], g['ts'], g['stt'], g['cp']
    psalloc, prep_elem, prep_pe, rotary, rsq = g['psalloc'], g['prep_elem'], g['prep_pe'], g['rotary'], g['rsq']
    identb, xin, hbuf, ssq, rstd = g['identb'], g['xin'], g['hbuf'], g['ssq'], g['rstd']
    DT, QWF, QWB, TF, g128, coefB, Fst, B_in, cwt = (g['DT'], g['QWF'], g['QWB'], g['TF'], g['g128'], g['coefB'],
                                                     g['Fst'], g['B_in'], g['cwt'])
    fg, gng, ropeq_d, dbg = g['fg'], g['gng'], g['ropeq_d'], g['dbg']

    g['rt_box'][0] = A.alloc([128, 4, 128], F32)
    fg_bc = A.alloc([128, D], F32)
    gn_bc = A.alloc([128, 1024], F32)
    NW = 4
    Wr = A.alloc([128, NW, 16, 256], BF16)
    ropeq = A.alloc([128, 2, 2, 128], F32)
    hT = A.alloc([128, 2, 16, T], BF16)
    uh = A.alloc([128, 8, 64], F32)
    q_rot = A.alloc([128, 2, 1024], BF16)
    sgn = A.alloc([128, 2, 1024], BF16)
    ccs = A.alloc([128, 2, T], F32)
    ubuf = A.alloc([128, 2, T + 2], F32)
    tacc = A.alloc([128, 2, T], F32)
    sgc = A.alloc([128, 2, T], F32)
    bconvT = A.alloc([128, 8, T], BF16)
    kvb = A.alloc([128, 2, 2048], BF16)
    Bl = A.alloc([128, 1024], F32)
    qT = A.alloc([128, 1024], BF16)
    kT = A.alloc([128, 1024], BF16)
    qTf = A.alloc([128, 1024], BF16)
    qTb = A.alloc([128, 1024], BF16)
    Bfull = A.alloc([128, 1024], BF16)
    Fbf = A.alloc([128, 1024], BF16)
    vF = A.alloc([128, 1024], BF16)
    STm = A.alloc([128, 1024], BF16)
    ssqo = A.alloc([128, 8], F32)
    rso = A.alloc([128, 8], F32)
    bret = A.alloc([128, 1024], BF16)
    hTh = bret.rearrange("p (k t) -> p k t", t=64)
    bretT = A.alloc([128, 8, T], BF16)
    gates = A.alloc([128, 2, 2, 256], BF16)
    m12 = A.alloc([128, 2, 2, 256], F32)
    merged = A.alloc([128, 2, 2, 256], BF16)
    mT = A.alloc([128, 16, T], BF16)
    xo = A.alloc([128, 2, D], F32)

    dma('sp', fg_bc, fg.partition_broadcast(128), [], ['fg_bc'], 'fg_bc')
    dma('sp', gn_bc, gng.partition_broadcast(128), [], ['gn_bc'], 'gn_bc')

    if REORDER:
        wseq = [8 + 4 * ctp + t for ctp in range(4) for t in range(2)] + list(range(8))
        for i in range(NT):
            wseq += list(range(8, 48)) + (list(range(8)) if i + 1 < NT else []) + list(range(48, 56))
    else:
        wseq = [8 + 4 * ctp + t for ctp in range(4) for t in range(2)]
        for i in range(NT):
            wseq += list(range(NPIECE))
    wstate = {'issued': 0, 'next': 0}

    def wget(expect_idx):
        gq = wstate['next']
        assert wseq[gq] == expect_idx, (gq, wseq[gq], expect_idx)
        while wstate['issued'] < min(gq + NW, len(wseq)):
            q = wstate['issued']
            s = q % NW
            idx = wseq[q]
            dma('sp', Wr[:, s], wsc[idx * 128:(idx + 1) * 128, :].rearrange("p (k f) -> p k f", f=256),
                [('wsc', idx)], [('W', s)], ('W', s))
            wstate['issued'] += 1
        wstate['next'] += 1
        s = gq % NW
        return Wr[:, s], ('W', s)

    prep_elem(0, 64, 1, src=xh, noload=True)
    prep_pe(64, hTh, ['bret'])
    g['xload'](128, 1)
    ccs_f = ccs.rearrange("p g t -> p (g t)")
    for ctp in range(4):
        pss = []
        for t in range(2):
            W, wk = wget(8 + 4 * ctp + t)
            psh, hk = psalloc(1, 'a')
            for gq in range(2):
                for k in range(16):
                    mm(psh[:, gq * 64:(gq + 1) * 64], W[:, k, gq * 128:(gq + 1) * 128], hTh[:, k, :], k == 0, k == 15,
                       [wk, 'bret'], hk)
            pss.append((psh, hk))
            if t == 0:
                cp('act', ccs_f[:, 0:128], psh[:, 0:128], hk, ['ccs'])
        psh, hk = pss[1]
        tt('dve', uh[:, 2 * ctp:2 * ctp + 2, :], psh[:, 0:128].rearrange("p (g t) -> p g t", t=64),
           ccs_f[:, 0:128].rearrange("p (g t) -> p g t", t=64), ALU.mult, hk + ['ccs'], ['uh'])
    dbg('uh', uh.rearrange("p a b -> p (a b)"), 'uh', [128, 512])

    xs = {'n': 0}

    def p2_prep_elem(i):
        slots = []
        for c in range(2):
            s = xs['n'] % 2
            xs['n'] += 1
            slots.append(s)
        return slots

    def p2_prep_e(i):
        hs_ = i % 2
        dma('sp', ropeq[:, hs_], ropeq_d.rearrange("p (n c) -> p n c", c=128)[:, 2 * i:2 * i + 2, :], [],
            [('ropeq', hs_)], ('ropeq', hs_))
        for c in range(2):
            prep_elem((2 * i + c) * 128, 128, c, hslot=c, noload=True)
            if i + 1 < NT:
                g['xload']((2 * (i + 1) + c) * 128, c)

    def p2_prep_p(i):
        hs_ = i % 2
        for c in range(2):
            pst, pk = psalloc(2, 'b')
            pstb = pst.bitcast(BF16)
            for k in range(16):
                tp(pstb[:, k * 128:(k + 1) * 128], hbuf[:, c, k * 128:(k + 1) * 128], identb, [('hbuf', c)], pk)
            v = pstb.rearrange("p (k t) -> p k t", t=128)
            cp('act', hT[:, hs_, 0:8, c * 128:(c + 1) * 128], v[:, 0:8, :], pk, [('hT', hs_, c)])
            cp('dve', hT[:, hs_, 8:16, c * 128:(c + 1) * 128], v[:, 8:16, :], pk, [('hT', hs_, c)])

    def tok_piece(i, idx):
        hs_ = i % 2
        W, wk = wget(idx)
        ps, pk = psalloc(1, 'a')
        for c in range(2):
            for k in range(16):
                mm(ps[:, c * 256:(c + 1) * 256], hT[:, hs_, k, c * 128:(c + 1) * 128], W[:, k, :], k == 0, k == 15,
                   [('hT', hs_, c), wk], pk)
        return ps.rearrange("p (c f) -> p c f", f=256), pk

    def feat_piece(i, idx):
        hs_ = i % 2
        W, wk = wget(idx)
        ps, pk = psalloc(1, 'a')
        for gq in range(2):
            for k in range(16):
                mm(ps[:, gq * 256:(gq + 1) * 256], W[:, k, gq * 128:(gq + 1) * 128], hT[:, hs_, k, :], k == 0, k == 15,
                   [('hT', hs_, 0), ('hT', hs_, 1), wk], pk)
        return ps.rearrange("p (g t) -> p g t", t=256), pk

    def conv_A(i, ctp):
        ps, pk = feat_piece(i, 8 + 4 * ctp)
        cp('act', ccs, ps, pk, ['ccs'])
        ps, pk = feat_piece(i, 9 + 4 * ctp)
        tt('dve', ubuf[:, :, 1:T + 1], ps, ccs, ALU.mult, pk + ['ccs'], ['ubuf'])
        for gq in range(2):
            ct = 2 * ctp + gq
            ta = tacc[:, gq, :]
            cp('pool', ubuf[:, gq, 0:1], uh[:, ct, 2 * i:2 * i + 1], ['uh'], ['ubuf'])
            cp('pool', ubuf[:, gq, T + 1:T + 2], uh[:, ct, 2 * i + 3:2 * i + 4], ['uh'], ['ubuf'])
            ts('dve', ta, ubuf[:, gq, 0:T], cwt[:, ct * 3:ct * 3 + 1], None, ALU.mult, None, ['ubuf', 'cwt'], ['tacc'])
            stt('dve', ta, ubuf[:, gq, 1:T + 1], cwt[:, ct * 3 + 1:ct * 3 + 2], ta, ALU.mult, ALU.add, ['ubuf', 'cwt', 'tacc'], ['tacc'])
            stt('dve', ta, ubuf[:, gq, 2:T + 2], cwt[:, ct * 3 + 2:ct * 3 + 3], ta, ALU.mult, ALU.add, ['ubuf', 'cwt', 'tacc'], ['tacc'])

    def conv_B(i, ctp):
        ps, pk = feat_piece(i, 10 + 4 * ctp)
        tt('dve', tacc, ps, tacc, ALU.mult, pk + ['tacc'], ['tacc'])
        ps, pk = feat_piece(i, 11 + 4 * ctp)
        act(sgc, ps, AF.Silu, pk, ['sgc'])
        tt('pool', bconvT[:, 2 * ctp:2 * ctp + 2, :], tacc, sgc, ALU.mult, ['tacc', 'sgc'],
           [('bconvT', 2 * ctp), ('bconvT', 2 * ctp + 1)])

    def conv_pair(i, s8):
        if s8 % 2 == 0:
            conv_A(i, s8 // 2)
        else:
            conv_B(i, s8 // 2)

    def ret_stage(i, c, st, S):
        n = 2 * i + c
        cur = n % 2
        new = 1 - cur
        H3 = lambda ap: ap.rearrange("p (h e) -> p h e", e=128)
        if st == 0:
            psq, qk = psalloc(1, 'b')
            psk, kk = psalloc(1, 'b')
            psqb, pskb = psq.bitcast(BF16), psk.bitcast(BF16)
            for h in range(NH):
                hs = slice(h * 128, (h + 1) * 128)
                tp(psqb[:, hs], q_rot[:, c, hs], identb, [('q_rot', c)], qk)
            for h in range(NH):
                hs = slice(h * 128, (h + 1) * 128)
                tp(pskb[:, hs], kvb[:, c, hs], identb, [('kvb', c)], kk)
            cp('act', qT, psqb, qk, ['qT'])
            cp('act', kT, pskb, kk, ['kT'])
            tt('dve', qTf, psqb, QWF, ALU.mult, qk + ['QWF'], ['qTf'])
            tt('dve', qTb, psqb, QWB, ALU.mult, qk + ['QWB'], ['qTb'])
            if n == 0:
                dma('sp', Bl, bst[0:128, :], [('bst', 0)], ['Bl'], 'Bl')
            for h in range(NH):
                hs = slice(h * 128, (h + 1) * 128)
                stt('dve', Bfull[:, hs], B_in[:, hs], coefB[:, n, h:h + 1], Bl[:, hs], ALU.mult, ALU.add,
                    ['B_in', 'coefB', 'Bl'], ['Bfull'])
            if n + 1 < NCH:
                dma('sp', Bl, bst[(n + 1) * 128:(n + 2) * 128, :], [('bst', n + 1)], ['Bl'], 'Bl')
            cp('act', Fbf, Fst[:, cur, :], [('Fst', cur)], ['Fbf'])
            tt('dve', vF, kvb[:, c, 1024:2048], TF, ALU.mult, [('kvb', c), 'TF'], ['vF'])
        elif st == 1:
            psS, sk = psalloc(2, 'b')
            for h in range(NH):
                hs = slice(h * 128, (h + 1) * 128)
                mm(psS[:, hs], kT[:, hs], qT[:, hs], True, True, ['kT', 'qT'], [sk[h // 4]])
            tt('dve', STm, psS, DT, ALU.mult, sk + ['DT'], ['STm'])
        elif st == 2:
            psO, ok = psalloc(2, 'b')
            S['psO'], S['ok'] = psO, ok
            for h in range(NH):
                hs = slice(h * 128, (h + 1) * 128)
                vs = slice(1024 + h * 128, 1024 + (h + 1) * 128)
                mm(psO[:, hs], STm[:, hs], kvb[:, c, vs], True, False, ['STm', ('kvb', c)], [ok[h // 4]])
                mm(psO[:, hs], qTf[:, hs], Fbf[:, hs], False, False, ['qTf', 'Fbf'], [ok[h // 4]])
                mm(psO[:, hs], qTb[:, hs], Bfull[:, hs], False, True, ['qTb', 'Bfull'], [ok[h // 4]])
            psK, kk = psalloc(2, 'b')
            for h in range(NH):
                hs = slice(h * 128, (h + 1) * 128)
                mm(psK[:, hs], kvb[:, c, hs], vF[:, hs], True, True, [('kvb', c), 'vF'], [kk[h // 4]])
            for h in range(NH):
                hs = slice(h * 128, (h + 1) * 128)
                act(bret[:, hs], psO[:, hs], AF.Square, [ok[h // 4]], ['bret', 'ssqo'], accum=ssqo[:, h:h + 1])
            ts('dve', rso, ssqo, 1.0 / 128, EPS, ALU.mult, ALU.add, ['ssqo'], ['rso'])
            rsq(rso, ['rso'])
            for h in range(NH):
                hs = slice(h * 128, (h + 1) * 128)
                stt('dve', bret[:, hs], psO[:, hs], rso[:, h:h + 1], sgn[:, c, hs], ALU.mult, ALU.mult,
                    [ok[h // 4], 'rso', ('sgn', c)], ['bret'])
            for h in range(NH):
                hs = slice(h * 128, (h + 1) * 128)
                stt('dve', Fst[:, new, hs], Fst[:, cur, hs], g128[:, h:h + 1], psK[:, hs], ALU.mult, ALU.add,
                    [('Fst', cur), 'g128', kk[h // 4]], [('Fst', new)])
        else:
            psb, bk = psalloc(1, 'b')
            psbb = psb.bitcast(BF16)
            for h in range(NH):
                hs = slice(h * 128, (h + 1) * 128)
                tp(psbb[:, hs], bret[:, hs], identb, ['bret'], bk)
            cp('act', bretT[:, :, c * 128:(c + 1) * 128], H3(psbb), bk, [('bretT', c)])

    def merged_T(i, j):
        sl = j % 2
        pst, pk = psalloc(1, 'b')
        pstb = pst.bitcast(BF16)
        for c in range(2):
            for kk in range(2):
                tp(pstb[:, kk * 256 + c * 128: kk * 256 + (c + 1) * 128], merged[:, sl, c, kk * 128:(kk + 1) * 128], identb,
                   [('merged', sl)], pk)
        eng = 'act' if j % 2 == 0 else 'dve'
        cp(eng, mT[:, 2 * j:2 * j + 2, :], pstb[:, 0:512].rearrange("p (k t) -> p k t", t=256), pk, [('mT', j)])

    out_keys = []

    def kv_loads(i):
        for c in range(2):
            n = 2 * i + c
            dma('sp', kvb[:, c, :], kvs[n * 128:(n + 1) * 128, :], [('kvs', n)], [('kvb', c)], ('kvb', c))

    def qg_pieces(i):
        hs_ = i % 2
        for p in range(4):
            ps, pk = tok_piece(i, p)
            for c in range(2):
                rotary(ps[:, c, :].rearrange("p (h e) -> p h e", e=128), 2, ropeq[:, hs_, c, 0:64], ropeq[:, hs_, c, 64:128],
                       q_rot[:, c, p * 256:(p + 1) * 256].rearrange("p (h e) -> p h e", e=128),
                       pk + [('ropeq', hs_)], [('q_rot', c)], comb='dve')
        for p in range(4):
            ps, pk = tok_piece(i, 4 + p)
            act(sgn[:, :, p * 256:(p + 1) * 256], ps, AF.Silu, pk, [('sgn', 0), ('sgn', 1)])
            tt('pool', sgn[:, :, p * 256:(p + 1) * 256], sgn[:, :, p * 256:(p + 1) * 256],
               gn_bc[:, p * 256:(p + 1) * 256].unsqueeze(1).broadcast_to([128, 2, 256]), ALU.mult,
               [('sgn', 0), ('sgn', 1), 'gn_bc'], [('sgn', 0), ('sgn', 1)])

    def final_norm(i):
        for c in range(2):
            n = 2 * i + c
            act(mT.rearrange("p k t -> p (k t)")[:, 0:D], xo[:, c, :], AF.Square, [('xo', c)],
                [('mT', j) for j in range(8)] + [('ssq2', c)], accum=ssq[:, 2 + c:3 + c])
            ts('dve', rstd[:, 2 + c:3 + c], ssq[:, 2 + c:3 + c], 1.0 / D, EPS, ALU.mult, ALU.add, [('ssq2', c)], [('rstd2', c)])
            rsq(rstd[:, 2 + c:3 + c], [('rstd2', c)])
            if c == 0:
                act(xo[:, c, :], xo[:, c, :], AF.Copy, [('xo', c), ('rstd2', c)], [('xo', c)], scale=rstd[:, 2 + c:3 + c])
                tt('pool', xo[:, c, :], xo[:, c, :], fg_bc, ALU.mult, [('xo', c), 'fg_bc'], [('xo', c)])
            else:
                stt('dve', xo[:, c, :], xo[:, c, :], rstd[:, 2 + c:3 + c], fg_bc, ALU.mult, ALU.mult,
                    [('xo', c), ('rstd2', c), 'fg_bc'], [('xo', c)])
            dma('sp', y[n * 128:(n + 1) * 128, :], xo[:, c, :], [('xo', c)], [('y', n)], ('yst', c))
            out_keys.append(('y', n))

    p2_prep_e(0)
    p2_prep_p(0)
    kv_loads(0)
    if REORDER:
        qg_pieces(0)
    for i in range(NT):
        hs_ = i % 2
        if not REORDER:
            qg_pieces(i)
        S = {}
        if i == 0:
            for s8 in range(8):
                conv_pair(i, s8)
            p2_prep_e(i + 1)
            g['finish_exchange'](xo.rearrange("p c f -> p (c f)").rearrange("p (r f) -> p r f", f=1024), [('xo', 0), ('xo', 1)])
            for s8 in range(8):
                ret_stage(i, s8 // 4, s8 % 4, S)
        else:
            for s8 in range(8):
                ret_stage(i, s8 // 4, s8 % 4, S)
                conv_pair(i, s8)
                if s8 == 0 and i + 1 < NT:
                    p2_prep_e(i + 1)
                if s8 == 1:
                    final_norm(i - 1)
        if i + 1 < NT:
            p2_prep_p(i + 1)
            kv_loads(i + 1)
        dma('sp', xo, x[i * T:(i + 1) * T, :].rearrange("(c p) f -> p c f", p=128), [], [('xo', 0), ('xo', 1)], 'xo')
        for j in range(8):
            for br in range(2):
                ps, pk = feat_piece(i, 24 + 3 * j + br)
                act(gates[:, br, :, :], ps, AF.Sigmoid, pk, [('gates', br)])
            W, wk = wget(24 + 3 * j + 2)
            psu, uk = psalloc(2, 'b')
            psu4 = psu.rearrange("p (hf b t) -> p hf b t", b=2, t=256)
            for hf in range(2):
                for br in range(2):
                    src = bretT if br == 0 else bconvT
                    for kf in range(8):
                        rk = [('bretT', 0), ('bretT', 1)] if br == 0 else [('bconvT', kf)]
                        mm(psu4[:, hf, br, :], W[:, br * 8 + kf, hf * 128:(hf + 1) * 128], src[:, kf, :], kf == 0, kf == 7,
                           rk + [wk], [uk[hf]])
            tt('dve', m12[:, :, 0, :], psu4[:, :, 0, :], gates[:, 0, :, :], ALU.mult, uk + [('gates', 0)], [('m12', 0)])
            tt('dve', m12[:, :, 1, :], psu4[:, :, 1, :], gates[:, 1, :, :], ALU.mult, uk + [('gates', 1)], [('m12', 1)])
            tt('pool', mT[:, 2 * j:2 * j + 2, :], m12[:, :, 0, :], m12[:, :, 1, :], ALU.add, [('m12', 0), ('m12', 1)], [('mT', j)])
        if REORDER and i + 1 < NT:
            qg_pieces(i + 1)
        for j in range(8):
            W, wk = wget(48 + j)
            ps, pk = psalloc(1, 'a')
            for c in range(2):
                for k in range(16):
                    mm(ps[:, c * 256:(c + 1) * 256], mT[:, k, c * 128:(c + 1) * 128], W[:, k, :], k == 0, k == 15,
                       [('mT', k // 2), wk], pk)
            xv = xo[:, :, j * 256:(j + 1) * 256]
            tt('dve', xv, ps.rearrange("p (c f) -> p c f", f=256), xv, ALU.add, pk + [('xo', 0), ('xo', 1)], [('xo', 0), ('xo', 1)])
        if i == NT - 1:
            final_norm(i)
    return out_keys


_NC_CACHE = {}


def _host_consts():
    c = np.zeros((128, 1184), np.float32)
    l = np.arange(128, dtype=np.float64)[:, None]
    cc = np.arange(128, dtype=np.float64)[None, :]
    c[:, 0:128] = np.eye(128)
    c[:, 128:256] = np.maximum(cc - l, 0)
    c[:, 256:384] = np.maximum(l - cc, 0)
    c[:, 384:512] = (cc >= l)
    c[:, 512:640] = (l > cc)
    c[:, 640:768] = l + 1
    c[:, 768:896] = 128 - l
    c[:, 896:1024] = cc
    c[:, 1024:1152] = 127 - cc
    c[:, 1152:1184] = 128.0 * (31 - np.arange(32))[None, :]
    return c


def _rope_tables(pos0):
    pos = (pos0 + np.arange(SEGT)).astype(np.float64)
    inv = 10000.0 ** (-np.arange(0, 128, 2, dtype=np.float64) / 128)
    ang = pos[:, None] * inv[None, :]
    cos, sin = np.cos(ang), np.sin(ang)
    sc = 128 ** -0.5

    def lay(a, b):
        t = np.concatenate([a, b], axis=1).reshape(NCH, 128, 128)
        return np.ascontiguousarray(t.transpose(1, 0, 2).reshape(128, NCH * 128)).astype(np.float32)

    return lay(cos, sin), lay(cos * sc, sin * sc)


def kernel(x, norm_gain, w_in, decay_logit_fwd, decay_logit_bwd, ret_gn_gain, conv_w, w_branch, w_out, final_gain):
    x = np.asarray(x, np.float32)
    Bn, S, _ = x.shape
    if 'nc' not in _NC_CACHE:
        _NC_CACHE['nc'] = build()
    nc, dbg_outs = _NC_CACHE['nc']
    w_in0 = np.ascontiguousarray(np.asarray(w_in, np.float32)[0])
    w_br0 = np.ascontiguousarray(np.asarray(w_branch, np.float32)[0].reshape(2 * 1024, D))
    w_out0 = np.ascontiguousarray(np.asarray(w_out, np.float32)[0])
    ng = np.asarray(norm_gain, np.float32)[0].reshape(1, D)
    fg = np.asarray(final_gain, np.float32).reshape(1, D)
    gng = np.asarray(ret_gn_gain, np.float32)[0].reshape(1, 1024)
    cw = np.ascontiguousarray(np.asarray(conv_w, np.float32)[0].reshape(3, 8, 128).transpose(2, 1, 0).reshape(128, 24))
    dl = np.concatenate([np.asarray(decay_logit_fwd, np.float32)[0], np.asarray(decay_logit_bwd, np.float32)[0]]).reshape(1, 16)
    consts = _host_consts()
    in_maps = []
    for c in range(NCORE):
        b, s = divmod(c, 4)
        t0 = s * SEGT
        xs = np.ascontiguousarray(x[b, t0:t0 + SEGT])
        xhal = np.zeros((64, D), np.float32)
        for i in range(NT + 1):
            for k, tok in enumerate((t0 + T * i - 1, t0 + T * i)):
                if 0 <= tok < S:
                    xhal[2 * i + k] = x[b, tok]
        rq, rk = _rope_tables(t0)
        seg = np.zeros((1, 16), np.float32)
        for r in range(4):
            if r < s:
                seg[0, r] = 4096.0 * (s - 1 - r)
                seg[0, 8 + r] = 1.0
            if r > s:
                seg[0, 4 + r] = 4096.0 * (r - s - 1)
                seg[0, 12 + r] = 1.0
        in_maps.append({"x": xs, "xh": xhal, "w_in": w_in0, "w_br": w_br0, "w_out": w_out0, "ng": ng, "fg": fg,
                        "gng": gng, "cw": cw, "dl": dl, "ropeq": rq, "ropek": rk, "consts": consts, "seg": seg})
    res = run_bass_kernel_spmd(nc, in_maps, core_ids=list(range(NCORE)))
    out = np.empty((Bn, S, D), np.float32)
    for c in range(NCORE):
        b, s = divmod(c, 4)
        out[b, s * SEGT:(s + 1) * SEGT] = res.results[c]["y"]
    if DEBUG:
        kernel.last_results = res.results
    return out
```
